# Optimizing a Trainium2 kernel written in Bass

```python
import jax, jax.numpy as jnp
from jax import lax
import numpy as np

D_MODEL = 1024
BATCH = 8
SEQ = 8192
DEPTH = 1

GRID_W = 64
PLE_DIM = 256
ATTN_HEADS = 8
ATTN_KV_HEADS = 2
ATTN_HEAD_DIM = 64
ATTN_GROUP = ATTN_HEADS // ATTN_KV_HEADS
Q_BLOCK = 128
ROPE_THETA = 10000.0
ATTN_Q_W = ATTN_HEADS * ATTN_HEAD_DIM
ATTN_KV_W = ATTN_KV_HEADS * ATTN_HEAD_DIM
DN_HEADS = 4
DN_HEAD_K = 128
DN_HEAD_V = 128
DN_CHUNK = 64
CONV_W = 5
DN_QK_W = DN_HEADS * DN_HEAD_K
DN_V_W = DN_HEADS * DN_HEAD_V
DN_QKV_W = 2 * DN_QK_W + DN_V_W
IN_W = ATTN_Q_W + 2 * ATTN_KV_W + DN_QKV_W + DN_V_W + 4 * DN_HEADS
MIX_W = ATTN_Q_W + DN_V_W
FF_DIM = -(-8 * D_MODEL // (3 * 256)) * 256
EPS = 1e-6

kernel_name = "hymba_gqa_axialrope_gdeltanet_swiglu_ple"


def rms_norm(x, gain):
    xf = x.astype(jnp.float32)
    y = xf * lax.rsqrt(jnp.mean(xf * xf, axis=-1, keepdims=True) + EPS)
    return (y * gain.astype(jnp.float32)).astype(x.dtype)


def l2_norm(x):
    return x * lax.rsqrt(jnp.sum(x * x, axis=-1, keepdims=True) + EPS)


def rope_1d(x, ang):
    c = jnp.cos(ang)[:, None, :].astype(x.dtype)
    s = jnp.sin(ang)[:, None, :].astype(x.dtype)
    half = x.shape[-1] // 2
    x1, x2 = x[..., :half], x[..., half:]
    return jnp.concatenate([x1 * c - x2 * s, x2 * c + x1 * s], axis=-1)


def axial_rope(x, ang_row, ang_col):
    half = x.shape[-1] // 2
    return jnp.concatenate([rope_1d(x[..., :half], ang_row), rope_1d(x[..., half:], ang_col)], axis=-1)


def grid_angles(s):
    rows = s // GRID_W
    row = jnp.broadcast_to(jnp.arange(rows, dtype=jnp.float32)[:, None], (rows, GRID_W)).reshape(s)
    col = jnp.broadcast_to(jnp.arange(GRID_W, dtype=jnp.float32)[None, :], (rows, GRID_W)).reshape(s)
    sec = ATTN_HEAD_DIM // 2
    inv_freq = ROPE_THETA ** (-jnp.arange(0, sec, 2, dtype=jnp.float32) / sec)
    return row[:, None] * inv_freq[None, :], col[:, None] * inv_freq[None, :]


def block_attention(q, k, v):
    b, s, _, d = q.shape
    nb = s // Q_BLOCK
    qb = jnp.moveaxis(q.reshape(b, nb, Q_BLOCK, ATTN_KV_HEADS, ATTN_GROUP, d), 1, 0)
    scale = d ** -0.5

    def one_block(q_blk):
        scores = jnp.einsum('bqkgd,bskd->bkgqs', q_blk, k, preferred_element_type=jnp.float32) * scale
        probs = jax.nn.softmax(scores, axis=-1).astype(v.dtype)
        return jnp.einsum('bkgqs,bskd->bqkgd', probs, v)

    out = lax.map(one_block, qb)
    return jnp.moveaxis(out, 0, 1).reshape(b, s, ATTN_HEADS * d)


def chunked_gated_delta_rule(q, k, v, g, beta):
    b, h, s, dk = q.shape
    dv = v.shape[-1]
    c = DN_CHUNK
    n = s // c
    q = q.reshape(b, h, n, c, dk)
    k = k.reshape(b, h, n, c, dk)
    v = v.reshape(b, h, n, c, dv)
    beta = beta.reshape(b, h, n, c)
    G = jnp.cumsum(g.reshape(b, h, n, c), axis=-1)
    incl = jnp.tril(jnp.ones((c, c), dtype=bool))
    strict = jnp.tril(jnp.ones((c, c), dtype=bool), -1)
    diff = G[..., :, None] - G[..., None, :]
    decay = jnp.where(incl, jnp.exp(jnp.where(incl, diff, 0.0)), 0.0)
    k_beta = k * beta[..., None]
    a_mat = jnp.where(strict, jnp.einsum('bhnid,bhnjd->bhnij', k_beta, k) * decay, 0.0) + jnp.eye(c, dtype=q.dtype)
    rhs = jnp.concatenate([v * beta[..., None], k_beta * jnp.exp(G)[..., None]], axis=-1)
    sol = lax.linalg.triangular_solve(a_mat, rhs, left_side=True, lower=True, unit_diagonal=True)
    u, w = sol[..., :dv], sol[..., dv:]
    qk = jnp.einsum('bhnid,bhnjd->bhnij', q, k) * decay
    q_dec = q * jnp.exp(G)[..., None]
    k_dec = k * jnp.exp(G[..., -1:] - G)[..., None]
    g_tot = jnp.exp(G[..., -1])
    xs = tuple(jnp.moveaxis(t, 2, 0) for t in (w, u, qk, q_dec, k_dec, g_tot))

    def step(state, inp):
        w_c, u_c, qk_c, qd_c, kd_c, gt_c = inp
        v_new = u_c - jnp.einsum('bhck,bhkv->bhcv', w_c, state)
        o_c = jnp.einsum('bhck,bhkv->bhcv', qd_c, state) + jnp.einsum('bhcj,bhjv->bhcv', qk_c, v_new)
        state = state * gt_c[..., None, None] + jnp.einsum('bhck,bhcv->bhkv', kd_c, v_new)
        return state, o_c

    state0 = jnp.zeros((b, h, dk, dv), dtype=q.dtype)
    _, o = lax.scan(step, state0, xs)
    return jnp.moveaxis(o, 0, 2).reshape(b, h, s, dv)


def deltanet_mixer(qkv, z, b_raw, a_raw, conv_w, a_log, dt_bias, norm_gain):
    b, s, _ = qkv.shape
    dtype = qkv.dtype
    f32 = jnp.float32
    qkv = jax.nn.silu(lax.conv_general_dilated(
        qkv, conv_w[:, None, :], window_strides=(1,), padding=[(CONV_W // 2, CONV_W // 2)],
        dimension_numbers=('NWC', 'WIO', 'NWC'), feature_group_count=DN_QKV_W))
    q, k, v = jnp.split(qkv.astype(f32), [DN_QK_W, 2 * DN_QK_W], axis=-1)
    q = l2_norm(q.reshape(b, s, DN_HEADS, DN_HEAD_K).transpose(0, 2, 1, 3)) * (DN_HEAD_K ** -0.5)
    k = l2_norm(k.reshape(b, s, DN_HEADS, DN_HEAD_K).transpose(0, 2, 1, 3))
    v = v.reshape(b, s, DN_HEADS, DN_HEAD_V).transpose(0, 2, 1, 3)
    beta = jax.nn.sigmoid(b_raw.astype(f32))
    g = -jnp.exp(a_log.astype(f32)) * jax.nn.softplus(a_raw.astype(f32) + dt_bias.astype(f32))
    q2 = jnp.concatenate([q, jnp.flip(q, axis=2)], axis=1)
    k2 = jnp.concatenate([k, jnp.flip(k, axis=2)], axis=1)
    v2 = jnp.concatenate([v, jnp.flip(v, axis=2)], axis=1)
    g2 = jnp.concatenate([g[..., :DN_HEADS], jnp.flip(g[..., DN_HEADS:], axis=1)], axis=-1).transpose(0, 2, 1)
    beta2 = jnp.concatenate([beta[..., :DN_HEADS], jnp.flip(beta[..., DN_HEADS:], axis=1)], axis=-1).transpose(0, 2, 1)
    o2 = chunked_gated_delta_rule(q2, k2, v2, g2, beta2)
    o = o2[:, :DN_HEADS] + jnp.flip(o2[:, DN_HEADS:], axis=2)
    o = o.transpose(0, 2, 1, 3)
    o = rms_norm(o, norm_gain) * jax.nn.silu(z.reshape(b, s, DN_HEADS, DN_HEAD_V).astype(f32))
    return o.reshape(b, s, DN_V_W).astype(dtype)


def setup_inputs(seed: int = 0) -> dict:
    key = jax.random.key(seed)
    ks = jax.random.split(key, 20)
    nrm = lambda k_, shape, scale: jax.random.normal(k_, shape, dtype=jnp.float32) * scale
    gain = lambda k_, shape: 1.0 + 0.01 * jax.random.normal(k_, shape, dtype=jnp.float32)
    dt = jnp.exp(jax.random.uniform(ks[8], (DEPTH, 2 * DN_HEADS), minval=np.log(0.001), maxval=np.log(0.1)))
    return {
        "x": nrm(ks[0], (BATCH, SEQ, D_MODEL), 1.0),
        "p": nrm(ks[1], (DEPTH, BATCH, SEQ, PLE_DIM), 1.0),
        "norm_mix": gain(ks[2], (DEPTH, D_MODEL)),
        "w_in": nrm(ks[3], (DEPTH, D_MODEL, IN_W), D_MODEL ** -0.5),
        "conv_w": nrm(ks[4], (DEPTH, CONV_W, DN_QKV_W), CONV_W ** -0.5),
        "q_norm": gain(ks[5], (DEPTH, ATTN_HEAD_DIM)),
        "k_norm": gain(ks[6], (DEPTH, ATTN_HEAD_DIM)),
        "a_log": jnp.log(jax.random.uniform(ks[7], (DEPTH, 2 * DN_HEADS), minval=1.0, maxval=16.0)),
        "dt_bias": dt + jnp.log(-jnp.expm1(-dt)),
        "dn_norm": gain(ks[9], (DEPTH, DN_HEAD_V)),
        "w_out": nrm(ks[10], (DEPTH, MIX_W, D_MODEL), MIX_W ** -0.5),
        "norm_ffn": gain(ks[11], (DEPTH, D_MODEL)),
        "w_gate": nrm(ks[12], (DEPTH, D_MODEL, FF_DIM), D_MODEL ** -0.5),
        "w_up": nrm(ks[13], (DEPTH, D_MODEL, FF_DIM), D_MODEL ** -0.5),
        "w_down": nrm(ks[14], (DEPTH, FF_DIM, D_MODEL), FF_DIM ** -0.5),
        "norm_ple": gain(ks[15], (DEPTH, D_MODEL)),
        "w_ple_gate": nrm(ks[16], (DEPTH, D_MODEL, D_MODEL), D_MODEL ** -0.5),
        "w_ple": nrm(ks[17], (DEPTH, PLE_DIM, D_MODEL), PLE_DIM ** -0.5),
        "norm_final": gain(ks[18], (D_MODEL,)),
    }


def reference(x, p, norm_mix, w_in, conv_w, q_norm, k_norm, a_log, dt_bias, dn_norm, w_out,
              norm_ffn, w_gate, w_up, w_down, norm_ple, w_ple_gate, w_ple, norm_final):
    b, s, _ = x.shape
    ang_row, ang_col = grid_angles(s)
    split_at = list(np.cumsum([ATTN_Q_W, ATTN_KV_W, ATTN_KV_W, DN_QKV_W, DN_V_W, 2 * DN_HEADS]))
    h = x
    for i in range(DEPTH):
        hn = rms_norm(h, norm_mix[i])
        proj = hn @ w_in[i]
        aq, ak, av, dqkv, dz, dbeta, dalpha = jnp.split(proj, split_at, axis=-1)
        aq = axial_rope(rms_norm(aq.reshape(b, s, ATTN_HEADS, ATTN_HEAD_DIM), q_norm[i]), ang_row, ang_col)
        ak = axial_rope(rms_norm(ak.reshape(b, s, ATTN_KV_HEADS, ATTN_HEAD_DIM), k_norm[i]), ang_row, ang_col)
        av = av.reshape(b, s, ATTN_KV_HEADS, ATTN_HEAD_DIM)
        attn_out = block_attention(aq, ak, av)
        dn_out = deltanet_mixer(dqkv, dz, dbeta, dalpha, conv_w[i], a_log[i], dt_bias[i], dn_norm[i])
        mixed = jnp.concatenate([attn_out.astype(h.dtype), dn_out.astype(h.dtype)], axis=-1)
        h = h + mixed @ w_out[i]
        hn = rms_norm(h, norm_ffn[i])
        h = h + (jax.nn.silu(hn @ w_gate[i]) * (hn @ w_up[i])) @ w_down[i]
        gate = jax.nn.sigmoid(rms_norm(h, norm_ple[i]) @ w_ple_gate[i])
        h = h + gate * (p[i] @ w_ple[i])
    return rms_norm(h, norm_final)
```

```python
import numpy as np
import concourse.bass as bass
import concourse.mybir as mybir
from concourse.bass_utils import run_bass_kernel_spmd
from contextlib import ExitStack
from itertools import zip_longest
import os
PHMAX = int(os.environ.get('KPH', '9'))

F32 = mybir.dt.float32
BF16 = mybir.dt.bfloat16
ALU = mybir.AluOpType
AF = mybir.ActivationFunctionType
AX = mybir.AxisListType

D = 1024
INW = 2832
FF = 2816
NFC = FF // 128
PLE = 256
EPS = 1e-6


class Buf:
    __slots__ = ("name", "w", "rs")

    def __init__(self, name):
        self.name = name
        self.w = None
        self.rs = []


class MBuf(Buf):
    __slots__ = ("ws",)

    def __init__(self, name):
        Buf.__init__(self, name)
        self.ws = []


def _compact(evl):
    best = {}
    for r in evl:
        if r[0] not in best or best[r[0]][2] < r[2]:
            best[r[0]] = r
    return list(best.values())


class Sched:
    ND = 24

    def __init__(self, nc, stack):
        self.nc = nc
        self.E = {"pe": nc.tensor, "act": nc.scalar, "dve": nc.vector,
                  "pool": nc.gpsimd, "sp": nc.sync}
        self.sem = {}
        self.cnt = {}
        for k in ("pe", "act", "dve", "pool"):
            self.sem[k] = stack.enter_context(nc.semaphore("sem_" + k))
            self.cnt[k] = 0
        self.dsem = [stack.enter_context(nc.semaphore("dsem%d" % i)) for i in range(self.ND)]
        self.dcnt = [0] * self.ND
        self.dnext = 0
        self.seen = {k: {} for k in self.E}
        self.nops = {k: 0 for k in self.E}

    def _wait(self, ek, ev):
        key, sem, val, src = ev
        if self.seen[ek].get(key, 0) >= val:
            return
        self.E[ek].wait_ge(sem, val)
        self.seen[ek][key] = val

    def _note(self, ev, reads, writes):
        for b in reads:
            b.rs.append(ev)
            if len(b.rs) > 16:
                best = {}
                for r in b.rs:
                    if r[0] not in best or best[r[0]][2] < r[2]:
                        best[r[0]] = r
                b.rs = list(best.values())
        for b in writes:
            if isinstance(b, MBuf):
                b.ws.append(ev)
                if len(b.ws) > 40:
                    b.ws = _compact(b.ws)
            else:
                b.w = ev
                b.rs = []

    def op(self, ek, fn, reads=(), writes=(), signal=True):
        evs = []
        for b in reads:
            if isinstance(b, MBuf):
                evs.extend(b.ws)
            elif b.w is not None:
                evs.append(b.w)
        for b in writes:
            if b.w is not None and b.w[3] != ek:
                evs.append(b.w)
            for r in b.rs:
                if r[3] != ek:
                    evs.append(r)
        for e in evs:
            if ek == "pe" and e[3] == "pe":
                continue
            self._wait(ek, e)
        inst = fn(self.E[ek])
        self.nops[ek] += 1
        if signal:
            self.cnt[ek] += 1
            inst.then_inc(self.sem[ek], 1)
            ev = (ek, self.sem[ek], self.cnt[ek], ek)
        else:
            ev = (ek, self.sem[ek], self.cnt[ek] + 1, ek)
        self._note(ev, reads, writes)
        return inst

    def dma(self, out, in_, reads=(), writes=(), q="sp", **kw):
        k = self.dnext
        self.dnext = (self.dnext + 1) % self.ND
        key = "d%d" % k
        if self.dcnt[k] > 0:
            self._wait(q, (key, self.dsem[k], self.dcnt[k], "dma"))
        evs = []
        for b in reads:
            if isinstance(b, MBuf):
                evs.extend(b.ws)
            elif b.w is not None:
                evs.append(b.w)
        for b in writes:
            if b.w is not None:
                evs.append(b.w)
            evs.extend(b.rs)
        for e in evs:
            self._wait(q, e)
        inst = self.E[q].dma_start(out=out, in_=in_, **kw)
        self.dcnt[k] += 16
        inst.then_inc(self.dsem[k], 16)
        self.nops[q] += 1
        ev = (key, self.dsem[k], self.dcnt[k], "dma")
        self._note(ev, reads, writes)
        return inst

    def finish(self):
        for k in range(self.ND):
            if self.dcnt[k] > 0:
                self._wait("sp", ("d%d" % k, self.dsem[k], self.dcnt[k], "dma"))


M_ID, M_LOWI, M_UPPI, M_LOWS, M_UPPS, M_BLK, M_ONES, M_CI0, M_CI1, M_RM = range(10)
V_GMIX, V_GFFN, V_GPLE, V_QNG, V_KNG, V_ALOG, V_DTB, V_CONV, V_END = 0, 8, 16, 24, 25, 26, 34, 42, 102


def build(S):
    T = S // 128
    NB = S // 512
    nc = bass.Bass("TRN2", target_bir_lowering=False)

    def din(name, shape, dt=F32):
        return nc.dram_tensor(name, list(shape), dt, kind="ExternalInput").ap()

    def dscr(name, shape, dt=F32):
        return nc.dram_tensor(name, list(shape), dt).ap()

    x_d = din("x", [S, D])
    p_d = din("p", [S, PLE])
    win_d = din("w_in", [D, INW])
    wout_d = din("w_out", [D, D])
    wg_d = din("w_gate", [D, FF])
    wu_d = din("w_up", [D, FF])
    wd_d = din("w_down", [FF, D])
    wpg_d = din("w_pg", [D, D])
    wple_d = din("w_ple", [PLE, D])
    cm_d = din("cm", [128, 10, 128])
    cv_d = din("cv", [128, V_END])
    cg_d = din("cg", [128, 128 + D])
    cos_d = din("cosT", [128, S])
    sin_d = din("sinT", [128, S])
    out_d = nc.dram_tensor("out", [S, D], F32, kind="ExternalOutput").ap()

    QT_s = dscr("QT_s", [4, 128, S], BF16)
    KT_s = dscr("KT_s", [2, 128, S], BF16)
    V_s = dscr("V_s", [S, 128], BF16)
    DQT_s = dscr("DQT_s", [4, 128, S])
    DKT_s = dscr("DKT_s", [4, 128, S])
    DK_s = dscr("DK_s", [S, 4, 128])
    DV_s = dscr("DV_s", [S, 4, 128])
    GB_s = dscr("GB_s", [S, 16])
    DZ_s = dscr("DZ_s", [S, 512])
    AO_s = dscr("AO_s", [8, 64, S], BF16)
    OF_s = dscr("OF_s", [2, S, 512])
    WO_s = dscr("WO_s", [D, D], BF16)
    WG_s = dscr("WG_s", [NFC, 128, 8, 128], BF16)
    WU_s = dscr("WU_s", [NFC, 128, 8, 128], BF16)
    WD_s = dscr("WD_s", [FF, D], BF16)
    WPG_s = dscr("WPG_s", [D, D], BF16)
    WPL_s = dscr("WPL_s", [PLE, D], BF16)
    bQT, bKT, bV = MBuf("QT_s"), MBuf("KT_s"), MBuf("V_s")
    bDN = MBuf("DN_s")
    bAO, bOF = MBuf("AO_s"), MBuf("OF_s")
    bW = MBuf("W_s")

    top = ExitStack()
    with top:
        Sc = Sched(nc, top)

        def sb(stack, name, shape, dt=F32):
            return stack.enter_context(nc.sbuf_tensor("sb_" + name, list(shape), dt)), Buf(name)

        pb = []
        pairs = []
        for i in range(4):
            pr_ = top.enter_context(nc.psum_tensor("pp%d" % i, [128, 2, 512], F32))
            pairs.append(pr_)
            for j in range(2):
                pb.append((pr_[:, j, :], Buf("pb%d" % (2 * i + j))))
        pbn = [0]

        def bank():
            i = pbn[0]
            pbn[0] = (i + 1) % 8
            return pb[i]

        cm, bcm = sb(top, "cm", [128, 10, 128])
        cv, bcv = sb(top, "cv", [128, V_END])
        idb, bidb = sb(top, "idb", [128, 128], BF16)
        epst, bepst = sb(top, "epst", [128, 2])
        nexpA, bnexpA = sb(top, "nexpA", [128, 8])
        Sc.dma(cm[:], cm_d, writes=[bcm])
        Sc.dma(cv[:], cv_d, writes=[bcv])
        Sc.op("dve", lambda e: e.tensor_copy(out=idb[:], in_=cm[:, M_ID, :]), reads=[bcm], writes=[bidb])
        Sc.op("pool", lambda e: e.memset(epst[:, 0:1], EPS), writes=[bepst])
        Sc.op("pool", lambda e: e.memset(epst[:, 1:2], 1.0), writes=[bepst])
        Sc.op("act", lambda e: e.activation(out=nexpA[:], in_=cv[:, V_ALOG:V_ALOG + 8], func=AF.Exp),
              reads=[bcv], writes=[bnexpA])
        Sc.op("dve", lambda e: e.tensor_scalar(out=nexpA[:], in0=nexpA[:], scalar1=-1.0, scalar2=None, op0=ALU.mult),
              reads=[bnexpA], writes=[bnexpA])
        ident = cm[:, M_ID, :]

        def rstd_from_ss(ss_ap, out_ap, tmp_ap, scale, bufs):
            Sc.op("act", lambda e: e.activation(out=tmp_ap, in_=ss_ap, func=AF.Ln, scale=scale, bias=epst[:, 0:1]),
                  reads=bufs + [bepst], writes=bufs)
            Sc.op("act", lambda e: e.activation(out=out_ap, in_=tmp_ap, func=AF.Exp, scale=-0.5),
                  reads=bufs, writes=bufs)

        def transpose_bf(src_tile, bsrc, nblk, dst_ap3, bdst, evac="act"):
            pt, bpt = bank()
            ptb = pt[:].bitcast(BF16)
            for k in range(nblk):
                Sc.op("pe", lambda e, k=k: e.transpose(out=ptb[:, k * 128:(k + 1) * 128],
                                                         in_=src_tile[:, k * 128:(k + 1) * 128], identity=idb[:]),
                      reads=[bsrc, bidb], writes=[bpt], signal=(k == nblk - 1))
            src3 = ptb[:, 0:nblk * 128].rearrange("p (k t) -> p k t", k=nblk)
            if evac == "act":
                Sc.op("act", lambda e: e.copy(out=dst_ap3, in_=src3), reads=[bpt], writes=[bdst])
            else:
                Sc.op("dve", lambda e: e.tensor_copy(out=dst_ap3, in_=src3), reads=[bpt], writes=[bdst])

        ph1 = ExitStack()
        with ph1:
            win, bwin = sb(ph1, "win", [128, 8, INW], BF16)
            wkd, bwkd = sb(ph1, "wkd", [128, 8, 2, 128], BF16)
            ph0 = ExitStack()
            ph0.__enter__()
            stg = [sb(ph0, "stg%d" % i, [128, INW]) for i in range(2)]
            stb = [sb(ph0, "stb%d" % i, [128, FF], BF16) for i in range(2)]
            for kc in range(8):
                st_, bst = stg[kc % 2]
                Sc.dma(st_[:], win_d[kc * 128:(kc + 1) * 128, :], writes=[bst])
                Sc.op("dve", lambda e, kc=kc, st_=st_: e.tensor_scalar(
                    out=win[:, kc, :], in0=st_[:], scalar1=cv[:, V_GMIX + kc:V_GMIX + kc + 1], scalar2=None,
                    op0=ALU.mult), reads=[bst, bcv], writes=[bwin])
            for g in range(2):
                for hf in range(2):
                    Sc.op("pool", lambda e, g=g, hf=hf: e.tensor_copy(
                        out=wkd[:, :, g, hf * 64:(hf + 1) * 64], in_=win[:, :, 512 + 64 * g:512 + 64 * g + 64]),
                        reads=[bwin], writes=[bwkd])
            cnt = [0]

            def conv_w(src_rows, ncols, gain_col, dst_ap):
                i = cnt[0] % 2
                cnt[0] += 1
                st_, bst = stg[i]
                sb_, bsb = stb[i]
                Sc.dma(st_[:, 0:ncols], src_rows, writes=[bst])
                if gain_col is None:
                    Sc.op("pool", lambda e: e.tensor_copy(out=sb_[:, 0:ncols], in_=st_[:, 0:ncols]),
                          reads=[bst], writes=[bsb])
                else:
                    Sc.op("dve", lambda e: e.tensor_scalar(out=sb_[:, 0:ncols], in0=st_[:, 0:ncols],
                                                           scalar1=cv[:, gain_col:gain_col + 1], scalar2=None,
                                                           op0=ALU.mult), reads=[bst, bcv], writes=[bsb])
                return sb_, bsb

            for kc in range(8):
                sb_, bsb = conv_w(wout_d[kc * 128:(kc + 1) * 128, :], D, None, None)
                Sc.dma(WO_s[kc * 128:(kc + 1) * 128, :], sb_[:, 0:D], reads=[bsb], writes=[bW])
            for kc in range(8):
                sb_, bsb = conv_w(wpg_d[kc * 128:(kc + 1) * 128, :], D, V_GPLE + kc, None)
                Sc.dma(WPG_s[kc * 128:(kc + 1) * 128, :], sb_[:, 0:D], reads=[bsb], writes=[bW])
            for kc in range(2):
                sb_, bsb = conv_w(wple_d[kc * 128:(kc + 1) * 128, :], D, None, None)
                Sc.dma(WPL_s[kc * 128:(kc + 1) * 128, :], sb_[:, 0:D], reads=[bsb], writes=[bW])
            for kc in range(NFC):
                sb_, bsb = conv_w(wd_d[kc * 128:(kc + 1) * 128, :], D, None, None)
                Sc.dma(WD_s[kc * 128:(kc + 1) * 128, :], sb_[:, 0:D], reads=[bsb], writes=[bW])
            for (src, dst) in ((wg_d, WG_s), (wu_d, WU_s)):
                for kc in range(8):
                    sb_, bsb = conv_w(src[kc * 128:(kc + 1) * 128, :], FF, V_GFFN + kc, None)
                    Sc.dma(dst[:, :, kc, :].rearrange("f p j -> p f j"),
                           sb_[:, 0:FF].rearrange("p (f j) -> p f j", f=NFC), reads=[bsb], writes=[bW])

            ph0.close()
            xt = [sb(ph1, "xt%d" % i, [128, D]) for i in range(2)]
            hnb = [sb(ph1, "hnb%d" % i, [128, D], BF16) for i in range(2)]
            junk, bjunk = sb(ph1, "junk", [128, D], BF16)
            st4, bst4 = sb(ph1, "st4", [128, 8])
            hnT = [sb(ph1, "hnT%d" % i, [128, 8, 512], BF16) for i in range(2)]
            pre = [sb(ph1, "pre%d" % i, [128, 12, 516]) for i in range(2)]
            cosb = [sb(ph1, "cosb%d" % i, [128, 512]) for i in range(2)]
            sinb = [sb(ph1, "sinb%d" % i, [128, 512]) for i in range(2)]
            wk = [sb(ph1, "wk%d" % i, [128, 512]) for i in range(8)]
            wkn = [0]

            def work():
                i = wkn[0]
                wkn[0] = (i + 1) % 8
                return wk[i]
            qkout = [sb(ph1, "qko%d" % i, [128, 512], BF16) for i in range(2)]
            vbt = [sb(ph1, "vbt%d" % i, [128, 128], BF16) for i in range(2)]
            zt = [sb(ph1, "zt%d" % i, [128, 512]) for i in range(2)]
            gbw, bgbw = sb(ph1, "gbw", [128, 64])
            gbo = [sb(ph1, "gbo%d" % i, [128, 16]) for i in range(2)]
            tkm = [sb(ph1, "tkm%d" % i, [128, 4, 128]) for i in range(2)]

            def qk_post(pt, bpt, kind, c, b):
                xs, bxs = work()
                Sc.op("act", lambda e: e.copy(out=xs[:], in_=pt[:]), reads=[bpt], writes=[bxs])
                sq, bsq = work()
                Sc.op("pool", lambda e: e.tensor_tensor(out=sq[:], in0=xs[:], in1=xs[:], op=ALU.mult),
                      reads=[bxs], writes=[bsq])
                p2, bp2 = bank()
                Sc.op("pe", lambda e: e.matmul(p2[:], lhsT=cm[:, M_BLK, :], rhs=sq[:], start=True, stop=True),
                      reads=[bcm, bsq], writes=[bp2])
                rn, brn = work()
                Sc.op("act", lambda e: e.activation(out=rn[:], in_=p2[:], func=AF.Ln, scale=1.0 / 64,
                                                    bias=epst[:, 0:1]), reads=[bp2, bepst], writes=[brn])
                Sc.op("act", lambda e: e.activation(out=rn[:], in_=rn[:], func=AF.Exp, scale=-0.5),
                      reads=[brn], writes=[brn])
                gcol = V_QNG if kind == "q" else V_KNG
                xn, bxn = work()
                Sc.op("dve", lambda e: e.scalar_tensor_tensor(out=xn[:], in0=xs[:], scalar=cv[:, gcol:gcol + 1],
                                                              in1=rn[:], op0=ALU.mult, op1=ALU.mult),
                      reads=[bxs, bcv, brn], writes=[bxn])
                p3, bp3 = bank()
                Sc.op("pe", lambda e: e.matmul(p3[:], lhsT=cm[:, M_RM, :], rhs=xn[:], start=True, stop=True),
                      reads=[bcm, bxn], writes=[bp3])
                t1, bt1 = work()
                Sc.op("pool", lambda e: e.tensor_tensor(out=t1[:], in0=xn[:], in1=cosb[b % 2][0][:], op=ALU.mult),
                      reads=[bxn, cosb[b % 2][1]], writes=[bt1])
                t2, bt2 = work()
                Sc.op("dve", lambda e: e.tensor_tensor(out=t2[:], in0=p3[:], in1=sinb[b % 2][0][:], op=ALU.mult),
                      reads=[bp3, sinb[b % 2][1]], writes=[bt2])
                qo, bqo = qkout[c % 2]
                Sc.op("dve", lambda e: e.tensor_tensor(out=qo[:], in0=t1[:], in1=t2[:], op=ALU.add),
                      reads=[bt1, bt2], writes=[bqo])
                if kind == "q":
                    Sc.dma(QT_s[c, :, b * 512:(b + 1) * 512], qo[:], reads=[bqo], writes=[bQT])
                else:
                    Sc.dma(KT_s[c, :, b * 512:(b + 1) * 512], qo[:], reads=[bqo], writes=[bKT])

            def dn_post(b):
                pr, bpr = pre[b % 2]
                for ci in range(12):
                    eng = "dve"
                    acc, bacc = work()
                    Sc.op(eng, lambda e: e.tensor_scalar(out=acc[:], in0=pr[:, ci, 0:512],
                                                         scalar1=cv[:, V_CONV + ci * 5:V_CONV + ci * 5 + 1],
                                                         scalar2=None, op0=ALU.mult),
                          reads=[bpr, bcv], writes=[bacc])
                    for tap in range(1, 5):
                        Sc.op(eng, lambda e, tap=tap: e.scalar_tensor_tensor(
                            out=acc[:], in0=pr[:, ci, tap:tap + 512],
                            scalar=cv[:, V_CONV + ci * 5 + tap:V_CONV + ci * 5 + tap + 1],
                            in1=acc[:], op0=ALU.mult, op1=ALU.add), reads=[bpr, bcv, bacc], writes=[bacc])
                    s, bs = work()
                    Sc.op("act", lambda e: e.activation(out=s[:], in_=acc[:], func=AF.Silu), reads=[bacc], writes=[bs])
                    h = ci % 4
                    if ci < 8:
                        sq, bsq = work()
                        Sc.op("pool", lambda e: e.tensor_tensor(out=sq[:], in0=s[:], in1=s[:], op=ALU.mult),
                              reads=[bs], writes=[bsq])
                        p2, bp2 = bank()
                        Sc.op("pe", lambda e: e.matmul(p2[:], lhsT=cm[:, M_ONES, :], rhs=sq[:], start=True, stop=True),
                              reads=[bcm, bsq], writes=[bp2])
                        rn, brn = work()
                        Sc.op("act", lambda e: e.activation(out=rn[:], in_=p2[:], func=AF.Ln, scale=1.0,
                                                            bias=epst[:, 0:1]), reads=[bp2, bepst], writes=[brn])
                        Sc.op("act", lambda e: e.activation(out=rn[:], in_=rn[:], func=AF.Exp, scale=-0.5),
                              reads=[brn], writes=[brn])
                        o, bo = work()
                        sc_ = (128.0 ** -0.5) if ci < 4 else 1.0
                        Sc.op("dve", lambda e: e.scalar_tensor_tensor(out=o[:], in0=s[:], scalar=sc_, in1=rn[:],
                                                                      op0=ALU.mult, op1=ALU.mult),
                              reads=[bs, brn], writes=[bo])
                        dst = DQT_s if ci < 4 else DKT_s
                        Sc.dma(dst[h, :, b * 512:(b + 1) * 512], o[:], reads=[bo], writes=[bDN])
                    else:
                        o, bo = s, bs
                    if ci >= 4:
                        pt, bpt = bank()
                        for tt in range(4):
                            Sc.op("pe", lambda e, tt=tt: e.transpose(out=pt[:, tt * 128:(tt + 1) * 128],
                                                                     in_=o[:, tt * 128:(tt + 1) * 128],
                                                                     identity=ident),
                                  reads=[bo, bcm], writes=[bpt], signal=(tt == 3))
                        tk, btk = tkm[ci % 2]
                        Sc.op("act", lambda e: e.copy(out=tk[:], in_=pt[:].rearrange("p (t d) -> p t d", t=4)),
                              reads=[bpt], writes=[btk])
                        dst = DK_s if ci < 8 else DV_s
                        Sc.dma(dst[b * 512:(b + 1) * 512, h, :].rearrange("(t p) d -> p t d", p=128), tk[:],
                               reads=[btk], writes=[bDN])

            for b in range(NB if PHMAX >= 1 else 0):
                hT, bhT = hnT[b % 2]
                Sc.dma(cosb[b % 2][0][:], cos_d[:, b * 512:(b + 1) * 512], writes=[cosb[b % 2][1]])
                Sc.dma(sinb[b % 2][0][:], sin_d[:, b * 512:(b + 1) * 512], writes=[sinb[b % 2][1]])
                for tt in range(4):
                    ti = 4 * b + tt
                    x_, bx_ = xt[ti % 2]
                    h_, bh_ = hnb[ti % 2]
                    Sc.dma(x_[:], x_d[ti * 128:(ti + 1) * 128, :], writes=[bx_])
                    Sc.op("pool", lambda e: e.memset(st4[:, 0:1], 0.0), writes=[bst4])
                    Sc.op("act", lambda e: e.activation(out=junk[:], in_=x_[:], func=AF.Square,
                                                        accum_out=st4[:, 0:1]), reads=[bx_, bst4], writes=[bjunk, bst4])
                    rstd_from_ss(st4[:, 0:1], st4[:, 2:3], st4[:, 1:2], 1.0 / D, [bst4])
                    Sc.op("dve", lambda e: e.tensor_scalar(out=h_[:], in0=x_[:], scalar1=st4[:, 2:3], scalar2=None,
                                                           op0=ALU.mult), reads=[bx_, bst4], writes=[bh_])
                    transpose_bf(h_, bh_, 8, hT[:, :, tt * 128:(tt + 1) * 128], bhT)
                chunks = [("q", c, win, lambda kc, c=c: win[:, kc, c * 128:(c + 1) * 128]) for c in range(4)]
                chunks += [("k", g, wkd, lambda kc, g=g: wkd[:, kc, g, :]) for g in range(2)]
                chunks += [("d", ci, win, lambda kc, ci=ci: win[:, kc, 768 + ci * 128:768 + (ci + 1) * 128])
                           for ci in range(12)]
                for kind, c, _, wsel in chunks:
                    pt, bpt = bank()
                    for kc in range(8):
                        Sc.op("pe", lambda e, kc=kc: e.matmul(pt[:], lhsT=wsel(kc), rhs=hT[:, kc, :],
                                                             start=(kc == 0), stop=(kc == 7)),
                              reads=[bwin, bwkd, bhT], writes=[bpt], signal=(kc == 7))
                    if kind == "d":
                        Sc.op("act", lambda e: e.copy(out=pre[b % 2][0][:, c, 2:514], in_=pt[:]),
                              reads=[bpt], writes=[pre[b % 2][1]])
                    else:
                        qk_post(pt, bpt, kind, c, b)
                for tt in range(4):
                    ti = 4 * b + tt
                    pt, bpt = bank()
                    for kc in range(8):
                        Sc.op("pe", lambda e, kc=kc: e.matmul(pt[:, 0:128], lhsT=hT[:, kc, tt * 128:(tt + 1) * 128],
                                                             rhs=win[:, kc, 640:768], start=(kc == 0), stop=(kc == 7)),
                              reads=[bwin, bhT], writes=[bpt], signal=False)
                    for kc in range(8):
                        Sc.op("pe", lambda e, kc=kc: e.matmul(pt[:, 128:144], lhsT=hT[:, kc, tt * 128:(tt + 1) * 128],
                                                             rhs=win[:, kc, 2816:2832], start=(kc == 0), stop=(kc == 7)),
                              reads=[bwin, bhT], writes=[bpt], signal=(kc == 7))
                    vb, bvb = vbt[ti % 2]
                    Sc.op("act", lambda e: e.copy(out=vb[:], in_=pt[:, 0:128]), reads=[bpt], writes=[bvb])
                    Sc.dma(V_s[ti * 128:(ti + 1) * 128, :], vb[:], reads=[bvb], writes=[bV])
                    go, bgo = gbo[ti % 2]
                    Sc.op("act", lambda e: e.activation(out=gbw[:, 0:8], in_=pt[:, 128:136], func=AF.Exp, scale=-1.0),
                          reads=[bpt], writes=[bgbw])
                    Sc.op("dve", lambda e: e.tensor_scalar(out=gbw[:, 0:8], in0=gbw[:, 0:8], scalar1=1.0, scalar2=None,
                                                           op0=ALU.add), reads=[bgbw], writes=[bgbw])
                    Sc.op("dve", lambda e: e.reciprocal(out=go[:, 8:16], in_=gbw[:, 0:8]), reads=[bgbw], writes=[bgo])
                    Sc.op("dve", lambda e: e.tensor_tensor(out=gbw[:, 8:16], in0=pt[:, 136:144],
                                                           in1=cv[:, V_DTB:V_DTB + 8], op=ALU.add),
                          reads=[bpt, bcv], writes=[bgbw])
                    Sc.op("act", lambda e: e.activation(out=gbw[:, 16:24], in_=gbw[:, 8:16], func=AF.Exp),
                          reads=[bgbw], writes=[bgbw])
                    Sc.op("act", lambda e: e.activation(out=gbw[:, 24:32], in_=gbw[:, 16:24], func=AF.Ln,
                                                        bias=epst[:, 1:2]), reads=[bgbw, bepst], writes=[bgbw])
                    Sc.op("dve", lambda e: e.tensor_tensor(out=go[:, 0:8], in0=gbw[:, 24:32], in1=nexpA[:], op=ALU.mult),
                          reads=[bgbw, bnexpA], writes=[bgo])
                    Sc.dma(GB_s[ti * 128:(ti + 1) * 128, :], go[:], reads=[bgo], writes=[bDN])
                    pz, bpz = bank()
                    for kc in range(8):
                        Sc.op("pe", lambda e, kc=kc: e.matmul(pz[:], lhsT=hT[:, kc, tt * 128:(tt + 1) * 128],
                                                             rhs=win[:, kc, 2304:2816], start=(kc == 0), stop=(kc == 7)),
                              reads=[bwin, bhT], writes=[bpz], signal=(kc == 7))
                    z_, bz_ = zt[ti % 2]
                    Sc.op("act", lambda e: e.copy(out=z_[:], in_=pz[:]), reads=[bpz], writes=[bz_])
                    Sc.dma(DZ_s[ti * 128:(ti + 1) * 128, :], z_[:], reads=[bz_], writes=[bDN])
                pr, bpr = pre[b % 2]
                if b == 0:
                    Sc.op("pool", lambda e: e.memset(pr[:, :, 0:2], 0.0), writes=[bpr])
                else:
                    pp, bpp = pre[(b - 1) % 2]
                    Sc.op("pool", lambda e: e.tensor_copy(out=pr[:, :, 0:2], in_=pp[:, :, 512:514]),
                          reads=[bpp], writes=[bpr])
                    Sc.op("pool", lambda e: e.tensor_copy(out=pp[:, :, 514:516], in_=pr[:, :, 2:4]),
                          reads=[bpr], writes=[bpp])
                    dn_post(b - 1)
                if b == NB - 1:
                    Sc.op("pool", lambda e: e.memset(pr[:, :, 514:516], 0.0), writes=[bpr])
                    dn_post(b)

        ph2 = ExitStack()
        with ph2:
            NW = 30
            dw = [[sb(ph2, "dw%d_%d" % (d, i), [128, 4, 128]) for i in range(NW)] for d in range(2)]
            dinp = [[[sb(ph2, "di%d_%d_%d" % (d, q, i), [128, 4, 128]) for i in range(4)] for q in range(2)]
                    for d in range(2)]
            Sst = [sb(ph2, "Sst%d" % d, [128, 4, 128]) for d in range(2)]
            dsm = [[sb(ph2, "dsm%d_%d" % (d, i), [128, 32]) for i in range(2)] for d in range(2)]
            for d in range(2):
                Sc.op("pool", lambda e, d=d: e.memset(Sst[d][0][:], 0.0), writes=[Sst[d][1]])

            def bc_h(ap_h):
                return ap_h.unsqueeze(2).broadcast_to([128, 4, 128])

            def bc_m(ap_m):
                return ap_m.unsqueeze(1).broadcast_to([128, 4, 128])

            def dn_unit(t, d, step):
                W_ = dw[d]
                wi = [0]

                def wt():
                    r = W_[wi[0]]
                    wi[0] += 1
                    return r
                s0 = t * 128
                (kT, bkT), (qT, bqT), (ktok, bktok), (vtok, bvtok) = dinp[d][step % 2]
                sm, bsm = dsm[d][step % 2]
                S_, bS_ = Sst[d]
                Sc.dma(kT[:], DKT_s[:, :, s0:s0 + 128].rearrange("h p s -> p h s"), reads=[bDN], writes=[bkT])
                Sc.dma(qT[:], DQT_s[:, :, s0:s0 + 128].rearrange("h p s -> p h s"), reads=[bDN], writes=[bqT])
                Sc.dma(ktok[:], DK_s[s0:s0 + 128, :, :], reads=[bDN], writes=[bktok])
                Sc.dma(vtok[:], DV_s[s0:s0 + 128, :, :], reads=[bDN], writes=[bvtok])
                Sc.dma(sm[:, 0:16], GB_s[s0:s0 + 128, :], reads=[bDN], writes=[bsm])
                g_ap = sm[:, 4 * d:4 * d + 4]
                beta_ap = sm[:, 8 + 4 * d:8 + 4 * d + 4]
                tri = M_UPPI if d == 0 else M_LOWI
                m_incl = M_LOWI if d == 0 else M_UPPI
                m_inclT = M_UPPI if d == 0 else M_LOWI
                m_str = M_LOWS if d == 0 else M_UPPS
                yield
                pa, bpa = bank()
                for j, mi in enumerate((tri, M_BLK, M_CI0, M_CI1)):
                    Sc.op("pe", lambda e, j=j, mi=mi: e.matmul(pa[:, 16 * j:16 * j + 16], lhsT=cm[:, mi, :], rhs=sm[:, 0:16],
                                                               start=True, stop=True),
                          reads=[bcm, bsm], writes=[bpa], signal=(j == 3))
                Sc.op("dve", lambda e: e.tensor_copy(
                    out=sm[:, 16:32].rearrange("p (j c) -> p j c", j=4),
                    in_=pa[:, 0:64].rearrange("p (j c) -> p j c", j=4)[:, :, 4 * d:4 * d + 4]), reads=[bpa], writes=[bsm])
                sm2, bsm2 = wt()
                sm2f = sm2[:].rearrange("p h d -> p (h d)")
                Sc.op("act", lambda e: e.activation(out=sm2f[:, 0:16], in_=sm[:, 16:32], func=AF.Exp),
                      reads=[bsm], writes=[bsm2])
                Sc.op("dve", lambda e: e.tensor_tensor(out=sm2f[:, 16:20], in0=sm[:, 20:24], in1=sm[:, 16:20],
                                                       op=ALU.subtract), reads=[bsm], writes=[bsm2])
                Sc.op("act", lambda e: e.activation(out=sm2f[:, 20:24], in_=sm2f[:, 16:20], func=AF.Exp),
                      reads=[bsm2], writes=[bsm2])
                Sc.op("dve", lambda e: e.tensor_tensor(out=sm2f[:, 24:28], in0=beta_ap, in1=sm2f[:, 0:4], op=ALU.mult),
                      reads=[bsm, bsm2], writes=[bsm2])
                G_ap = sm[:, 16:20]
                eGl_ap = sm2f[:, 20:24]
                beG_ap = sm2f[:, 24:28]
                gt_ap = [sm2f[:, 8:12], sm2f[:, 12:16]]
                yield
                dg, bdg = wt()
                Sc.op("dve", lambda e: e.tensor_tensor(out=dg[:], in0=bc_m(ident), in1=bc_h(G_ap), op=ALU.mult),
                      reads=[bcm, bsm], writes=[bdg])
                pB, bpB = bank()
                for h in range(4):
                    Sc.op("pe", lambda e, h=h: e.matmul(pB[:, h * 128:(h + 1) * 128], lhsT=cm[:, M_ONES, :],
                                                        rhs=dg[:, h, :], start=True, stop=True),
                          reads=[bcm, bdg], writes=[bpB], signal=(h == 3))
                pB3 = pB[:].rearrange("p (h d) -> p h d", h=4)
                t1, bt1 = wt()
                ebc, bebc = wt()
                Sc.op("act", lambda e: e.copy(out=ebc[:], in_=pB3), reads=[bpB], writes=[bebc])
                Sc.op("dve", lambda e: e.tensor_tensor(out=t1[:], in0=ebc[:], in1=bc_h(G_ap), op=ALU.subtract),
                      reads=[bebc, bsm], writes=[bt1])
                Sc.op("act", lambda e: e.activation(out=ebc[:], in_=ebc[:], func=AF.Exp), reads=[bebc, bt1], writes=[bebc])
                ta, bta = wt()
                tb, btb = wt()
                Sc.op("dve", lambda e: e.tensor_scalar(out=ta[:], in0=t1[:], scalar1=0.0, scalar2=-1.0,
                                                       op0=ALU.max, op1=ALU.mult), reads=[bt1], writes=[bta])
                Sc.op("pool", lambda e: e.tensor_scalar(out=tb[:], in0=t1[:], scalar1=0.0, scalar2=None,
                                                        op0=ALU.min), reads=[bt1], writes=[btb])
                Sc.op("act", lambda e: e.activation(out=ta[:], in_=ta[:], func=AF.Exp), reads=[bta], writes=[bta])
                Sc.op("act", lambda e: e.activation(out=tb[:], in_=tb[:], func=AF.Exp), reads=[btb], writes=[btb])
                yield
                DmS, bDmS = wt()
                DmT, bDmT = wt()
                Sc.op("pool", lambda e: e.tensor_tensor(out=DmS[:], in0=ta[:], in1=bc_m(cm[:, m_str, :]), op=ALU.mult),
                      reads=[bta, bcm], writes=[bDmS])
                Sc.op("pool", lambda e: e.tensor_tensor(out=DmT[:], in0=tb[:], in1=bc_m(cm[:, m_inclT, :]), op=ALU.mult),
                      reads=[btb, bcm], writes=[bDmT])
                pC, bpC = bank()
                pD, bpD = bank()
                for h in range(4):
                    Sc.op("pe", lambda e, h=h: e.matmul(pC[:, h * 128:(h + 1) * 128], lhsT=kT[:, h, :], rhs=kT[:, h, :],
                                                        start=True, stop=True), reads=[bkT], writes=[bpC],
                          signal=(h == 3))
                for h in range(4):
                    Sc.op("pe", lambda e, h=h: e.matmul(pD[:, h * 128:(h + 1) * 128], lhsT=kT[:, h, :], rhs=qT[:, h, :],
                                                        start=True, stop=True), reads=[bkT, bqT], writes=[bpD],
                          signal=(h == 3))
                PB = [wt(), wt()]
                QB = [wt(), wt()]
                WB = [wt(), wt()]
                P_, bP_ = PB[0]
                Sc.op("dve", lambda e: e.tensor_tensor(out=P_[:], in0=pC[:].rearrange("p (h d) -> p h d", h=4),
                                                       in1=DmS[:], op=ALU.mult), reads=[bpC, bDmS], writes=[bP_])
                Sc.op("dve", lambda e: e.tensor_tensor(out=P_[:], in0=P_[:], in1=bc_h(beta_ap), op=ALU.mult),
                      reads=[bP_, bsm], writes=[bP_])
                QKDT, bQKDT = wt()
                Sc.op("dve", lambda e: e.tensor_tensor(out=QKDT[:], in0=pD[:].rearrange("p (h d) -> p h d", h=4),
                                                       in1=DmT[:], op=ALU.mult), reads=[bpD, bDmT], writes=[bQKDT])
                yield
                pE, bpE = bank()
                for h in range(4):
                    Sc.op("pe", lambda e, h=h: e.transpose(out=pE[:, h * 128:(h + 1) * 128], in_=P_[:, h, :],
                                                           identity=ident), reads=[bP_, bcm], writes=[bpE],
                          signal=(h == 3))
                pE3 = pE[:].rearrange("p (h d) -> p h d", h=4)
                Q_, bQ_ = QB[0]
                Wc, bWc = WB[0]
                Sc.op("act", lambda e: e.copy(out=Q_[:], in_=pE3), reads=[bpE], writes=[bQ_])
                Sc.op("dve", lambda e: e.tensor_tensor(out=Wc[:], in0=bc_m(ident), in1=Q_[:], op=ALU.subtract),
                      reads=[bcm, bQ_], writes=[bWc])
                yield
                for lvl in range(1, 6):
                    pX, bpX = bank()
                    for h in range(4):
                        Sc.op("pe", lambda e, h=h: e.matmul(pX[:, h * 128:(h + 1) * 128], lhsT=Q_[:, h, :],
                                                            rhs=P_[:, h, :], start=True, stop=True),
                              reads=[bQ_, bP_], writes=[bpX], signal=(h == 3))
                    if lvl < 5:
                        pY, bpY = bank()
                        for h in range(4):
                            Sc.op("pe", lambda e, h=h: e.matmul(pY[:, h * 128:(h + 1) * 128], lhsT=P_[:, h, :],
                                                                rhs=Q_[:, h, :], start=True, stop=True),
                                  reads=[bQ_, bP_], writes=[bpY], signal=(h == 3))
                    Pn, bPn = PB[lvl % 2]
                    Sc.op("act", lambda e: e.copy(out=Pn[:], in_=pX[:].rearrange("p (h d) -> p h d", h=4)),
                          reads=[bpX], writes=[bPn])
                    if lvl < 5:
                        Qn, bQn = QB[lvl % 2]
                        Sc.op("dve", lambda e: e.tensor_copy(out=Qn[:], in_=pY[:].rearrange("p (h d) -> p h d", h=4)),
                              reads=[bpY], writes=[bQn])
                    pZ, bpZ = bank()
                    for h in range(4):
                        Sc.op("pe", lambda e, h=h: e.matmul(pZ[:, h * 128:(h + 1) * 128], lhsT=Pn[:, h, :],
                                                            rhs=Wc[:, h, :], start=True, stop=True),
                              reads=[bPn, bWc], writes=[bpZ], signal=(h == 3))
                    Wn, bWn = WB[lvl % 2]
                    Sc.op("dve", lambda e: e.tensor_tensor(out=Wn[:], in0=pZ[:].rearrange("p (h d) -> p h d", h=4),
                                                           in1=Wc[:], op=ALU.add), reads=[bpZ, bWc], writes=[bWn])
                    Wc, bWc = Wn, bWn
                    P_, bP_ = Pn, bPn
                    if lvl < 5:
                        Q_, bQ_ = Qn, bQn
                    yield
                T1, bT1 = wt()
                T2, bT2 = wt()
                Sc.op("pool", lambda e: e.tensor_tensor(out=T1[:], in0=Wc[:], in1=bc_h(beta_ap), op=ALU.mult),
                      reads=[bWc, bsm], writes=[bT1])
                Sc.op("dve", lambda e: e.tensor_tensor(out=T2[:], in0=Wc[:], in1=bc_h(beG_ap), op=ALU.mult),
                      reads=[bWc, bsm2], writes=[bT2])
                pU, bpU = bank()
                pW, bpW = bank()
                for h in range(4):
                    Sc.op("pe", lambda e, h=h: e.matmul(pU[:, h * 128:(h + 1) * 128], lhsT=T1[:, h, :],
                                                        rhs=vtok[:, h, :], start=True, stop=True),
                          reads=[bT1, bvtok], writes=[bpU], signal=(h == 3))
                for h in range(4):
                    Sc.op("pe", lambda e, h=h: e.matmul(pW[:, h * 128:(h + 1) * 128], lhsT=ktok[:, h, :],
                                                        rhs=T2[:, h, :], start=True, stop=True),
                          reads=[bT2, bktok], writes=[bpW], signal=(h == 3))
                u_, bu_ = wt()
                wT_, bwT_ = wt()
                Sc.op("act", lambda e: e.copy(out=u_[:], in_=pU[:].rearrange("p (h d) -> p h d", h=4)),
                      reads=[bpU], writes=[bu_])
                Sc.op("dve", lambda e: e.tensor_copy(out=wT_[:], in_=pW[:].rearrange("p (h d) -> p h d", h=4)),
                      reads=[bpW], writes=[bwT_])
                kdec, bkdec = wt()
                qdT, bqdT = wt()
                Sc.op("pool", lambda e: e.tensor_tensor(out=kdec[:], in0=ktok[:], in1=bc_h(eGl_ap), op=ALU.mult),
                      reads=[bktok, bsm2], writes=[bkdec])
                Sc.op("pool", lambda e: e.tensor_tensor(out=qdT[:], in0=qT[:], in1=ebc[:], op=ALU.mult),
                      reads=[bqT, bebc], writes=[bqdT])
                vn, bvn = wt()
                stmp, bstmp = wt()
                if step == 0:
                    Sc.op("pool", lambda e: e.memset(vn[:], 0.0), writes=[bvn])
                yield
                ot, bot = wt()
                corder = (0, 1) if d == 0 else (1, 0)
                for c in corder:
                    c0 = 64 * c
                    cs = slice(c0, c0 + 64)
                    tp = (c0, 0) if c0 else None
                    pV, bpV = bank()
                    for h in range(4):
                        Sc.op("pe", lambda e, h=h: e.matmul(pV[:, h * 128:(h + 1) * 128], lhsT=wT_[:, h, :],
                                                            rhs=S_[:, h, :], start=True, stop=True),
                              reads=[bwT_, bS_], writes=[bpV], signal=(h == 3))
                    Sc.op("dve", lambda e: e.tensor_tensor(out=vn[cs], in0=u_[cs],
                                                           in1=pV[cs, :].rearrange("p (h d) -> p h d", h=4),
                                                           op=ALU.subtract), reads=[bu_, bpV], writes=[bvn])
                    pO, bpO = bank()
                    for h in range(4):
                        Sc.op("pe", lambda e, h=h: e.matmul(pO[:, h * 128:(h + 1) * 128], lhsT=qdT[:, h, :],
                                                            rhs=S_[:, h, :], start=True, stop=False),
                              reads=[bqdT, bS_], writes=[bpO], signal=False)
                        Sc.op("pe", lambda e, h=h: e.matmul(pO[:, h * 128:(h + 1) * 128], lhsT=QKDT[:, h, :],
                                                            rhs=vn[:, h, :], start=False, stop=True),
                              reads=[bQKDT, bvn], writes=[bpO], signal=(h == 3))
                    pS, bpS = bank()
                    for h in range(4):
                        Sc.op("pe", lambda e, h=h: e.matmul(pS[:, h * 128:(h + 1) * 128], lhsT=kdec[cs, h, :],
                                                            rhs=vn[cs, h, :], start=True, stop=True,
                                                            tile_position=tp),
                              reads=[bkdec, bvn], writes=[bpS], signal=(h == 3))
                    Sc.op("act", lambda e: e.copy(out=ot[cs], in_=pO[cs, :].rearrange("p (h d) -> p h d", h=4)),
                          reads=[bpO], writes=[bot])
                    Sc.op("pool", lambda e: e.tensor_tensor(out=stmp[:], in0=S_[:], in1=bc_h(gt_ap[c]), op=ALU.mult),
                          reads=[bS_, bsm2], writes=[bstmp])
                    Sc.op("dve", lambda e: e.tensor_tensor(out=S_[:], in0=stmp[:],
                                                           in1=pS[:].rearrange("p (h d) -> p h d", h=4), op=ALU.add),
                          reads=[bstmp, bpS], writes=[bS_])
                    yield
                Sc.dma(OF_s[d, s0:s0 + 128, :].rearrange("p (h d) -> p h d", h=4), ot[:], reads=[bot], writes=[bOF])

            for step in range(T if PHMAX >= 2 else 0):
                gens = [dn_unit(step, 0, step), dn_unit(T - 1 - step, 1, step)]
                for _ in zip_longest(*gens):
                    pass

        ph3 = ExitStack()
        with ph3:
            KT2, bKT2 = sb(ph3, "KT2", [128, 2, S], BF16)
            Vaug, bVaug = sb(ph3, "Vaug", [128, T, 2, 65], BF16)
            onesr, bonesr = sb(ph3, "onesr", [128, 64])
            Sc.op("pool", lambda e: e.memset(Vaug[:], 1.0), writes=[bVaug])
            Sc.op("pool", lambda e: e.memset(onesr[:], 1.0), writes=[bonesr])
            for g in range(2):
                Sc.dma(KT2[:, g, :], KT_s[g], reads=[bKT], writes=[bKT2])
            TG = 8
            for t0 in range(0, T, TG):
                nt = min(TG, T - t0)
                for g in range(2):
                    Sc.dma(Vaug[:, t0:t0 + nt, g, 0:64],
                           V_s[t0 * 128:(t0 + nt) * 128, g * 64:(g + 1) * 64].rearrange("(t p) d -> p t d", p=128),
                           reads=[bV], writes=[bVaug])
            qtc = [sb(ph3, "qtc%d" % i, [128, 512], BF16) for i in range(2)]
            PT = [sb(ph3, "PT%d" % i, [128, 2, 512], BF16) for i in range(3)]
            oa = [sb(ph3, "oa%d" % i, [128, 512]) for i in range(2)]
            rs_, brs_ = sb(ph3, "rs", [128, 512])
            aob = [sb(ph3, "aob%d" % i, [64, 512], BF16) for i in range(2)]
            it = 0
            for qb in range(NB if PHMAX >= 3 else 0):
                for c in range(4):
                    g = c // 2
                    q_, bq_ = qtc[it % 2]
                    it += 1
                    Sc.dma(q_[:], QT_s[c, :, qb * 512:(qb + 1) * 512], reads=[bQT], writes=[bq_])
                    pOa, bpOa = pb[6]
                    pOb, bpOb = pb[7]

                    def qk(kt):
                        pSa, bpSa = pb[2 * (kt % 2)]
                        pSb, bpSb = pb[2 * (kt % 2) + 1]
                        Sc.op("pe", lambda e: e.matmul(pSa[:], lhsT=KT2[0:64, g, kt * 128:(kt + 1) * 128],
                                                       rhs=q_[0:64, :], start=True, stop=True),
                              reads=[bKT2, bq_], writes=[bpSa], signal=False)
                        Sc.op("pe", lambda e: e.matmul(pSb[:], lhsT=KT2[64:128, g, kt * 128:(kt + 1) * 128],
                                                       rhs=q_[64:128, :], start=True, stop=True,
                                                       tile_position=(64, 0)),
                              reads=[bKT2, bq_], writes=[bpSb])

                    qk(0)
                    for kt in range(T):
                        if kt + 1 < T:
                            qk(kt + 1)
                        bpSa = pb[2 * (kt % 2)][1]
                        bpSb = pb[2 * (kt % 2) + 1][1]
                        P_, bP_ = PT[kt % 3]
                        Sc.op("act", lambda e: e.activation(out=P_[:], in_=pairs[kt % 2][:], func=AF.Exp, scale=0.125),
                              reads=[bpSa, bpSb], writes=[bP_])
                        Sc.op("pe", lambda e: e.matmul(pOa[0:65, :], lhsT=Vaug[:, kt, g, :], rhs=P_[:, 0, :],
                                                       start=(kt == 0), stop=(kt == T - 1)),
                              reads=[bVaug, bP_], writes=[bpOa], signal=False)
                        Sc.op("pe", lambda e: e.matmul(pOb[0:65, :], lhsT=Vaug[:, kt, g, :], rhs=P_[:, 1, :],
                                                       start=(kt == 0), stop=(kt == T - 1)),
                              reads=[bVaug, bP_], writes=[bpOb])
                    for j, (pO_, bpO_) in enumerate(((pOa, bpOa), (pOb, bpOb))):
                        hh = 2 * c + j
                        Sc.op("dve", lambda e: e.reciprocal(out=rs_[64:65, :], in_=pO_[64:65, :]),
                              reads=[bpO_], writes=[brs_])
                        o_, bo_ = oa[j]
                        Sc.op("act", lambda e: e.copy(out=o_[0:64, :], in_=pO_[0:64, :]), reads=[bpO_], writes=[bo_])
                        pN, bpN = pb[4 + j]
                        Sc.op("pe", lambda e: e.matmul(pN[0:64, :], lhsT=onesr[64:65, 0:64], rhs=rs_[64:65, :],
                                                       start=True, stop=True, tile_position=(64, 0)),
                              reads=[bonesr, brs_], writes=[bpN])
                        ab, bab = aob[j]
                        Sc.op("dve", lambda e: e.tensor_tensor(out=ab[:], in0=o_[0:64, :], in1=pN[0:64, :], op=ALU.mult),
                              reads=[bo_, bpN], writes=[bab])
                        Sc.dma(AO_s[hh, :, qb * 512:(qb + 1) * 512], ab[:], reads=[bab], writes=[bAO])

        ph4 = ExitStack()
        with ph4:
            cg, bcg = sb(ph4, "cg", [128, 128 + D])
            Sc.dma(cg[:], cg_d, writes=[bcg])
            woA, bwoA = sb(ph4, "woA", [64, 8, D], BF16)
            woD, bwoD = sb(ph4, "woD", [128, 4, D], BF16)
            wdn, bwdn = sb(ph4, "wdn", [128, NFC, D], BF16)
            wpg, bwpg = sb(ph4, "wpg", [128, 8, D], BF16)
            wpl, bwpl = sb(ph4, "wpl", [128, 2, D], BF16)
            Sc.dma(woA[:], WO_s[0:512, :].rearrange("(h p) n -> p h n", p=64), reads=[bW], writes=[bwoA])
            Sc.dma(woD[:], WO_s[512:1024, :].rearrange("(c p) n -> p c n", p=128), reads=[bW], writes=[bwoD])
            for f0 in range(0, NFC, 6):
                f1 = min(NFC, f0 + 6)
                Sc.dma(wdn[:, f0:f1, :], WD_s[f0 * 128:f1 * 128, :].rearrange("(c p) n -> p c n", p=128),
                       reads=[bW], writes=[bwdn])
            Sc.dma(wpg[:], WPG_s.rearrange("(c p) n -> p c n", p=128), reads=[bW], writes=[bwpg])
            Sc.dma(wpl[:], WPL_s.rearrange("(c p) n -> p c n", p=128), reads=[bW], writes=[bwpl])
            wgs = [sb(ph4, "wgs%d" % i, [128, 8, 128], BF16) for i in range(2)]
            wus = [sb(ph4, "wus%d" % i, [128, 8, 128], BF16) for i in range(2)]
            hx = [sb(ph4, "hx%d" % i, [128, 4, D]) for i in range(1)]
            hT4, bhT4 = sb(ph4, "hT4", [128, 8, 512], BF16)
            hb4 = [sb(ph4, "hb4_%d" % i, [128, D], BF16) for i in range(2)]
            junk4, bjunk4 = sb(ph4, "junk4", [128, D], BF16)
            s4, bs4 = sb(ph4, "s4", [128, 16])
            actT, bactT = sb(ph4, "actT", [128, NFC, 512], BF16)
            aoT, baoT = sb(ph4, "aoT", [64, 8, 512], BF16)
            dnT, bdnT = sb(ph4, "dnT", [128, 4, 512], BF16)
            e4 = [sb(ph4, "e4_%d" % i, [128, 4, 128]) for i in range(6)]
            dnb = [sb(ph4, "dnb%d" % i, [128, 512], BF16) for i in range(2)]
            sg4 = [sb(ph4, "sg4_%d" % i, [128, 512]) for i in range(2)]
            pin, bpin = sb(ph4, "pin", [128, PLE])
            pinb, bpinb = sb(ph4, "pinb", [128, PLE], BF16)
            pT4, bpT4 = sb(ph4, "pT4", [128, 2, 512], BF16)
            yo = [sb(ph4, "yo%d" % i, [128, D]) for i in range(1)]

            def norm_to_T(src_ap, bsrc, tt, idx):
                Sc.op("pool", lambda e: e.memset(s4[:, 0:1], 0.0), writes=[bs4])
                Sc.op("act", lambda e: e.activation(out=junk4[:], in_=src_ap, func=AF.Square,
                                                    accum_out=s4[:, 0:1]), reads=[bsrc, bs4], writes=[bjunk4, bs4])
                rstd_from_ss(s4[:, 0:1], s4[:, 2:3], s4[:, 1:2], 1.0 / D, [bs4])
                h_, bh_ = hb4[idx % 2]
                Sc.op("dve", lambda e: e.tensor_scalar(out=h_[:], in0=src_ap, scalar1=s4[:, 2:3], scalar2=None,
                                                       op0=ALU.mult), reads=[bsrc, bs4], writes=[bh_])
                transpose_bf(h_, bh_, 8, hT4[:, :, tt * 128:(tt + 1) * 128], bhT4)

            for b in range(NB if PHMAX >= 4 else 0):
                H, bH = hx[0]
                for tt in range(4):
                    ti = 4 * b + tt
                    r0 = ti * 128
                    of_, bof_ = e4[0]
                    ob_, bob_ = e4[1]
                    z_, bz_ = e4[2]
                    Sc.dma(of_[:], OF_s[0, r0:r0 + 128, :].rearrange("p (h d) -> p h d", h=4), reads=[bOF], writes=[bof_])
                    Sc.dma(ob_[:], OF_s[1, r0:r0 + 128, :].rearrange("p (h d) -> p h d", h=4), reads=[bOF], writes=[bob_])
                    Sc.dma(z_[:], DZ_s[r0:r0 + 128, :].rearrange("p (h d) -> p h d", h=4), reads=[bDN], writes=[bz_])
                    o_, bo_ = e4[3]
                    Sc.op("pool", lambda e: e.tensor_tensor(out=o_[:], in0=of_[:], in1=ob_[:], op=ALU.add),
                          reads=[bof_, bob_], writes=[bo_])
                    sq_, bsq_ = e4[4]
                    Sc.op("pool", lambda e: e.tensor_tensor(out=sq_[:], in0=o_[:], in1=o_[:], op=ALU.mult),
                          reads=[bo_], writes=[bsq_])
                    Sc.op("dve", lambda e: e.tensor_reduce(out=s4[:, 4:8], in_=sq_[:], axis=AX.X, op=ALU.add),
                          reads=[bsq_], writes=[bs4])
                    rstd_from_ss(s4[:, 4:8], s4[:, 12:16], s4[:, 8:12], 1.0 / 128, [bs4])
                    Sc.op("dve", lambda e: e.tensor_tensor(out=o_[:], in0=o_[:],
                                                           in1=s4[:, 12:16].unsqueeze(2).broadcast_to([128, 4, 128]),
                                                           op=ALU.mult), reads=[bo_, bs4], writes=[bo_])
                    Sc.op("pool", lambda e: e.tensor_tensor(out=o_[:], in0=o_[:],
                                                            in1=cg[:, 0:128].unsqueeze(1).broadcast_to([128, 4, 128]),
                                                            op=ALU.mult), reads=[bo_, bcg], writes=[bo_])
                    zs_, bzs_ = e4[5]
                    Sc.op("act", lambda e: e.activation(out=zs_[:], in_=z_[:], func=AF.Silu), reads=[bz_], writes=[bzs_])
                    db_, bdb_ = dnb[tt % 2]
                    Sc.op("dve", lambda e: e.tensor_tensor(out=db_[:].rearrange("p (h d) -> p h d", h=4), in0=o_[:],
                                                           in1=zs_[:], op=ALU.mult), reads=[bo_, bzs_], writes=[bdb_])
                    transpose_bf(db_, bdb_, 4, dnT[:, :, tt * 128:(tt + 1) * 128], bdnT, evac="dve")
                Sc.dma(aoT[:], AO_s[:, :, b * 512:(b + 1) * 512].rearrange("h p s -> p h s"), reads=[bAO], writes=[baoT])
                for tt in range(4):
                    ti = 4 * b + tt
                    Sc.dma(H[:, tt, :], x_d[ti * 128:(ti + 1) * 128, :], writes=[bH])
                for tt in range(4):
                    ts_ = slice(tt * 128, (tt + 1) * 128)
                    for nh in range(2):
                        ns = slice(nh * 512, (nh + 1) * 512)
                        pt, bpt = bank()
                        for h in range(8):
                            Sc.op("pe", lambda e, h=h: e.matmul(pt[:], lhsT=aoT[:, h, ts_], rhs=woA[:, h, ns],
                                                                start=(h == 0), stop=False),
                                  reads=[baoT, bwoA], writes=[bpt], signal=False)
                        for cc in range(4):
                            Sc.op("pe", lambda e, cc=cc: e.matmul(pt[:], lhsT=dnT[:, cc, ts_], rhs=woD[:, cc, ns],
                                                                  start=False, stop=(cc == 3)),
                                  reads=[bdnT, bwoD], writes=[bpt], signal=(cc == 3))
                        Sc.op("dve", lambda e: e.tensor_tensor(out=H[:, tt, ns], in0=H[:, tt, ns], in1=pt[:], op=ALU.add),
                              reads=[bH, bpt], writes=[bH])
                    norm_to_T(H[:, tt, :], bH, tt, tt)
                for fc in range(NFC):
                    wg_, bwg_ = wgs[fc % 2]
                    wu_, bwu_ = wus[fc % 2]
                    Sc.dma(wg_[:], WG_s[fc], reads=[bW], writes=[bwg_])
                    Sc.dma(wu_[:], WU_s[fc], reads=[bW], writes=[bwu_])
                    pg, bpg = bank()
                    pu, bpu = bank()
                    for kc in range(8):
                        Sc.op("pe", lambda e, kc=kc: e.matmul(pg[:], lhsT=wg_[:, kc, :], rhs=hT4[:, kc, :],
                                                             start=(kc == 0), stop=(kc == 7)),
                              reads=[bwg_, bhT4], writes=[bpg], signal=(kc == 7))
                    for kc in range(8):
                        Sc.op("pe", lambda e, kc=kc: e.matmul(pu[:], lhsT=wu_[:, kc, :], rhs=hT4[:, kc, :],
                                                             start=(kc == 0), stop=(kc == 7)),
                              reads=[bwu_, bhT4], writes=[bpu], signal=(kc == 7))
                    sg_, bsg_ = sg4[fc % 2]
                    Sc.op("act", lambda e: e.activation(out=sg_[:], in_=pg[:], func=AF.Silu), reads=[bpg], writes=[bsg_])
                    Sc.op("dve", lambda e: e.tensor_tensor(out=actT[:, fc, :], in0=sg_[:], in1=pu[:], op=ALU.mult),
                          reads=[bsg_, bpu], writes=[bactT])
                for tt in range(4):
                    ts_ = slice(tt * 128, (tt + 1) * 128)
                    for nh in range(2):
                        ns = slice(nh * 512, (nh + 1) * 512)
                        pt, bpt = bank()
                        for fc in range(NFC):
                            Sc.op("pe", lambda e, fc=fc: e.matmul(pt[:], lhsT=actT[:, fc, ts_], rhs=wdn[:, fc, ns],
                                                                  start=(fc == 0), stop=(fc == NFC - 1)),
                                  reads=[bactT, bwdn], writes=[bpt], signal=(fc == NFC - 1))
                        Sc.op("dve", lambda e: e.tensor_tensor(out=H[:, tt, ns], in0=H[:, tt, ns], in1=pt[:], op=ALU.add),
                              reads=[bH, bpt], writes=[bH])
                    norm_to_T(H[:, tt, :], bH, tt, tt)
                for tt in range(4):
                    ti = 4 * b + tt
                    Sc.dma(pin[:], p_d[ti * 128:(ti + 1) * 128, :], writes=[bpin])
                    Sc.op("pool", lambda e: e.tensor_copy(out=pinb[:], in_=pin[:]), reads=[bpin], writes=[bpinb])
                    transpose_bf(pinb, bpinb, 2, pT4[:, :, tt * 128:(tt + 1) * 128], bpT4)
                for tt in range(4):
                    ti = 4 * b + tt
                    ts_ = slice(tt * 128, (tt + 1) * 128)
                    for nh in range(2):
                        ns = slice(nh * 512, (nh + 1) * 512)
                        pg, bpg = bank()
                        pl, bpl = bank()
                        for kc in range(8):
                            Sc.op("pe", lambda e, kc=kc: e.matmul(pg[:], lhsT=hT4[:, kc, ts_], rhs=wpg[:, kc, ns],
                                                                  start=(kc == 0), stop=(kc == 7)),
                                  reads=[bhT4, bwpg], writes=[bpg], signal=(kc == 7))
                        for kc in range(2):
                            Sc.op("pe", lambda e, kc=kc: e.matmul(pl[:], lhsT=pT4[:, kc, ts_], rhs=wpl[:, kc, ns],
                                                                  start=(kc == 0), stop=(kc == 1)),
                                  reads=[bpT4, bwpl], writes=[bpl], signal=(kc == 1))
                        sg_, bsg_ = sg4[nh]
                        Sc.op("act", lambda e: e.activation(out=sg_[:], in_=pg[:], func=AF.Sigmoid),
                              reads=[bpg], writes=[bsg_])
                        Sc.op("dve", lambda e: e.tensor_tensor(out=sg_[:], in0=sg_[:], in1=pl[:], op=ALU.mult),
                              reads=[bsg_, bpl], writes=[bsg_])
                        Sc.op("pool", lambda e: e.tensor_tensor(out=H[:, tt, ns], in0=H[:, tt, ns], in1=sg_[:], op=ALU.add),
                              reads=[bH, bsg_], writes=[bH])
                    Sc.op("pool", lambda e: e.memset(s4[:, 0:1], 0.0), writes=[bs4])
                    Sc.op("act", lambda e: e.activation(out=junk4[:], in_=H[:, tt, :], func=AF.Square,
                                                        accum_out=s4[:, 0:1]), reads=[bH, bs4], writes=[bjunk4, bs4])
                    rstd_from_ss(s4[:, 0:1], s4[:, 2:3], s4[:, 1:2], 1.0 / D, [bs4])
                    y_, by_ = yo[0]
                    Sc.op("dve", lambda e: e.scalar_tensor_tensor(out=y_[:], in0=H[:, tt, :], scalar=s4[:, 2:3],
                                                                  in1=cg[:, 128:128 + D], op0=ALU.mult, op1=ALU.mult),
                          reads=[bH, bs4, bcg], writes=[by_])
                    Sc.dma(out_d[ti * 128:(ti + 1) * 128, :], y_[:], reads=[by_])
        Sc.finish()
        print("ops:", Sc.nops, "cnt:", Sc.cnt)
    return nc


def host_consts(S, norm_mix, norm_ffn, norm_ple, q_norm, k_norm, a_log, dt_bias, conv_w, dn_norm, norm_final):
    f = np.float32
    i = np.arange(128)[:, None]
    j = np.arange(128)[None, :]
    same = (i // 64) == (j // 64)
    cm = np.zeros((128, 10, 128), f)
    cm[:, M_ID] = (i == j)
    cm[:, M_LOWI] = same & (i >= j)
    cm[:, M_UPPI] = same & (i <= j)
    cm[:, M_LOWS] = same & (i > j)
    cm[:, M_UPPS] = same & (i < j)
    cm[:, M_BLK] = same
    cm[:, M_ONES] = 1.0
    cm[:, M_CI0] = (i < 64) & (j >= 0)
    cm[:, M_CI1] = (i >= 64) & (j >= 0)
    Rm = np.zeros((128, 128), f)
    for fo in range(128):
        idx = fo % 32
        if idx < 16:
            Rm[fo + 16, fo] = -1.0
        else:
            Rm[fo - 16, fo] = 1.0
    cm[:, M_RM] = Rm
    cv = np.zeros((128, V_END), f)
    cv[:, V_GMIX:V_GMIX + 8] = norm_mix.reshape(8, 128).T
    cv[:, V_GFFN:V_GFFN + 8] = norm_ffn.reshape(8, 128).T
    cv[:, V_GPLE:V_GPLE + 8] = norm_ple.reshape(8, 128).T
    cv[:, V_QNG] = np.tile(q_norm, 2)
    cv[:, V_KNG] = np.tile(k_norm, 2)
    cv[:, V_ALOG:V_ALOG + 8] = a_log[None, :]
    cv[:, V_DTB:V_DTB + 8] = dt_bias[None, :]
    cv[:, V_CONV:V_CONV + 60] = conv_w.reshape(5, 12, 128).transpose(2, 1, 0).reshape(128, 60)
    cg = np.zeros((128, 128 + D), f)
    cg[:, 0:128] = dn_norm[None, :]
    cg[:, 128:] = norm_final[None, :]
    t = np.arange(S)
    row = (t // 64).astype(np.float64)
    col = (t % 64).astype(np.float64)
    inv_freq = (10000.0 ** (-np.arange(0, 32, 2, dtype=np.float32) / np.float32(32))).astype(np.float32)
    cosT = np.zeros((128, S), f)
    sinT = np.zeros((128, S), f)
    for pp in range(128):
        dd = pp % 64
        pos = row if dd < 32 else col
        ang = (pos.astype(np.float32) * inv_freq[dd % 16]).astype(np.float32)
        cosT[pp] = np.cos(ang)
        sinT[pp] = np.sin(ang)
    return cm, cv, cg, cosT, sinT


_NC_CACHE = {}


def run(S, ncores, x, p, norm_mix, w_in, conv_w, q_norm, k_norm, a_log, dt_bias, dn_norm, w_out,
        norm_ffn, w_gate, w_up, w_down, norm_ple, w_ple_gate, w_ple, norm_final):
    A = lambda a: np.ascontiguousarray(np.asarray(a, dtype=np.float32))
    cm, cv, cg, cosT, sinT = host_consts(S, A(norm_mix)[0], A(norm_ffn)[0], A(norm_ple)[0], A(q_norm)[0], A(k_norm)[0],
                                         A(a_log)[0], A(dt_bias)[0], A(conv_w)[0], A(dn_norm)[0], A(norm_final))
    if S not in _NC_CACHE:
        _NC_CACHE[S] = build(S)
    nc = _NC_CACHE[S]
    x = A(x)
    p = A(p)
    shared = {"w_in": A(w_in)[0], "w_out": A(w_out)[0], "w_gate": A(w_gate)[0], "w_up": A(w_up)[0],
              "w_down": A(w_down)[0], "w_pg": A(w_ple_gate)[0], "w_ple": A(w_ple)[0],
              "cm": cm, "cv": cv, "cg": cg, "cosT": cosT, "sinT": sinT}
    in_maps = []
    for c in range(ncores):
        m = dict(shared)
        m["x"] = np.ascontiguousarray(x[c])
        m["p"] = np.ascontiguousarray(p[0, c])
        in_maps.append(m)
    res = run_bass_kernel_spmd(nc, in_maps, core_ids=list(range(ncores)))
    return np.stack([np.asarray(r["out"], dtype=np.float32) for r in res.results], axis=0)


def kernel(**inputs):
    x = inputs["x"]
    return run(x.shape[1], x.shape[0], **inputs)
```

```python
import numpy as np
import concourse.bass as bass
import concourse.mybir as mybir
from concourse.bass_utils import run_bass_kernel_spmd
from contextlib import ExitStack
from itertools import zip_longest
import os
PHMAX = int(os.environ.get('KPH', '9'))

F32 = mybir.dt.float32
BF16 = mybir.dt.bfloat16
ALU = mybir.AluOpType
AF = mybir.ActivationFunctionType
AX = mybir.AxisListType

D = 1024
INW = 2832
FF = 2816
NFC = FF // 128
PLE = 256
EPS = 1e-6


class Buf:
    __slots__ = ("name", "w", "rs")

    def __init__(self, name):
        self.name = name
        self.w = None
        self.rs = []


class MBuf(Buf):
    __slots__ = ("ws",)

    def __init__(self, name):
        Buf.__init__(self, name)
        self.ws = []


def _compact(evl):
    best = {}
    for r in evl:
        if r[0] not in best or best[r[0]][2] < r[2]:
            best[r[0]] = r
    return list(best.values())


class Sched:
    ND = 24

    def __init__(self, nc, stack):
        self.nc = nc
        self.E = {"pe": nc.tensor, "act": nc.scalar, "dve": nc.vector,
                  "pool": nc.gpsimd, "sp": nc.sync}
        self.sem = {}
        self.cnt = {}
        for k in ("pe", "act", "dve", "pool"):
            self.sem[k] = stack.enter_context(nc.semaphore("sem_" + k))
            self.cnt[k] = 0
        self.dsem = [stack.enter_context(nc.semaphore("dsem%d" % i)) for i in range(self.ND)]
        self.dcnt = [0] * self.ND
        self.dnext = 0
        self.seen = {k: {} for k in self.E}
        self.nops = {k: 0 for k in self.E}

    def _wait(self, ek, ev):
        key, sem, val, src = ev
        if self.seen[ek].get(key, 0) >= val:
            return
        self.E[ek].wait_ge(sem, val)
        self.seen[ek][key] = val

    def _note(self, ev, reads, writes):
        for b in reads:
            b.rs.append(ev)
            if len(b.rs) > 16:
                best = {}
                for r in b.rs:
                    if r[0] not in best or best[r[0]][2] < r[2]:
                        best[r[0]] = r
                b.rs = list(best.values())
        for b in writes:
            if isinstance(b, MBuf):
                b.ws.append(ev)
                if len(b.ws) > 40:
                    b.ws = _compact(b.ws)
            else:
                b.w = ev
                b.rs = []

    def op(self, ek, fn, reads=(), writes=(), signal=True):
        evs = []
        for b in reads:
            if isinstance(b, MBuf):
                evs.extend(b.ws)
            elif b.w is not None:
                evs.append(b.w)
        for b in writes:
            if b.w is not None and b.w[3] != ek:
                evs.append(b.w)
            for r in b.rs:
                if r[3] != ek:
                    evs.append(r)
        for e in evs:
            if ek == "pe" and e[3] == "pe":
                continue
            self._wait(ek, e)
        inst = fn(self.E[ek])
        self.nops[ek] += 1
        if signal:
            self.cnt[ek] += 1
            inst.then_inc(self.sem[ek], 1)
            ev = (ek, self.sem[ek], self.cnt[ek], ek)
        else:
            ev = (ek, self.sem[ek], self.cnt[ek] + 1, ek)
        self._note(ev, reads, writes)
        return inst

    def dma(self, out, in_, reads=(), writes=(), q="sp", **kw):
        k = self.dnext
        self.dnext = (self.dnext + 1) % self.ND
        key = "d%d" % k
        if self.dcnt[k] > 0:
            self._wait(q, (key, self.dsem[k], self.dcnt[k], "dma"))
        evs = []
        for b in reads:
            if isinstance(b, MBuf):
                evs.extend(b.ws)
            elif b.w is not None:
                evs.append(b.w)
        for b in writes:
            if b.w is not None:
                evs.append(b.w)
            evs.extend(b.rs)
        for e in evs:
            self._wait(q, e)
        inst = self.E[q].dma_start(out=out, in_=in_, **kw)
        self.dcnt[k] += 16
        inst.then_inc(self.dsem[k], 16)
        self.nops[q] += 1
        ev = (key, self.dsem[k], self.dcnt[k], "dma")
        self._note(ev, reads, writes)
        return inst

    def finish(self):
        for k in range(self.ND):
            if self.dcnt[k] > 0:
                self._wait("sp", ("d%d" % k, self.dsem[k], self.dcnt[k], "dma"))


M_ID, M_LOWI, M_UPPI, M_LOWS, M_UPPS, M_BLK, M_ONES, M_CI0, M_CI1, M_RM = range(10)
V_GMIX, V_GFFN, V_GPLE, V_QNG, V_KNG, V_ALOG, V_DTB, V_CONV, V_END = 0, 8, 16, 24, 25, 26, 34, 42, 102


def build(S):
    T = S // 128
    NB = S // 512
    nc = bass.Bass("TRN2", target_bir_lowering=False)

    def din(name, shape, dt=F32):
        return nc.dram_tensor(name, list(shape), dt, kind="ExternalInput").ap()

    def dscr(name, shape, dt=F32):
        return nc.dram_tensor(name, list(shape), dt).ap()

    x_d = din("x", [S, D])
    p_d = din("p", [S, PLE])
    win_d = din("w_in", [D, INW])
    wout_d = din("w_out", [D, D])
    wg_d = din("w_gate", [D, FF])
    wu_d = din("w_up", [D, FF])
    wd_d = din("w_down", [FF, D])
    wpg_d = din("w_pg", [D, D])
    wple_d = din("w_ple", [PLE, D])
    cm_d = din("cm", [128, 10, 128])
    cv_d = din("cv", [128, V_END])
    cg_d = din("cg", [128, 128 + D])
    cos_d = din("cosT", [128, S])
    sin_d = din("sinT", [128, S])
    out_d = nc.dram_tensor("out", [S, D], F32, kind="ExternalOutput").ap()

    QT_s = dscr("QT_s", [4, 128, S], BF16)
    KT_s = dscr("KT_s", [2, 128, S], BF16)
    V_s = dscr("V_s", [S, 128], BF16)
    DQT_s = dscr("DQT_s", [4, 128, S])
    DKT_s = dscr("DKT_s", [4, 128, S])
    DK_s = dscr("DK_s", [S, 4, 128])
    DV_s = dscr("DV_s", [S, 4, 128])
    GB_s = dscr("GB_s", [S, 16])
    DZ_s = dscr("DZ_s", [S, 512])
    AO_s = dscr("AO_s", [8, 64, S], BF16)
    OF_s = dscr("OF_s", [2, S, 512])
    WO_s = dscr("WO_s", [D, D], BF16)
    WG_s = dscr("WG_s", [NFC, 128, 8, 128], BF16)
    WU_s = dscr("WU_s", [NFC, 128, 8, 128], BF16)
    WD_s = dscr("WD_s", [FF, D], BF16)
    WPG_s = dscr("WPG_s", [D, D], BF16)
    WPL_s = dscr("WPL_s", [PLE, D], BF16)
    bQT, bKT, bV = MBuf("QT_s"), MBuf("KT_s"), MBuf("V_s")
    bDN = MBuf("DN_s")
    bAO, bOF = MBuf("AO_s"), MBuf("OF_s")
    bW = MBuf("W_s")

    top = ExitStack()
    with top:
        Sc = Sched(nc, top)

        def sb(stack, name, shape, dt=F32):
            return stack.enter_context(nc.sbuf_tensor("sb_" + name, list(shape), dt)), Buf(name)

        pb = []
        pairs = []
        for i in range(4):
            pr_ = top.enter_context(nc.psum_tensor("pp%d" % i, [128, 2, 512], F32))
            pairs.append(pr_)
            for j in range(2):
                pb.append((pr_[:, j, :], Buf("pb%d" % (2 * i + j))))
        pbn = [0]

        def bank():
            i = pbn[0]
            pbn[0] = (i + 1) % 8
            return pb[i]

        cm, bcm = sb(top, "cm", [128, 10, 128])
        cv, bcv = sb(top, "cv", [128, V_END])
        idb, bidb = sb(top, "idb", [128, 128], BF16)
        epst, bepst = sb(top, "epst", [128, 2])
        nexpA, bnexpA = sb(top, "nexpA", [128, 8])
        Sc.dma(cm[:], cm_d, writes=[bcm])
        Sc.dma(cv[:], cv_d, writes=[bcv])
        Sc.op("dve", lambda e: e.tensor_copy(out=idb[:], in_=cm[:, M_ID, :]), reads=[bcm], writes=[bidb])
        Sc.op("pool", lambda e: e.memset(epst[:, 0:1], EPS), writes=[bepst])
        Sc.op("pool", lambda e: e.memset(epst[:, 1:2], 1.0), writes=[bepst])
        Sc.op("act", lambda e: e.activation(out=nexpA[:], in_=cv[:, V_ALOG:V_ALOG + 8], func=AF.Exp),
              reads=[bcv], writes=[bnexpA])
        Sc.op("dve", lambda e: e.tensor_scalar(out=nexpA[:], in0=nexpA[:], scalar1=-1.0, scalar2=None, op0=ALU.mult),
              reads=[bnexpA], writes=[bnexpA])
        ident = cm[:, M_ID, :]

        def rstd_from_ss(ss_ap, out_ap, tmp_ap, scale, bufs):
            Sc.op("act", lambda e: e.activation(out=tmp_ap, in_=ss_ap, func=AF.Ln, scale=scale, bias=epst[:, 0:1]),
                  reads=bufs + [bepst], writes=bufs)
            Sc.op("act", lambda e: e.activation(out=out_ap, in_=tmp_ap, func=AF.Exp, scale=-0.5),
                  reads=bufs, writes=bufs)

        def transpose_bf(src_tile, bsrc, nblk, dst_ap3, bdst, evac="act"):
            pt, bpt = bank()
            ptb = pt[:].bitcast(BF16)
            for k in range(nblk):
                Sc.op("pe", lambda e, k=k: e.transpose(out=ptb[:, k * 128:(k + 1) * 128],
                                                         in_=src_tile[:, k * 128:(k + 1) * 128], identity=idb[:]),
                      reads=[bsrc, bidb], writes=[bpt], signal=(k == nblk - 1))
            src3 = ptb[:, 0:nblk * 128].rearrange("p (k t) -> p k t", k=nblk)
            if evac == "act":
                Sc.op("act", lambda e: e.copy(out=dst_ap3, in_=src3), reads=[bpt], writes=[bdst])
            else:
                Sc.op("dve", lambda e: e.tensor_copy(out=dst_ap3, in_=src3), reads=[bpt], writes=[bdst])

        ph1 = ExitStack()
        with ph1:
            win, bwin = sb(ph1, "win", [128, 8, INW], BF16)
            wkd, bwkd = sb(ph1, "wkd", [128, 8, 2, 128], BF16)
            ph0 = ExitStack()
            ph0.__enter__()
            stg = [sb(ph0, "stg%d" % i, [128, INW]) for i in range(2)]
            stb = [sb(ph0, "stb%d" % i, [128, FF], BF16) for i in range(2)]
            for kc in range(8):
                st_, bst = stg[kc % 2]
                Sc.dma(st_[:], win_d[kc * 128:(kc + 1) * 128, :], writes=[bst])
                Sc.op("dve", lambda e, kc=kc, st_=st_: e.tensor_scalar(
                    out=win[:, kc, :], in0=st_[:], scalar1=cv[:, V_GMIX + kc:V_GMIX + kc + 1], scalar2=None,
                    op0=ALU.mult), reads=[bst, bcv], writes=[bwin])
            for g in range(2):
                for hf in range(2):
                    Sc.op("pool", lambda e, g=g, hf=hf: e.tensor_copy(
                        out=wkd[:, :, g, hf * 64:(hf + 1) * 64], in_=win[:, :, 512 + 64 * g:512 + 64 * g + 64]),
                        reads=[bwin], writes=[bwkd])
            cnt = [0]

            def conv_w(src_rows, ncols, gain_col, dst_ap):
                i = cnt[0] % 2
                cnt[0] += 1
                st_, bst = stg[i]
                sb_, bsb = stb[i]
                Sc.dma(st_[:, 0:ncols], src_rows, writes=[bst])
                if gain_col is None:
                    Sc.op("pool", lambda e: e.tensor_copy(out=sb_[:, 0:ncols], in_=st_[:, 0:ncols]),
                          reads=[bst], writes=[bsb])
                else:
                    Sc.op("dve", lambda e: e.tensor_scalar(out=sb_[:, 0:ncols], in0=st_[:, 0:ncols],
                                                           scalar1=cv[:, gain_col:gain_col + 1], scalar2=None,
                                                           op0=ALU.mult), reads=[bst, bcv], writes=[bsb])
                return sb_, bsb

            for kc in range(8):
                sb_, bsb = conv_w(wout_d[kc * 128:(kc + 1) * 128, :], D, None, None)
                Sc.dma(WO_s[kc * 128:(kc + 1) * 128, :], sb_[:, 0:D], reads=[bsb], writes=[bW])
            for kc in range(8):
                sb_, bsb = conv_w(wpg_d[kc * 128:(kc + 1) * 128, :], D, V_GPLE + kc, None)
                Sc.dma(WPG_s[kc * 128:(kc + 1) * 128, :], sb_[:, 0:D], reads=[bsb], writes=[bW])
            for kc in range(2):
                sb_, bsb = conv_w(wple_d[kc * 128:(kc + 1) * 128, :], D, None, None)
                Sc.dma(WPL_s[kc * 128:(kc + 1) * 128, :], sb_[:, 0:D], reads=[bsb], writes=[bW])
            for kc in range(NFC):
                sb_, bsb = conv_w(wd_d[kc * 128:(kc + 1) * 128, :], D, None, None)
                Sc.dma(WD_s[kc * 128:(kc + 1) * 128, :], sb_[:, 0:D], reads=[bsb], writes=[bW])
            for (src, dst) in ((wg_d, WG_s), (wu_d, WU_s)):
                for kc in range(8):
                    sb_, bsb = conv_w(src[kc * 128:(kc + 1) * 128, :], FF, V_GFFN + kc, None)
                    Sc.dma(dst[:, :, kc, :].rearrange("f p j -> p f j"),
                           sb_[:, 0:FF].rearrange("p (f j) -> p f j", f=NFC), reads=[bsb], writes=[bW])

            ph0.close()
            xt = [sb(ph1, "xt%d" % i, [128, D]) for i in range(2)]
            hnb = [sb(ph1, "hnb%d" % i, [128, D], BF16) for i in range(2)]
            junk, bjunk = sb(ph1, "junk", [128, D], BF16)
            st4, bst4 = sb(ph1, "st4", [128, 8])
            hnT = [sb(ph1, "hnT%d" % i, [128, 8, 512], BF16) for i in range(2)]
            pre = [sb(ph1, "pre%d" % i, [128, 12, 516]) for i in range(2)]
            cosb = [sb(ph1, "cosb%d" % i, [128, 512]) for i in range(2)]
            sinb = [sb(ph1, "sinb%d" % i, [128, 512]) for i in range(2)]
            wk = [sb(ph1, "wk%d" % i, [128, 512]) for i in range(8)]
            wkn = [0]

            def work():
                i = wkn[0]
                wkn[0] = (i + 1) % 8
                return wk[i]
            qkout = [sb(ph1, "qko%d" % i, [128, 512], BF16) for i in range(2)]
            vbt = [sb(ph1, "vbt%d" % i, [128, 128], BF16) for i in range(2)]
            zt = [sb(ph1, "zt%d" % i, [128, 512]) for i in range(2)]
            gbw, bgbw = sb(ph1, "gbw", [128, 64])
            gbo = [sb(ph1, "gbo%d" % i, [128, 16]) for i in range(2)]
            tkm = [sb(ph1, "tkm%d" % i, [128, 4, 128]) for i in range(2)]

            def qk_post(pt, bpt, kind, c, b):
                xs, bxs = work()
                Sc.op("act", lambda e: e.copy(out=xs[:], in_=pt[:]), reads=[bpt], writes=[bxs])
                sq, bsq = work()
                Sc.op("pool", lambda e: e.tensor_tensor(out=sq[:], in0=xs[:], in1=xs[:], op=ALU.mult),
                      reads=[bxs], writes=[bsq])
                p2, bp2 = bank()
                Sc.op("pe", lambda e: e.matmul(p2[:], lhsT=cm[:, M_BLK, :], rhs=sq[:], start=True, stop=True),
                      reads=[bcm, bsq], writes=[bp2])
                rn, brn = work()
                Sc.op("act", lambda e: e.activation(out=rn[:], in_=p2[:], func=AF.Ln, scale=1.0 / 64,
                                                    bias=epst[:, 0:1]), reads=[bp2, bepst], writes=[brn])
                Sc.op("act", lambda e: e.activation(out=rn[:], in_=rn[:], func=AF.Exp, scale=-0.5),
                      reads=[brn], writes=[brn])
                gcol = V_QNG if kind == "q" else V_KNG
                xn, bxn = work()
                Sc.op("dve", lambda e: e.scalar_tensor_tensor(out=xn[:], in0=xs[:], scalar=cv[:, gcol:gcol + 1],
                                                              in1=rn[:], op0=ALU.mult, op1=ALU.mult),
                      reads=[bxs, bcv, brn], writes=[bxn])
                p3, bp3 = bank()
                Sc.op("pe", lambda e: e.matmul(p3[:], lhsT=cm[:, M_RM, :], rhs=xn[:], start=True, stop=True),
                      reads=[bcm, bxn], writes=[bp3])
                t1, bt1 = work()
                Sc.op("pool", lambda e: e.tensor_tensor(out=t1[:], in0=xn[:], in1=cosb[b % 2][0][:], op=ALU.mult),
                      reads=[bxn, cosb[b % 2][1]], writes=[bt1])
                t2, bt2 = work()
                Sc.op("dve", lambda e: e.tensor_tensor(out=t2[:], in0=p3[:], in1=sinb[b % 2][0][:], op=ALU.mult),
                      reads=[bp3, sinb[b % 2][1]], writes=[bt2])
                qo, bqo = qkout[c % 2]
                Sc.op("dve", lambda e: e.tensor_tensor(out=qo[:], in0=t1[:], in1=t2[:], op=ALU.add),
                      reads=[bt1, bt2], writes=[bqo])
                if kind == "q":
                    Sc.dma(QT_s[c, :, b * 512:(b + 1) * 512], qo[:], reads=[bqo], writes=[bQT])
                else:
                    Sc.dma(KT_s[c, :, b * 512:(b + 1) * 512], qo[:], reads=[bqo], writes=[bKT])

            def dn_post(b):
                pr, bpr = pre[b % 2]
                for ci in range(12):
                    eng = "dve"
                    acc, bacc = work()
                    Sc.op(eng, lambda e: e.tensor_scalar(out=acc[:], in0=pr[:, ci, 0:512],
                                                         scalar1=cv[:, V_CONV + ci * 5:V_CONV + ci * 5 + 1],
                                                         scalar2=None, op0=ALU.mult),
                          reads=[bpr, bcv], writes=[bacc])
                    for tap in range(1, 5):
                        Sc.op(eng, lambda e, tap=tap: e.scalar_tensor_tensor(
                            out=acc[:], in0=pr[:, ci, tap:tap + 512],
                            scalar=cv[:, V_CONV + ci * 5 + tap:V_CONV + ci * 5 + tap + 1],
                            in1=acc[:], op0=ALU.mult, op1=ALU.add), reads=[bpr, bcv, bacc], writes=[bacc])
                    s, bs = work()
                    Sc.op("act", lambda e: e.activation(out=s[:], in_=acc[:], func=AF.Silu), reads=[bacc], writes=[bs])
                    h = ci % 4
                    if ci < 8:
                        sq, bsq = work()
                        Sc.op("pool", lambda e: e.tensor_tensor(out=sq[:], in0=s[:], in1=s[:], op=ALU.mult),
                              reads=[bs], writes=[bsq])
                        p2, bp2 = bank()
                        Sc.op("pe", lambda e: e.matmul(p2[:], lhsT=cm[:, M_ONES, :], rhs=sq[:], start=True, stop=True),
                              reads=[bcm, bsq], writes=[bp2])
                        rn, brn = work()
                        Sc.op("act", lambda e: e.activation(out=rn[:], in_=p2[:], func=AF.Ln, scale=1.0,
                                                            bias=epst[:, 0:1]), reads=[bp2, bepst], writes=[brn])
                        Sc.op("act", lambda e: e.activation(out=rn[:], in_=rn[:], func=AF.Exp, scale=-0.5),
                              reads=[brn], writes=[brn])
                        o, bo = work()
                        sc_ = (128.0 ** -0.5) if ci < 4 else 1.0
                        Sc.op("dve", lambda e: e.scalar_tensor_tensor(out=o[:], in0=s[:], scalar=sc_, in1=rn[:],
                                                                      op0=ALU.mult, op1=ALU.mult),
                              reads=[bs, brn], writes=[bo])
                        dst = DQT_s if ci < 4 else DKT_s
                        Sc.dma(dst[h, :, b * 512:(b + 1) * 512], o[:], reads=[bo], writes=[bDN])
                    else:
                        o, bo = s, bs
                    if ci >= 4:
                        pt, bpt = bank()
                        for tt in range(4):
                            Sc.op("pe", lambda e, tt=tt: e.transpose(out=pt[:, tt * 128:(tt + 1) * 128],
                                                                     in_=o[:, tt * 128:(tt + 1) * 128],
                                                                     identity=ident),
                                  reads=[bo, bcm], writes=[bpt], signal=(tt == 3))
                        tk, btk = tkm[ci % 2]
                        Sc.op("act", lambda e: e.copy(out=tk[:], in_=pt[:].rearrange("p (t d) -> p t d", t=4)),
                              reads=[bpt], writes=[btk])
                        dst = DK_s if ci < 8 else DV_s
                        Sc.dma(dst[b * 512:(b + 1) * 512, h, :].rearrange("(t p) d -> p t d", p=128), tk[:],
                               reads=[btk], writes=[bDN])

            for b in range(NB if PHMAX >= 1 else 0):
                hT, bhT = hnT[b % 2]
                Sc.dma(cosb[b % 2][0][:], cos_d[:, b * 512:(b + 1) * 512], writes=[cosb[b % 2][1]])
                Sc.dma(sinb[b % 2][0][:], sin_d[:, b * 512:(b + 1) * 512], writes=[sinb[b % 2][1]])
                for tt in range(4):
                    ti = 4 * b + tt
                    x_, bx_ = xt[ti % 2]
                    h_, bh_ = hnb[ti % 2]
                    Sc.dma(x_[:], x_d[ti * 128:(ti + 1) * 128, :], writes=[bx_])
                    Sc.op("pool", lambda e: e.memset(st4[:, 0:1], 0.0), writes=[bst4])
                    Sc.op("act", lambda e: e.activation(out=junk[:], in_=x_[:], func=AF.Square,
                                                        accum_out=st4[:, 0:1]), reads=[bx_, bst4], writes=[bjunk, bst4])
                    rstd_from_ss(st4[:, 0:1], st4[:, 2:3], st4[:, 1:2], 1.0 / D, [bst4])
                    Sc.op("dve", lambda e: e.tensor_scalar(out=h_[:], in0=x_[:], scalar1=st4[:, 2:3], scalar2=None,
                                                           op0=ALU.mult), reads=[bx_, bst4], writes=[bh_])
                    transpose_bf(h_, bh_, 8, hT[:, :, tt * 128:(tt + 1) * 128], bhT)
                chunks = [("q", c, win, lambda kc, c=c: win[:, kc, c * 128:(c + 1) * 128]) for c in range(4)]
                chunks += [("k", g, wkd, lambda kc, g=g: wkd[:, kc, g, :]) for g in range(2)]
                chunks += [("d", ci, win, lambda kc, ci=ci: win[:, kc, 768 + ci * 128:768 + (ci + 1) * 128])
                           for ci in range(12)]
                for kind, c, _, wsel in chunks:
                    pt, bpt = bank()
                    for kc in range(8):
                        Sc.op("pe", lambda e, kc=kc: e.matmul(pt[:], lhsT=wsel(kc), rhs=hT[:, kc, :],
                                                             start=(kc == 0), stop=(kc == 7)),
                              reads=[bwin, bwkd, bhT], writes=[bpt], signal=(kc == 7))
                    if kind == "d":
                        Sc.op("act", lambda e: e.copy(out=pre[b % 2][0][:, c, 2:514], in_=pt[:]),
                              reads=[bpt], writes=[pre[b % 2][1]])
                    else:
                        qk_post(pt, bpt, kind, c, b)
                for tt in range(4):
                    ti = 4 * b + tt
                    pt, bpt = bank()
                    for kc in range(8):
                        Sc.op("pe", lambda e, kc=kc: e.matmul(pt[:, 0:128], lhsT=hT[:, kc, tt * 128:(tt + 1) * 128],
                                                             rhs=win[:, kc, 640:768], start=(kc == 0), stop=(kc == 7)),
                              reads=[bwin, bhT], writes=[bpt], signal=False)
                    for kc in range(8):
                        Sc.op("pe", lambda e, kc=kc: e.matmul(pt[:, 128:144], lhsT=hT[:, kc, tt * 128:(tt + 1) * 128],
                                                             rhs=win[:, kc, 2816:2832], start=(kc == 0), stop=(kc == 7)),
                              reads=[bwin, bhT], writes=[bpt], signal=(kc == 7))
                    vb, bvb = vbt[ti % 2]
                    Sc.op("act", lambda e: e.copy(out=vb[:], in_=pt[:, 0:128]), reads=[bpt], writes=[bvb])
                    Sc.dma(V_s[ti * 128:(ti + 1) * 128, :], vb[:], reads=[bvb], writes=[bV])
                    go, bgo = gbo[ti % 2]
                    Sc.op("act", lambda e: e.activation(out=gbw[:, 0:8], in_=pt[:, 128:136], func=AF.Exp, scale=-1.0),
                          reads=[bpt], writes=[bgbw])
                    Sc.op("dve", lambda e: e.tensor_scalar(out=gbw[:, 0:8], in0=gbw[:, 0:8], scalar1=1.0, scalar2=None,
                                                           op0=ALU.add), reads=[bgbw], writes=[bgbw])
                    Sc.op("dve", lambda e: e.reciprocal(out=go[:, 8:16], in_=gbw[:, 0:8]), reads=[bgbw], writes=[bgo])
                    Sc.op("dve", lambda e: e.tensor_tensor(out=gbw[:, 8:16], in0=pt[:, 136:144],
                                                           in1=cv[:, V_DTB:V_DTB + 8], op=ALU.add),
                          reads=[bpt, bcv], writes=[bgbw])
                    Sc.op("act", lambda e: e.activation(out=gbw[:, 16:24], in_=gbw[:, 8:16], func=AF.Exp),
                          reads=[bgbw], writes=[bgbw])
                    Sc.op("act", lambda e: e.activation(out=gbw[:, 24:32], in_=gbw[:, 16:24], func=AF.Ln,
                                                        bias=epst[:, 1:2]), reads=[bgbw, bepst], writes=[bgbw])
                    Sc.op("dve", lambda e: e.tensor_tensor(out=go[:, 0:8], in0=gbw[:, 24:32], in1=nexpA[:], op=ALU.mult),
                          reads=[bgbw, bnexpA], writes=[bgo])
                    Sc.dma(GB_s[ti * 128:(ti + 1) * 128, :], go[:], reads=[bgo], writes=[bDN])
                    pz, bpz = bank()
                    for kc in range(8):
                        Sc.op("pe", lambda e, kc=kc: e.matmul(pz[:], lhsT=hT[:, kc, tt * 128:(tt + 1) * 128],
                                                             rhs=win[:, kc, 2304:2816], start=(kc == 0), stop=(kc == 7)),
                              reads=[bwin, bhT], writes=[bpz], signal=(kc == 7))
                    z_, bz_ = zt[ti % 2]
                    Sc.op("act", lambda e: e.copy(out=z_[:], in_=pz[:]), reads=[bpz], writes=[bz_])
                    Sc.dma(DZ_s[ti * 128:(ti + 1) * 128, :], z_[:], reads=[bz_], writes=[bDN])
                pr, bpr = pre[b % 2]
                if b == 0:
                    Sc.op("pool", lambda e: e.memset(pr[:, :, 0:2], 0.0), writes=[bpr])
                else:
                    pp, bpp = pre[(b - 1) % 2]
                    Sc.op("pool", lambda e: e.tensor_copy(out=pr[:, :, 0:2], in_=pp[:, :, 512:514]),
                          reads=[bpp], writes=[bpr])
                    Sc.op("pool", lambda e: e.tensor_copy(out=pp[:, :, 514:516], in_=pr[:, :, 2:4]),
                          reads=[bpr], writes=[bpp])
                    dn_post(b - 1)
                if b == NB - 1:
                    Sc.op("pool", lambda e: e.memset(pr[:, :, 514:516], 0.0), writes=[bpr])
                    dn_post(b)

        ph2 = ExitStack()
        with ph2:
            NW = 30
            dw = [[sb(ph2, "dw%d_%d" % (d, i), [128, 4, 128]) for i in range(NW)] for d in range(2)]
            dinp = [[[sb(ph2, "di%d_%d_%d" % (d, q, i), [128, 4, 128]) for i in range(4)] for q in range(2)]
                    for d in range(2)]
            Sst = [sb(ph2, "Sst%d" % d, [128, 4, 128]) for d in range(2)]
            dsm = [[sb(ph2, "dsm%d_%d" % (d, i), [128, 32]) for i in range(2)] for d in range(2)]
            for d in range(2):
                Sc.op("pool", lambda e, d=d: e.memset(Sst[d][0][:], 0.0), writes=[Sst[d][1]])

            def bc_h(ap_h):
                return ap_h.unsqueeze(2).broadcast_to([128, 4, 128])

            def bc_m(ap_m):
                return ap_m.unsqueeze(1).broadcast_to([128, 4, 128])

            def dn_unit(t, d, step):
                W_ = dw[d]
                wi = [0]

                def wt():
                    r = W_[wi[0]]
                    wi[0] += 1
                    return r
                s0 = t * 128
                (kT, bkT), (qT, bqT), (ktok, bktok), (vtok, bvtok) = dinp[d][step % 2]
                sm, bsm = dsm[d][step % 2]
                S_, bS_ = Sst[d]
                Sc.dma(kT[:], DKT_s[:, :, s0:s0 + 128].rearrange("h p s -> p h s"), reads=[bDN], writes=[bkT])
                Sc.dma(qT[:], DQT_s[:, :, s0:s0 + 128].rearrange("h p s -> p h s"), reads=[bDN], writes=[bqT])
                Sc.dma(ktok[:], DK_s[s0:s0 + 128, :, :], reads=[bDN], writes=[bktok])
                Sc.dma(vtok[:], DV_s[s0:s0 + 128, :, :], reads=[bDN], writes=[bvtok])
                Sc.dma(sm[:, 0:16], GB_s[s0:s0 + 128, :], reads=[bDN], writes=[bsm])
                g_ap = sm[:, 4 * d:4 * d + 4]
                beta_ap = sm[:, 8 + 4 * d:8 + 4 * d + 4]
                tri = M_UPPI if d == 0 else M_LOWI
                m_incl = M_LOWI if d == 0 else M_UPPI
                m_inclT = M_UPPI if d == 0 else M_LOWI
                m_str = M_LOWS if d == 0 else M_UPPS
                yield
                pa, bpa = bank()
                for j, mi in enumerate((tri, M_BLK, M_CI0, M_CI1)):
                    Sc.op("pe", lambda e, j=j, mi=mi: e.matmul(pa[:, 16 * j:16 * j + 16], lhsT=cm[:, mi, :], rhs=sm[:, 0:16],
                                                               start=True, stop=True),
                          reads=[bcm, bsm], writes=[bpa], signal=(j == 3))
                Sc.op("dve", lambda e: e.tensor_copy(
                    out=sm[:, 16:32].rearrange("p (j c) -> p j c", j=4),
                    in_=pa[:, 0:64].rearrange("p (j c) -> p j c", j=4)[:, :, 4 * d:4 * d + 4]), reads=[bpa], writes=[bsm])
                sm2, bsm2 = wt()
                sm2f = sm2[:].rearrange("p h d -> p (h d)")
                Sc.op("act", lambda e: e.activation(out=sm2f[:, 0:16], in_=sm[:, 16:32], func=AF.Exp),
                      reads=[bsm], writes=[bsm2])
                Sc.op("dve", lambda e: e.tensor_tensor(out=sm2f[:, 16:20], in0=sm[:, 20:24], in1=sm[:, 16:20],
                                                       op=ALU.subtract), reads=[bsm], writes=[bsm2])
                Sc.op("act", lambda e: e.activation(out=sm2f[:, 20:24], in_=sm2f[:, 16:20], func=AF.Exp),
                      reads=[bsm2], writes=[bsm2])
                Sc.op("dve", lambda e: e.tensor_tensor(out=sm2f[:, 24:28], in0=beta_ap, in1=sm2f[:, 0:4], op=ALU.mult),
                      reads=[bsm, bsm2], writes=[bsm2])
                G_ap = sm[:, 16:20]
                eGl_ap = sm2f[:, 20:24]
                beG_ap = sm2f[:, 24:28]
                gt_ap = [sm2f[:, 8:12], sm2f[:, 12:16]]
                yield
                dg, bdg = wt()
                Sc.op("dve", lambda e: e.tensor_tensor(out=dg[:], in0=bc_m(ident), in1=bc_h(G_ap), op=ALU.mult),
                      reads=[bcm, bsm], writes=[bdg])
                pB, bpB = bank()
                for h in range(4):
                    Sc.op("pe", lambda e, h=h: e.matmul(pB[:, h * 128:(h + 1) * 128], lhsT=cm[:, M_ONES, :],
                                                        rhs=dg[:, h, :], start=True, stop=True),
                          reads=[bcm, bdg], writes=[bpB], signal=(h == 3))
                pB3 = pB[:].rearrange("p (h d) -> p h d", h=4)
                t1, bt1 = wt()
                ebc, bebc = wt()
                Sc.op("act", lambda e: e.copy(out=ebc[:], in_=pB3), reads=[bpB], writes=[bebc])
                Sc.op("dve", lambda e: e.tensor_tensor(out=t1[:], in0=ebc[:], in1=bc_h(G_ap), op=ALU.subtract),
                      reads=[bebc, bsm], writes=[bt1])
                Sc.op("act", lambda e: e.activation(out=ebc[:], in_=ebc[:], func=AF.Exp), reads=[bebc, bt1], writes=[bebc])
                ta, bta = wt()
                tb, btb = wt()
                Sc.op("dve", lambda e: e.tensor_scalar(out=ta[:], in0=t1[:], scalar1=0.0, scalar2=-1.0,
                                                       op0=ALU.max, op1=ALU.mult), reads=[bt1], writes=[bta])
                Sc.op("pool", lambda e: e.tensor_scalar(out=tb[:], in0=t1[:], scalar1=0.0, scalar2=None,
                                                        op0=ALU.min), reads=[bt1], writes=[btb])
                Sc.op("act", lambda e: e.activation(out=ta[:], in_=ta[:], func=AF.Exp), reads=[bta], writes=[bta])
                Sc.op("act", lambda e: e.activation(out=tb[:], in_=tb[:], func=AF.Exp), reads=[btb], writes=[btb])
                yield
                DmS, bDmS = wt()
                DmT, bDmT = wt()
                Sc.op("pool", lambda e: e.tensor_tensor(out=DmS[:], in0=ta[:], in1=bc_m(cm[:, m_str, :]), op=ALU.mult),
                      reads=[bta, bcm], writes=[bDmS])
                Sc.op("pool", lambda e: e.tensor_tensor(out=DmT[:], in0=tb[:], in1=bc_m(cm[:, m_inclT, :]), op=ALU.mult),
                      reads=[btb, bcm], writes=[bDmT])
                pC, bpC = bank()
                pD, bpD = bank()
                for h in range(4):
                    Sc.op("pe", lambda e, h=h: e.matmul(pC[:, h * 128:(h + 1) * 128], lhsT=kT[:, h, :], rhs=kT[:, h, :],
                                                        start=True, stop=True), reads=[bkT], writes=[bpC],
                          signal=(h == 3))
                for h in range(4):
                    Sc.op("pe", lambda e, h=h: e.matmul(pD[:, h * 128:(h + 1) * 128], lhsT=kT[:, h, :], rhs=qT[:, h, :],
                                                        start=True, stop=True), reads=[bkT, bqT], writes=[bpD],
                          signal=(h == 3))
                PB = [wt(), wt()]
                QB = [wt(), wt()]
                WB = [wt(), wt()]
                P_, bP_ = PB[0]
                Sc.op("dve", lambda e: e.tensor_tensor(out=P_[:], in0=pC[:].rearrange("p (h d) -> p h d", h=4),
                                                       in1=DmS[:], op=ALU.mult), reads=[bpC, bDmS], writes=[bP_])
                Sc.op("dve", lambda e: e.tensor_tensor(out=P_[:], in0=P_[:], in1=bc_h(beta_ap), op=ALU.mult),
                      reads=[bP_, bsm], writes=[bP_])
                QKDT, bQKDT = wt()
                Sc.op("dve", lambda e: e.tensor_tensor(out=QKDT[:], in0=pD[:].rearrange("p (h d) -> p h d", h=4),
                                                       in1=DmT[:], op=ALU.mult), reads=[bpD, bDmT], writes=[bQKDT])
                yield
                pE, bpE = bank()
                for h in range(4):
                    Sc.op("pe", lambda e, h=h: e.transpose(out=pE[:, h * 128:(h + 1) * 128], in_=P_[:, h, :],
                                                           identity=ident), reads=[bP_, bcm], writes=[bpE],
                          signal=(h == 3))
                pE3 = pE[:].rearrange("p (h d) -> p h d", h=4)
                Q_, bQ_ = QB[0]
                Wc, bWc = WB[0]
                Sc.op("act", lambda e: e.copy(out=Q_[:], in_=pE3), reads=[bpE], writes=[bQ_])
                Sc.op("dve", lambda e: e.tensor_tensor(out=Wc[:], in0=bc_m(ident), in1=Q_[:], op=ALU.subtract),
                      reads=[bcm, bQ_], writes=[bWc])
                yield
                for lvl in range(1, 6):
                    pX, bpX = bank()
                    for h in range(4):
                        Sc.op("pe", lambda e, h=h: e.matmul(pX[:, h * 128:(h + 1) * 128], lhsT=Q_[:, h, :],
                                                            rhs=P_[:, h, :], start=True, stop=True),
                              reads=[bQ_, bP_], writes=[bpX], signal=(h == 3))
                    if lvl < 5:
                        pY, bpY = bank()
                        for h in range(4):
                            Sc.op("pe", lambda e, h=h: e.matmul(pY[:, h * 128:(h + 1) * 128], lhsT=P_[:, h, :],
                                                                rhs=Q_[:, h, :], start=True, stop=True),
                                  reads=[bQ_, bP_], writes=[bpY], signal=(h == 3))
                    Pn, bPn = PB[lvl % 2]
                    Sc.op("act", lambda e: e.copy(out=Pn[:], in_=pX[:].rearrange("p (h d) -> p h d", h=4)),
                          reads=[bpX], writes=[bPn])
                    if lvl < 5:
                        Qn, bQn = QB[lvl % 2]
                        Sc.op("dve", lambda e: e.tensor_copy(out=Qn[:], in_=pY[:].rearrange("p (h d) -> p h d", h=4)),
                              reads=[bpY], writes=[bQn])
                    pZ, bpZ = bank()
                    for h in range(4):
                        Sc.op("pe", lambda e, h=h: e.matmul(pZ[:, h * 128:(h + 1) * 128], lhsT=Pn[:, h, :],
                                                            rhs=Wc[:, h, :], start=True, stop=True),
                              reads=[bPn, bWc], writes=[bpZ], signal=(h == 3))
                    Wn, bWn = WB[lvl % 2]
                    Sc.op("dve", lambda e: e.tensor_tensor(out=Wn[:], in0=pZ[:].rearrange("p (h d) -> p h d", h=4),
                                                           in1=Wc[:], op=ALU.add), reads=[bpZ, bWc], writes=[bWn])
                    Wc, bWc = Wn, bWn
                    P_, bP_ = Pn, bPn
                    if lvl < 5:
                        Q_, bQ_ = Qn, bQn
                    yield
                T1, bT1 = wt()
                T2, bT2 = wt()
                Sc.op("pool", lambda e: e.tensor_tensor(out=T1[:], in0=Wc[:], in1=bc_h(beta_ap), op=ALU.mult),
                      reads=[bWc, bsm], writes=[bT1])
                Sc.op("dve", lambda e: e.tensor_tensor(out=T2[:], in0=Wc[:], in1=bc_h(beG_ap), op=ALU.mult),
                      reads=[bWc, bsm2], writes=[bT2])
                pU, bpU = bank()
                pW, bpW = bank()
                for h in range(4):
                    Sc.op("pe", lambda e, h=h: e.matmul(pU[:, h * 128:(h + 1) * 128], lhsT=T1[:, h, :],
                                                        rhs=vtok[:, h, :], start=True, stop=True),
                          reads=[bT1, bvtok], writes=[bpU], signal=(h == 3))
                for h in range(4):
                    Sc.op("pe", lambda e, h=h: e.matmul(pW[:, h * 128:(h + 1) * 128], lhsT=ktok[:, h, :],
                                                        rhs=T2[:, h, :], start=True, stop=True),
                          reads=[bT2, bktok], writes=[bpW], signal=(h == 3))
                u_, bu_ = wt()
                wT_, bwT_ = wt()
                Sc.op("act", lambda e: e.copy(out=u_[:], in_=pU[:].rearrange("p (h d) -> p h d", h=4)),
                      reads=[bpU], writes=[bu_])
                Sc.op("dve", lambda e: e.tensor_copy(out=wT_[:], in_=pW[:].rearrange("p (h d) -> p h d", h=4)),
                      reads=[bpW], writes=[bwT_])
                kdec, bkdec = wt()
                qdT, bqdT = wt()
                Sc.op("pool", lambda e: e.tensor_tensor(out=kdec[:], in0=ktok[:], in1=bc_h(eGl_ap), op=ALU.mult),
                      reads=[bktok, bsm2], writes=[bkdec])
                Sc.op("pool", lambda e: e.tensor_tensor(out=qdT[:], in0=qT[:], in1=ebc[:], op=ALU.mult),
                      reads=[bqT, bebc], writes=[bqdT])
                vn, bvn = wt()
                stmp, bstmp = wt()
                if step == 0:
                    Sc.op("pool", lambda e: e.memset(vn[:], 0.0), writes=[bvn])
                yield
                ot, bot = wt()
                corder = (0, 1) if d == 0 else (1, 0)
                for c in corder:
                    c0 = 64 * c
                    cs = slice(c0, c0 + 64)
                    tp = (c0, 0) if c0 else None
                    pV, bpV = bank()
                    for h in range(4):
                        Sc.op("pe", lambda e, h=h: e.matmul(pV[:, h * 128:(h + 1) * 128], lhsT=wT_[:, h, :],
                                                            rhs=S_[:, h, :], start=True, stop=True),
                              reads=[bwT_, bS_], writes=[bpV], signal=(h == 3))
                    Sc.op("dve", lambda e: e.tensor_tensor(out=vn[cs], in0=u_[cs],
                                                           in1=pV[cs, :].rearrange("p (h d) -> p h d", h=4),
                                                           op=ALU.subtract), reads=[bu_, bpV], writes=[bvn])
                    pO, bpO = bank()
                    for h in range(4):
                        Sc.op("pe", lambda e, h=h: e.matmul(pO[:, h * 128:(h + 1) * 128], lhsT=qdT[:, h, :],
                                                            rhs=S_[:, h, :], start=True, stop=False),
                              reads=[bqdT, bS_], writes=[bpO], signal=False)
                        Sc.op("pe", lambda e, h=h: e.matmul(pO[:, h * 128:(h + 1) * 128], lhsT=QKDT[:, h, :],
                                                            rhs=vn[:, h, :], start=False, stop=True),
                              reads=[bQKDT, bvn], writes=[bpO], signal=(h == 3))
                    pS, bpS = bank()
                    for h in range(4):
                        Sc.op("pe", lambda e, h=h: e.matmul(pS[:, h * 128:(h + 1) * 128], lhsT=kdec[cs, h, :],
                                                            rhs=vn[cs, h, :], start=True, stop=True,
                                                            tile_position=tp),
                              reads=[bkdec, bvn], writes=[bpS], signal=(h == 3))
                    Sc.op("act", lambda e: e.copy(out=ot[cs], in_=pO[cs, :].rearrange("p (h d) -> p h d", h=4)),
                          reads=[bpO], writes=[bot])
                    Sc.op("pool", lambda e: e.tensor_tensor(out=stmp[:], in0=S_[:], in1=bc_h(gt_ap[c]), op=ALU.mult),
                          reads=[bS_, bsm2], writes=[bstmp])
                    Sc.op("dve", lambda e: e.tensor_tensor(out=S_[:], in0=stmp[:],
                                                           in1=pS[:].rearrange("p (h d) -> p h d", h=4), op=ALU.add),
                          reads=[bstmp, bpS], writes=[bS_])
                    yield
                Sc.dma(OF_s[d, s0:s0 + 128, :].rearrange("p (h d) -> p h d", h=4), ot[:], reads=[bot], writes=[bOF])

            for step in range(T if PHMAX >= 2 else 0):
                gens = [dn_unit(step, 0, step), dn_unit(T - 1 - step, 1, step)]
                for _ in zip_longest(*gens):
                    pass

        ph3 = ExitStack()
        with ph3:
            KT2, bKT2 = sb(ph3, "KT2", [128, 2, S], BF16)
            Vaug, bVaug = sb(ph3, "Vaug", [128, T, 2, 65], BF16)
            onesr, bonesr = sb(ph3, "onesr", [128, 64])
            Sc.op("pool", lambda e: e.memset(Vaug[:], 1.0), writes=[bVaug])
            Sc.op("pool", lambda e: e.memset(onesr[:], 1.0), writes=[bonesr])
            for g in range(2):
                Sc.dma(KT2[:, g, :], KT_s[g], reads=[bKT], writes=[bKT2])
            TG = 8
            for t0 in range(0, T, TG):
                nt = min(TG, T - t0)
                for g in range(2):
                    Sc.dma(Vaug[:, t0:t0 + nt, g, 0:64],
                           V_s[t0 * 128:(t0 + nt) * 128, g * 64:(g + 1) * 64].rearrange("(t p) d -> p t d", p=128),
                           reads=[bV], writes=[bVaug])
            qtc = [sb(ph3, "qtc%d" % i, [128, 512], BF16) for i in range(2)]
            PT = [sb(ph3, "PT%d" % i, [128, 2, 512], BF16) for i in range(3)]
            oa = [sb(ph3, "oa%d" % i, [128, 512]) for i in range(2)]
            rs_, brs_ = sb(ph3, "rs", [128, 512])
            aob = [sb(ph3, "aob%d" % i, [64, 512], BF16) for i in range(2)]
            it = 0
            for qb in range(NB if PHMAX >= 3 else 0):
                for c in range(4):
                    g = c // 2
                    q_, bq_ = qtc[it % 2]
                    it += 1
                    Sc.dma(q_[:], QT_s[c, :, qb * 512:(qb + 1) * 512], reads=[bQT], writes=[bq_])
                    pOa, bpOa = pb[6]
                    pOb, bpOb = pb[7]

                    def qk(kt):
                        pSa, bpSa = pb[2 * (kt % 2)]
                        pSb, bpSb = pb[2 * (kt % 2) + 1]
                        Sc.op("pe", lambda e: e.matmul(pSa[:], lhsT=KT2[0:64, g, kt * 128:(kt + 1) * 128],
                                                       rhs=q_[0:64, :], start=True, stop=True),
                              reads=[bKT2, bq_], writes=[bpSa], signal=False)
                        Sc.op("pe", lambda e: e.matmul(pSb[:], lhsT=KT2[64:128, g, kt * 128:(kt + 1) * 128],
                                                       rhs=q_[64:128, :], start=True, stop=True,
                                                       tile_position=(64, 0)),
                              reads=[bKT2, bq_], writes=[bpSb])

                    qk(0)
                    for kt in range(T):
                        if kt + 1 < T:
                            qk(kt + 1)
                        bpSa = pb[2 * (kt % 2)][1]
                        bpSb = pb[2 * (kt % 2) + 1][1]
                        P_, bP_ = PT[kt % 3]
                        Sc.op("act", lambda e: e.activation(out=P_[:], in_=pairs[kt % 2][:], func=AF.Exp, scale=0.125),
                              reads=[bpSa, bpSb], writes=[bP_])
                        Sc.op("pe", lambda e: e.matmul(pOa[0:65, :], lhsT=Vaug[:, kt, g, :], rhs=P_[:, 0, :],
                                                       start=(kt == 0), stop=(kt == T - 1)),
                              reads=[bVaug, bP_], writes=[bpOa], signal=False)
                        Sc.op("pe", lambda e: e.matmul(pOb[0:65, :], lhsT=Vaug[:, kt, g, :], rhs=P_[:, 1, :],
                                                       start=(kt == 0), stop=(kt == T - 1)),
                              reads=[bVaug, bP_], writes=[bpOb])
                    for j, (pO_, bpO_) in enumerate(((pOa, bpOa), (pOb, bpOb))):
                        hh = 2 * c + j
                        Sc.op("dve", lambda e: e.reciprocal(out=rs_[64:65, :], in_=pO_[64:65, :]),
                              reads=[bpO_], writes=[brs_])
                        o_, bo_ = oa[j]
                        Sc.op("act", lambda e: e.copy(out=o_[0:64, :], in_=pO_[0:64, :]), reads=[bpO_], writes=[bo_])
                        pN, bpN = pb[4 + j]
                        Sc.op("pe", lambda e: e.matmul(pN[0:64, :], lhsT=onesr[64:65, 0:64], rhs=rs_[64:65, :],
                                                       start=True, stop=True, tile_position=(64, 0)),
                              reads=[bonesr, brs_], writes=[bpN])
                        ab, bab = aob[j]
                        Sc.op("dve", lambda e: e.tensor_tensor(out=ab[:], in0=o_[0:64, :], in1=pN[0:64, :], op=ALU.mult),
                              reads=[bo_, bpN], writes=[bab])
                        Sc.dma(AO_s[hh, :, qb * 512:(qb + 1) * 512], ab[:], reads=[bab], writes=[bAO])

        ph4 = ExitStack()
        with ph4:
            cg, bcg = sb(ph4, "cg", [128, 128 + D])
            Sc.dma(cg[:], cg_d, writes=[bcg])
            woA, bwoA = sb(ph4, "woA", [64, 8, D], BF16)
            woD, bwoD = sb(ph4, "woD", [128, 4, D], BF16)
            wdn, bwdn = sb(ph4, "wdn", [128, NFC, D], BF16)
            wpg, bwpg = sb(ph4, "wpg", [128, 8, D], BF16)
            wpl, bwpl = sb(ph4, "wpl", [128, 2, D], BF16)
            Sc.dma(woA[:], WO_s[0:512, :].rearrange("(h p) n -> p h n", p=64), reads=[bW], writes=[bwoA])
            Sc.dma(woD[:], WO_s[512:1024, :].rearrange("(c p) n -> p c n", p=128), reads=[bW], writes=[bwoD])
            for f0 in range(0, NFC, 6):
                f1 = min(NFC, f0 + 6)
                Sc.dma(wdn[:, f0:f1, :], WD_s[f0 * 128:f1 * 128, :].rearrange("(c p) n -> p c n", p=128),
                       reads=[bW], writes=[bwdn])
            Sc.dma(wpg[:], WPG_s.rearrange("(c p) n -> p c n", p=128), reads=[bW], writes=[bwpg])
            Sc.dma(wpl[:], WPL_s.rearrange("(c p) n -> p c n", p=128), reads=[bW], writes=[bwpl])
            wgs = [sb(ph4, "wgs%d" % i, [128, 8, 128], BF16) for i in range(4)]
            wus = [sb(ph4, "wus%d" % i, [128, 8, 128], BF16) for i in range(4)]
            hx = [sb(ph4, "hx%d" % i, [128, 4, D]) for i in range(1)]
            hT4, bhT4 = sb(ph4, "hT4", [128, 8, 512], BF16)
            hb4 = [sb(ph4, "hb4_%d" % i, [128, D], BF16) for i in range(2)]
            junk4, bjunk4 = sb(ph4, "junk4", [128, D], BF16)
            s4, bs4 = sb(ph4, "s4", [128, 16])
            actT, bactT = sb(ph4, "actT", [128, NFC, 512], BF16)
            aoT, baoT = sb(ph4, "aoT", [64, 8, 512], BF16)
            dnT, bdnT = sb(ph4, "dnT", [128, 4, 512], BF16)
            e4 = [sb(ph4, "e4_%d" % i, [128, 4, 128]) for i in range(6)]
            dnb = [sb(ph4, "dnb%d" % i, [128, 512], BF16) for i in range(2)]
            sg4 = [sb(ph4, "sg4_%d" % i, [128, 512]) for i in range(2)]
            pin, bpin = sb(ph4, "pin", [128, PLE])
            pinb, bpinb = sb(ph4, "pinb", [128, PLE], BF16)
            pT4, bpT4 = sb(ph4, "pT4", [128, 2, 512], BF16)
            yo = [sb(ph4, "yo%d" % i, [128, D]) for i in range(1)]

            def norm_to_T(src_ap, bsrc, tt, idx):
                Sc.op("pool", lambda e: e.memset(s4[:, 0:1], 0.0), writes=[bs4])
                Sc.op("act", lambda e: e.activation(out=junk4[:], in_=src_ap, func=AF.Square,
                                                    accum_out=s4[:, 0:1]), reads=[bsrc, bs4], writes=[bjunk4, bs4])
                rstd_from_ss(s4[:, 0:1], s4[:, 2:3], s4[:, 1:2], 1.0 / D, [bs4])
                h_, bh_ = hb4[idx % 2]
                Sc.op("dve", lambda e: e.tensor_scalar(out=h_[:], in0=src_ap, scalar1=s4[:, 2:3], scalar2=None,
                                                       op0=ALU.mult), reads=[bsrc, bs4], writes=[bh_])
                transpose_bf(h_, bh_, 8, hT4[:, :, tt * 128:(tt + 1) * 128], bhT4)

            for b in range(NB if PHMAX >= 4 else 0):
                H, bH = hx[0]
                for tt in range(4):
                    ti = 4 * b + tt
                    r0 = ti * 128
                    of_, bof_ = e4[3 * (tt % 2)]
                    ob_, bob_ = e4[3 * (tt % 2) + 1]
                    z_, bz_ = e4[3 * (tt % 2) + 2]
                    Sc.dma(of_[:], OF_s[0, r0:r0 + 128, :].rearrange("p (h d) -> p h d", h=4), reads=[bOF], writes=[bof_])
                    Sc.dma(ob_[:], OF_s[1, r0:r0 + 128, :].rearrange("p (h d) -> p h d", h=4), reads=[bOF], writes=[bob_])
                    Sc.dma(z_[:], DZ_s[r0:r0 + 128, :].rearrange("p (h d) -> p h d", h=4), reads=[bDN], writes=[bz_])
                    o_, bo_ = of_, bof_
                    Sc.op("pool", lambda e: e.tensor_tensor(out=o_[:], in0=of_[:], in1=ob_[:], op=ALU.add),
                          reads=[bof_, bob_], writes=[bo_])
                    sq_, bsq_ = ob_, bob_
                    Sc.op("pool", lambda e: e.tensor_tensor(out=sq_[:], in0=o_[:], in1=o_[:], op=ALU.mult),
                          reads=[bo_], writes=[bsq_])
                    Sc.op("dve", lambda e: e.tensor_reduce(out=s4[:, 4:8], in_=sq_[:], axis=AX.X, op=ALU.add),
                          reads=[bsq_], writes=[bs4])
                    rstd_from_ss(s4[:, 4:8], s4[:, 12:16], s4[:, 8:12], 1.0 / 128, [bs4])
                    Sc.op("dve", lambda e: e.tensor_tensor(out=o_[:], in0=o_[:],
                                                           in1=s4[:, 12:16].unsqueeze(2).broadcast_to([128, 4, 128]),
                                                           op=ALU.mult), reads=[bo_, bs4], writes=[bo_])
                    Sc.op("pool", lambda e: e.tensor_tensor(out=o_[:], in0=o_[:],
                                                            in1=cg[:, 0:128].unsqueeze(1).broadcast_to([128, 4, 128]),
                                                            op=ALU.mult), reads=[bo_, bcg], writes=[bo_])
                    zs_, bzs_ = z_, bz_
                    Sc.op("act", lambda e: e.activation(out=zs_[:], in_=z_[:], func=AF.Silu), reads=[bz_], writes=[bzs_])
                    db_, bdb_ = dnb[tt % 2]
                    Sc.op("dve", lambda e: e.tensor_tensor(out=db_[:].rearrange("p (h d) -> p h d", h=4), in0=o_[:],
                                                           in1=zs_[:], op=ALU.mult), reads=[bo_, bzs_], writes=[bdb_])
                    transpose_bf(db_, bdb_, 4, dnT[:, :, tt * 128:(tt + 1) * 128], bdnT, evac="dve")
                Sc.dma(aoT[:], AO_s[:, :, b * 512:(b + 1) * 512].rearrange("h p s -> p h s"), reads=[bAO], writes=[baoT])
                for tt in range(4):
                    ti = 4 * b + tt
                    Sc.dma(H[:, tt, :], x_d[ti * 128:(ti + 1) * 128, :], writes=[bH])
                for tt in range(4):
                    ts_ = slice(tt * 128, (tt + 1) * 128)
                    for nh in range(2):
                        ns = slice(nh * 512, (nh + 1) * 512)
                        pt, bpt = bank()
                        for h in range(8):
                            Sc.op("pe", lambda e, h=h: e.matmul(pt[:], lhsT=aoT[:, h, ts_], rhs=woA[:, h, ns],
                                                                start=(h == 0), stop=False),
                                  reads=[baoT, bwoA], writes=[bpt], signal=False)
                        for cc in range(4):
                            Sc.op("pe", lambda e, cc=cc: e.matmul(pt[:], lhsT=dnT[:, cc, ts_], rhs=woD[:, cc, ns],
                                                                  start=False, stop=(cc == 3)),
                                  reads=[bdnT, bwoD], writes=[bpt], signal=(cc == 3))
                        Sc.op("dve", lambda e: e.tensor_tensor(out=H[:, tt, ns], in0=H[:, tt, ns], in1=pt[:], op=ALU.add),
                              reads=[bH, bpt], writes=[bH])
                    norm_to_T(H[:, tt, :], bH, tt, tt)
                for fc in range(NFC):
                    wg_, bwg_ = wgs[fc % 4]
                    wu_, bwu_ = wus[fc % 4]
                    Sc.dma(wg_[:], WG_s[fc], reads=[bW], writes=[bwg_])
                    Sc.dma(wu_[:], WU_s[fc], reads=[bW], writes=[bwu_])
                    pg, bpg = bank()
                    pu, bpu = bank()
                    for kc in range(8):
                        Sc.op("pe", lambda e, kc=kc: e.matmul(pg[:], lhsT=wg_[:, kc, :], rhs=hT4[:, kc, :],
                                                             start=(kc == 0), stop=(kc == 7)),
                              reads=[bwg_, bhT4], writes=[bpg], signal=(kc == 7))
                    for kc in range(8):
                        Sc.op("pe", lambda e, kc=kc: e.matmul(pu[:], lhsT=wu_[:, kc, :], rhs=hT4[:, kc, :],
                                                             start=(kc == 0), stop=(kc == 7)),
                              reads=[bwu_, bhT4], writes=[bpu], signal=(kc == 7))
                    sg_, bsg_ = sg4[fc % 2]
                    Sc.op("act", lambda e: e.activation(out=sg_[:], in_=pg[:], func=AF.Silu), reads=[bpg], writes=[bsg_])
                    Sc.op("dve", lambda e: e.tensor_tensor(out=actT[:, fc, :], in0=sg_[:], in1=pu[:], op=ALU.mult),
                          reads=[bsg_, bpu], writes=[bactT])
                for tt in range(4):
                    ts_ = slice(tt * 128, (tt + 1) * 128)
                    for nh in range(2):
                        ns = slice(nh * 512, (nh + 1) * 512)
                        pt, bpt = bank()
                        for fc in range(NFC):
                            Sc.op("pe", lambda e, fc=fc: e.matmul(pt[:], lhsT=actT[:, fc, ts_], rhs=wdn[:, fc, ns],
                                                                  start=(fc == 0), stop=(fc == NFC - 1)),
                                  reads=[bactT, bwdn], writes=[bpt], signal=(fc == NFC - 1))
                        Sc.op("dve", lambda e: e.tensor_tensor(out=H[:, tt, ns], in0=H[:, tt, ns], in1=pt[:], op=ALU.add),
                              reads=[bH, bpt], writes=[bH])
                    norm_to_T(H[:, tt, :], bH, tt, tt)
                for tt in range(4):
                    ti = 4 * b + tt
                    Sc.dma(pin[:], p_d[ti * 128:(ti + 1) * 128, :], writes=[bpin])
                    Sc.op("pool", lambda e: e.tensor_copy(out=pinb[:], in_=pin[:]), reads=[bpin], writes=[bpinb])
                    transpose_bf(pinb, bpinb, 2, pT4[:, :, tt * 128:(tt + 1) * 128], bpT4)
                for tt in range(4):
                    ti = 4 * b + tt
                    ts_ = slice(tt * 128, (tt + 1) * 128)
                    for nh in range(2):
                        ns = slice(nh * 512, (nh + 1) * 512)
                        pg, bpg = bank()
                        pl, bpl = bank()
                        for kc in range(8):
                            Sc.op("pe", lambda e, kc=kc: e.matmul(pg[:], lhsT=hT4[:, kc, ts_], rhs=wpg[:, kc, ns],
                                                                  start=(kc == 0), stop=(kc == 7)),
                                  reads=[bhT4, bwpg], writes=[bpg], signal=(kc == 7))
                        for kc in range(2):
                            Sc.op("pe", lambda e, kc=kc: e.matmul(pl[:], lhsT=pT4[:, kc, ts_], rhs=wpl[:, kc, ns],
                                                                  start=(kc == 0), stop=(kc == 1)),
                                  reads=[bpT4, bwpl], writes=[bpl], signal=(kc == 1))
                        sg_, bsg_ = sg4[nh]
                        Sc.op("act", lambda e: e.activation(out=sg_[:], in_=pg[:], func=AF.Sigmoid),
                              reads=[bpg], writes=[bsg_])
                        Sc.op("dve", lambda e: e.tensor_tensor(out=sg_[:], in0=sg_[:], in1=pl[:], op=ALU.mult),
                              reads=[bsg_, bpl], writes=[bsg_])
                        Sc.op("pool", lambda e: e.tensor_tensor(out=H[:, tt, ns], in0=H[:, tt, ns], in1=sg_[:], op=ALU.add),
                              reads=[bH, bsg_], writes=[bH])
                    Sc.op("pool", lambda e: e.memset(s4[:, 0:1], 0.0), writes=[bs4])
                    Sc.op("act", lambda e: e.activation(out=junk4[:], in_=H[:, tt, :], func=AF.Square,
                                                        accum_out=s4[:, 0:1]), reads=[bH, bs4], writes=[bjunk4, bs4])
                    rstd_from_ss(s4[:, 0:1], s4[:, 2:3], s4[:, 1:2], 1.0 / D, [bs4])
                    y_, by_ = yo[0]
                    Sc.op("dve", lambda e: e.scalar_tensor_tensor(out=y_[:], in0=H[:, tt, :], scalar=s4[:, 2:3],
                                                                  in1=cg[:, 128:128 + D], op0=ALU.mult, op1=ALU.mult),
                          reads=[bH, bs4, bcg], writes=[by_])
                    Sc.dma(out_d[ti * 128:(ti + 1) * 128, :], y_[:], reads=[by_])
        Sc.finish()
        print("ops:", Sc.nops, "cnt:", Sc.cnt)
    return nc


def host_consts(S, norm_mix, norm_ffn, norm_ple, q_norm, k_norm, a_log, dt_bias, conv_w, dn_norm, norm_final):
    f = np.float32
    i = np.arange(128)[:, None]
    j = np.arange(128)[None, :]
    same = (i // 64) == (j // 64)
    cm = np.zeros((128, 10, 128), f)
    cm[:, M_ID] = (i == j)
    cm[:, M_LOWI] = same & (i >= j)
    cm[:, M_UPPI] = same & (i <= j)
    cm[:, M_LOWS] = same & (i > j)
    cm[:, M_UPPS] = same & (i < j)
    cm[:, M_BLK] = same
    cm[:, M_ONES] = 1.0
    cm[:, M_CI0] = (i < 64) & (j >= 0)
    cm[:, M_CI1] = (i >= 64) & (j >= 0)
    Rm = np.zeros((128, 128), f)
    for fo in range(128):
        idx = fo % 32
        if idx < 16:
            Rm[fo + 16, fo] = -1.0
        else:
            Rm[fo - 16, fo] = 1.0
    cm[:, M_RM] = Rm
    cv = np.zeros((128, V_END), f)
    cv[:, V_GMIX:V_GMIX + 8] = norm_mix.reshape(8, 128).T
    cv[:, V_GFFN:V_GFFN + 8] = norm_ffn.reshape(8, 128).T
    cv[:, V_GPLE:V_GPLE + 8] = norm_ple.reshape(8, 128).T
    cv[:, V_QNG] = np.tile(q_norm, 2)
    cv[:, V_KNG] = np.tile(k_norm, 2)
    cv[:, V_ALOG:V_ALOG + 8] = a_log[None, :]
    cv[:, V_DTB:V_DTB + 8] = dt_bias[None, :]
    cv[:, V_CONV:V_CONV + 60] = conv_w.reshape(5, 12, 128).transpose(2, 1, 0).reshape(128, 60)
    cg = np.zeros((128, 128 + D), f)
    cg[:, 0:128] = dn_norm[None, :]
    cg[:, 128:] = norm_final[None, :]
    t = np.arange(S)
    row = (t // 64).astype(np.float64)
    col = (t % 64).astype(np.float64)
    inv_freq = (10000.0 ** (-np.arange(0, 32, 2, dtype=np.float32) / np.float32(32))).astype(np.float32)
    cosT = np.zeros((128, S), f)
    sinT = np.zeros((128, S), f)
    for pp in range(128):
        dd = pp % 64
        pos = row if dd < 32 else col
        ang = (pos.astype(np.float32) * inv_freq[dd % 16]).astype(np.float32)
        cosT[pp] = np.cos(ang)
        sinT[pp] = np.sin(ang)
    return cm, cv, cg, cosT, sinT


_NC_CACHE = {}


def run(S, ncores, x, p, norm_mix, w_in, conv_w, q_norm, k_norm, a_log, dt_bias, dn_norm, w_out,
        norm_ffn, w_gate, w_up, w_down, norm_ple, w_ple_gate, w_ple, norm_final):
    A = lambda a: np.ascontiguousarray(np.asarray(a, dtype=np.float32))
    cm, cv, cg, cosT, sinT = host_consts(S, A(norm_mix)[0], A(norm_ffn)[0], A(norm_ple)[0], A(q_norm)[0], A(k_norm)[0],
                                         A(a_log)[0], A(dt_bias)[0], A(conv_w)[0], A(dn_norm)[0], A(norm_final))
    if S not in _NC_CACHE:
        _NC_CACHE[S] = build(S)
    nc = _NC_CACHE[S]
    x = A(x)
    p = A(p)
    shared = {"w_in": A(w_in)[0], "w_out": A(w_out)[0], "w_gate": A(w_gate)[0], "w_up": A(w_up)[0],
              "w_down": A(w_down)[0], "w_pg": A(w_ple_gate)[0], "w_ple": A(w_ple)[0],
              "cm": cm, "cv": cv, "cg": cg, "cosT": cosT, "sinT": sinT}
    in_maps = []
    for c in range(ncores):
        m = dict(shared)
        m["x"] = np.ascontiguousarray(x[c])
        m["p"] = np.ascontiguousarray(p[0, c])
        in_maps.append(m)
    res = run_bass_kernel_spmd(nc, in_maps, core_ids=list(range(ncores)))
    return np.stack([np.asarray(r["out"], dtype=np.float32) for r in res.results], axis=0)


def kernel(**inputs):
    x = inputs["x"]
    return run(x.shape[1], x.shape[0], **inputs)
```

```python
import numpy as np
import concourse.bass as bass
import concourse.mybir as mybir
from concourse.bass_utils import run_bass_kernel_spmd
from contextlib import ExitStack
from itertools import zip_longest
import os
PHMAX = int(os.environ.get('KPH', '9'))

F32 = mybir.dt.float32
BF16 = mybir.dt.bfloat16
ALU = mybir.AluOpType
AF = mybir.ActivationFunctionType
AX = mybir.AxisListType

D = 1024
INW = 2832
FF = 2816
NFC = FF // 128
PLE = 256
EPS = 1e-6


class Buf:
    __slots__ = ("name", "w", "rs")

    def __init__(self, name):
        self.name = name
        self.w = None
        self.rs = []


class MBuf(Buf):
    __slots__ = ("ws",)

    def __init__(self, name):
        Buf.__init__(self, name)
        self.ws = []


def _compact(evl):
    best = {}
    for r in evl:
        if r[0] not in best or best[r[0]][2] < r[2]:
            best[r[0]] = r
    return list(best.values())


class Sched:
    ND = 24

    def __init__(self, nc, stack):
        self.nc = nc
        self.E = {"pe": nc.tensor, "act": nc.scalar, "dve": nc.vector,
                  "pool": nc.gpsimd, "sp": nc.sync}
        self.sem = {}
        self.cnt = {}
        for k in ("pe", "act", "dve", "pool"):
            self.sem[k] = stack.enter_context(nc.semaphore("sem_" + k))
            self.cnt[k] = 0
        self.dsem = [stack.enter_context(nc.semaphore("dsem%d" % i)) for i in range(self.ND)]
        self.dcnt = [0] * self.ND
        self.dnext = 0
        self.seen = {k: {} for k in self.E}
        self.nops = {k: 0 for k in self.E}

    def _wait(self, ek, ev):
        key, sem, val, src = ev
        if self.seen[ek].get(key, 0) >= val:
            return
        self.E[ek].wait_ge(sem, val)
        self.seen[ek][key] = val

    def _note(self, ev, reads, writes):
        for b in reads:
            b.rs.append(ev)
            if len(b.rs) > 16:
                best = {}
                for r in b.rs:
                    if r[0] not in best or best[r[0]][2] < r[2]:
                        best[r[0]] = r
                b.rs = list(best.values())
        for b in writes:
            if isinstance(b, MBuf):
                b.ws.append(ev)
                if len(b.ws) > 40:
                    b.ws = _compact(b.ws)
            else:
                b.w = ev
                b.rs = []

    def op(self, ek, fn, reads=(), writes=(), signal=True):
        evs = []
        for b in reads:
            if isinstance(b, MBuf):
                evs.extend(b.ws)
            elif b.w is not None:
                evs.append(b.w)
        for b in writes:
            if b.w is not None and b.w[3] != ek:
                evs.append(b.w)
            for r in b.rs:
                if r[3] != ek:
                    evs.append(r)
        for e in evs:
            if ek == "pe" and e[3] == "pe":
                continue
            self._wait(ek, e)
        inst = fn(self.E[ek])
        self.nops[ek] += 1
        if signal:
            self.cnt[ek] += 1
            inst.then_inc(self.sem[ek], 1)
            ev = (ek, self.sem[ek], self.cnt[ek], ek)
        else:
            ev = (ek, self.sem[ek], self.cnt[ek] + 1, ek)
        self._note(ev, reads, writes)
        return inst

    def dma(self, out, in_, reads=(), writes=(), q="sp", **kw):
        k = self.dnext
        self.dnext = (self.dnext + 1) % self.ND
        key = "d%d" % k
        if self.dcnt[k] > 0:
            self._wait(q, (key, self.dsem[k], self.dcnt[k], "dma"))
        evs = []
        for b in reads:
            if isinstance(b, MBuf):
                evs.extend(b.ws)
            elif b.w is not None:
                evs.append(b.w)
        for b in writes:
            if b.w is not None:
                evs.append(b.w)
            evs.extend(b.rs)
        for e in evs:
            self._wait(q, e)
        inst = self.E[q].dma_start(out=out, in_=in_, **kw)
        self.dcnt[k] += 16
        inst.then_inc(self.dsem[k], 16)
        self.nops[q] += 1
        ev = (key, self.dsem[k], self.dcnt[k], "dma")
        self._note(ev, reads, writes)
        return inst

    def barrier(self):
        evs = [(k, self.sem[k], self.cnt[k], k) for k in ("pe", "act", "dve", "pool") if self.cnt[k] > 0]
        evs += [("d%d" % k, self.dsem[k], self.dcnt[k], "dma") for k in range(self.ND) if self.dcnt[k] > 0]
        for ek in ("sp", "pe", "act", "dve", "pool"):
            for e in evs:
                self._wait(ek, e)

    def finish(self):
        for k in range(self.ND):
            if self.dcnt[k] > 0:
                self._wait("sp", ("d%d" % k, self.dsem[k], self.dcnt[k], "dma"))


M_ID, M_LOWI, M_UPPI, M_LOWS, M_UPPS, M_BLK, M_ONES, M_CI0, M_CI1, M_RM = range(10)
V_GMIX, V_GFFN, V_GPLE, V_QNG, V_KNG, V_ALOG, V_DTB, V_CONV, V_END = 0, 8, 16, 24, 25, 26, 34, 42, 102


def build(S):
    T = S // 128
    NB = S // 512
    nc = bass.Bass("TRN2", target_bir_lowering=False)

    def din(name, shape, dt=F32):
        return nc.dram_tensor(name, list(shape), dt, kind="ExternalInput").ap()

    def dscr(name, shape, dt=F32):
        return nc.dram_tensor(name, list(shape), dt).ap()

    x_d = din("x", [S, D])
    p_d = din("p", [S, PLE])
    win_d = din("w_in", [D, INW])
    wout_d = din("w_out", [D, D])
    wg_d = din("w_gate", [D, FF])
    wu_d = din("w_up", [D, FF])
    wd_d = din("w_down", [FF, D])
    wpg_d = din("w_pg", [D, D])
    wple_d = din("w_ple", [PLE, D])
    cm_d = din("cm", [128, 10, 128])
    cv_d = din("cv", [128, V_END])
    cg_d = din("cg", [128, 128 + D])
    cos_d = din("cosT", [128, S])
    sin_d = din("sinT", [128, S])
    out_d = nc.dram_tensor("out", [S, D], F32, kind="ExternalOutput").ap()

    QT_s = dscr("QT_s", [4, 128, S], BF16)
    KT_s = dscr("KT_s", [2, 128, S], BF16)
    V_s = dscr("V_s", [S, 128], BF16)
    DQT_s = dscr("DQT_s", [4, 128, S])
    DKT_s = dscr("DKT_s", [4, 128, S])
    DK_s = dscr("DK_s", [S, 4, 128])
    DV_s = dscr("DV_s", [S, 4, 128])
    GB_s = dscr("GB_s", [S, 16])
    DZ_s = dscr("DZ_s", [S, 512])
    AO_s = dscr("AO_s", [8, 64, S], BF16)
    OF_s = dscr("OF_s", [2, S, 512])
    WO_s = dscr("WO_s", [D, D], BF16)
    WG_s = dscr("WG_s", [NFC, 128, 8, 128], BF16)
    WU_s = dscr("WU_s", [NFC, 128, 8, 128], BF16)
    WD_s = dscr("WD_s", [FF, D], BF16)
    WPG_s = dscr("WPG_s", [D, D], BF16)
    WPL_s = dscr("WPL_s", [PLE, D], BF16)
    bQT, bKT, bV = MBuf("QT_s"), MBuf("KT_s"), MBuf("V_s")
    bDN = MBuf("DN_s")
    bAO, bOF = MBuf("AO_s"), MBuf("OF_s")
    bW = MBuf("W_s")

    top = ExitStack()
    with top:
        Sc = Sched(nc, top)

        def sb(stack, name, shape, dt=F32):
            return stack.enter_context(nc.sbuf_tensor("sb_" + name, list(shape), dt)), Buf(name)

        pb = []
        pairs = []
        for i in range(4):
            pr_ = top.enter_context(nc.psum_tensor("pp%d" % i, [128, 2, 512], F32))
            pairs.append(pr_)
            for j in range(2):
                pb.append((pr_[:, j, :], Buf("pb%d" % (2 * i + j))))
        pbn = [0]
        bankset = list(range(8))

        def bank():
            pbn[0] = (pbn[0] + 1) % len(bankset)
            return pb[bankset[pbn[0]]]

        cm, bcm = sb(top, "cm", [128, 10, 128])
        cv, bcv = sb(top, "cv", [128, V_END])
        idb, bidb = sb(top, "idb", [128, 128], BF16)
        epst, bepst = sb(top, "epst", [128, 2])
        nexpA, bnexpA = sb(top, "nexpA", [128, 8])
        Sc.dma(cm[:], cm_d, writes=[bcm])
        Sc.dma(cv[:], cv_d, writes=[bcv])
        Sc.op("dve", lambda e: e.tensor_copy(out=idb[:], in_=cm[:, M_ID, :]), reads=[bcm], writes=[bidb])
        Sc.op("pool", lambda e: e.memset(epst[:, 0:1], EPS), writes=[bepst])
        Sc.op("pool", lambda e: e.memset(epst[:, 1:2], 1.0), writes=[bepst])
        Sc.op("act", lambda e: e.activation(out=nexpA[:], in_=cv[:, V_ALOG:V_ALOG + 8], func=AF.Exp),
              reads=[bcv], writes=[bnexpA])
        Sc.op("dve", lambda e: e.tensor_scalar(out=nexpA[:], in0=nexpA[:], scalar1=-1.0, scalar2=None, op0=ALU.mult),
              reads=[bnexpA], writes=[bnexpA])
        ident = cm[:, M_ID, :]

        def rstd_from_ss(ss_ap, out_ap, tmp_ap, scale, bufs):
            Sc.op("act", lambda e: e.activation(out=tmp_ap, in_=ss_ap, func=AF.Ln, scale=scale, bias=epst[:, 0:1]),
                  reads=bufs + [bepst], writes=bufs)
            Sc.op("act", lambda e: e.activation(out=out_ap, in_=tmp_ap, func=AF.Exp, scale=-0.5),
                  reads=bufs, writes=bufs)

        def transpose_bf(src_tile, bsrc, nblk, dst_ap3, bdst, evac="act"):
            pt, bpt = bank()
            ptb = pt[:].bitcast(BF16)
            for k in range(nblk):
                Sc.op("pe", lambda e, k=k: e.transpose(out=ptb[:, k * 128:(k + 1) * 128],
                                                         in_=src_tile[:, k * 128:(k + 1) * 128], identity=idb[:]),
                      reads=[bsrc, bidb], writes=[bpt], signal=(k == nblk - 1))
            src3 = ptb[:, 0:nblk * 128].rearrange("p (k t) -> p k t", k=nblk)
            if evac == "act":
                Sc.op("act", lambda e: e.copy(out=dst_ap3, in_=src3), reads=[bpt], writes=[bdst])
            else:
                Sc.op("dve", lambda e: e.tensor_copy(out=dst_ap3, in_=src3), reads=[bpt], writes=[bdst])

        ph1 = ExitStack()
        with ph1:
            win, bwin = sb(ph1, "win", [128, 8, INW], BF16)
            wkd, bwkd = sb(ph1, "wkd", [128, 8, 2, 128], BF16)
            ph0 = ExitStack()
            ph0.__enter__()
            stg = [sb(ph0, "stg%d" % i, [128, INW]) for i in range(2)]
            stb = [sb(ph0, "stb%d" % i, [128, FF], BF16) for i in range(2)]
            for kc in range(8):
                st_, bst = stg[kc % 2]
                Sc.dma(st_[:], win_d[kc * 128:(kc + 1) * 128, :], writes=[bst])
                Sc.op("dve", lambda e, kc=kc, st_=st_: e.tensor_scalar(
                    out=win[:, kc, :], in0=st_[:], scalar1=cv[:, V_GMIX + kc:V_GMIX + kc + 1], scalar2=None,
                    op0=ALU.mult), reads=[bst, bcv], writes=[bwin])
            for g in range(2):
                for hf in range(2):
                    Sc.op("pool", lambda e, g=g, hf=hf: e.tensor_copy(
                        out=wkd[:, :, g, hf * 64:(hf + 1) * 64], in_=win[:, :, 512 + 64 * g:512 + 64 * g + 64]),
                        reads=[bwin], writes=[bwkd])
            cnt = [0]

            def conv_w(src_rows, ncols, gain_col, dst_ap):
                i = cnt[0] % 2
                cnt[0] += 1
                st_, bst = stg[i]
                sb_, bsb = stb[i]
                Sc.dma(st_[:, 0:ncols], src_rows, writes=[bst])
                if gain_col is None:
                    Sc.op("pool", lambda e: e.tensor_copy(out=sb_[:, 0:ncols], in_=st_[:, 0:ncols]),
                          reads=[bst], writes=[bsb])
                else:
                    Sc.op("dve", lambda e: e.tensor_scalar(out=sb_[:, 0:ncols], in0=st_[:, 0:ncols],
                                                           scalar1=cv[:, gain_col:gain_col + 1], scalar2=None,
                                                           op0=ALU.mult), reads=[bst, bcv], writes=[bsb])
                return sb_, bsb

            for kc in range(8):
                sb_, bsb = conv_w(wout_d[kc * 128:(kc + 1) * 128, :], D, None, None)
                Sc.dma(WO_s[kc * 128:(kc + 1) * 128, :], sb_[:, 0:D], reads=[bsb], writes=[bW])
            for kc in range(8):
                sb_, bsb = conv_w(wpg_d[kc * 128:(kc + 1) * 128, :], D, V_GPLE + kc, None)
                Sc.dma(WPG_s[kc * 128:(kc + 1) * 128, :], sb_[:, 0:D], reads=[bsb], writes=[bW])
            for kc in range(2):
                sb_, bsb = conv_w(wple_d[kc * 128:(kc + 1) * 128, :], D, None, None)
                Sc.dma(WPL_s[kc * 128:(kc + 1) * 128, :], sb_[:, 0:D], reads=[bsb], writes=[bW])
            for kc in range(NFC):
                sb_, bsb = conv_w(wd_d[kc * 128:(kc + 1) * 128, :], D, None, None)
                Sc.dma(WD_s[kc * 128:(kc + 1) * 128, :], sb_[:, 0:D], reads=[bsb], writes=[bW])
            for (src, dst) in ((wg_d, WG_s), (wu_d, WU_s)):
                for kc in range(8):
                    sb_, bsb = conv_w(src[kc * 128:(kc + 1) * 128, :], FF, V_GFFN + kc, None)
                    Sc.dma(dst[:, :, kc, :].rearrange("f p j -> p f j"),
                           sb_[:, 0:FF].rearrange("p (f j) -> p f j", f=NFC), reads=[bsb], writes=[bW])

            Sc.barrier()
            ph0.close()
            xt = [sb(ph1, "xt%d" % i, [128, D]) for i in range(2)]
            hnb = [sb(ph1, "hnb%d" % i, [128, D], BF16) for i in range(2)]
            junk, bjunk = sb(ph1, "junk", [128, D], BF16)
            st4, bst4 = sb(ph1, "st4", [128, 8])
            hnT = [sb(ph1, "hnT%d" % i, [128, 8, 512], BF16) for i in range(2)]
            pre = [sb(ph1, "pre%d" % i, [128, 12, 516]) for i in range(2)]
            cosb = [sb(ph1, "cosb%d" % i, [128, 512]) for i in range(2)]
            sinb = [sb(ph1, "sinb%d" % i, [128, 512]) for i in range(2)]
            wk = [sb(ph1, "wk%d" % i, [128, 512]) for i in range(8)]
            wkn = [0]

            def work():
                i = wkn[0]
                wkn[0] = (i + 1) % 8
                return wk[i]
            qkout = [sb(ph1, "qko%d" % i, [128, 512], BF16) for i in range(2)]
            vbt = [sb(ph1, "vbt%d" % i, [128, 128], BF16) for i in range(2)]
            zt = [sb(ph1, "zt%d" % i, [128, 512]) for i in range(2)]
            gbw, bgbw = sb(ph1, "gbw", [128, 64])
            gbo = [sb(ph1, "gbo%d" % i, [128, 16]) for i in range(2)]
            tkm = [sb(ph1, "tkm%d" % i, [128, 4, 128]) for i in range(2)]

            def qk_post(pt, bpt, kind, c, b):
                xs, bxs = work()
                Sc.op("act", lambda e: e.copy(out=xs[:], in_=pt[:]), reads=[bpt], writes=[bxs])
                sq, bsq = work()
                Sc.op("pool", lambda e: e.tensor_tensor(out=sq[:], in0=xs[:], in1=xs[:], op=ALU.mult),
                      reads=[bxs], writes=[bsq])
                p2, bp2 = bank()
                Sc.op("pe", lambda e: e.matmul(p2[:], lhsT=cm[:, M_BLK, :], rhs=sq[:], start=True, stop=True),
                      reads=[bcm, bsq], writes=[bp2])
                rn, brn = work()
                Sc.op("act", lambda e: e.activation(out=rn[:], in_=p2[:], func=AF.Ln, scale=1.0 / 64,
                                                    bias=epst[:, 0:1]), reads=[bp2, bepst], writes=[brn])
                Sc.op("act", lambda e: e.activation(out=rn[:], in_=rn[:], func=AF.Exp, scale=-0.5),
                      reads=[brn], writes=[brn])
                gcol = V_QNG if kind == "q" else V_KNG
                xn, bxn = work()
                Sc.op("dve", lambda e: e.scalar_tensor_tensor(out=xn[:], in0=xs[:], scalar=cv[:, gcol:gcol + 1],
                                                              in1=rn[:], op0=ALU.mult, op1=ALU.mult),
                      reads=[bxs, bcv, brn], writes=[bxn])
                p3, bp3 = bank()
                Sc.op("pe", lambda e: e.matmul(p3[:], lhsT=cm[:, M_RM, :], rhs=xn[:], start=True, stop=True),
                      reads=[bcm, bxn], writes=[bp3])
                t1, bt1 = work()
                Sc.op("pool", lambda e: e.tensor_tensor(out=t1[:], in0=xn[:], in1=cosb[b % 2][0][:], op=ALU.mult),
                      reads=[bxn, cosb[b % 2][1]], writes=[bt1])
                t2, bt2 = work()
                Sc.op("dve", lambda e: e.tensor_tensor(out=t2[:], in0=p3[:], in1=sinb[b % 2][0][:], op=ALU.mult),
                      reads=[bp3, sinb[b % 2][1]], writes=[bt2])
                qo, bqo = qkout[c % 2]
                Sc.op("dve", lambda e: e.tensor_tensor(out=qo[:], in0=t1[:], in1=t2[:], op=ALU.add),
                      reads=[bt1, bt2], writes=[bqo])
                if kind == "q":
                    Sc.dma(QT_s[c, :, b * 512:(b + 1) * 512], qo[:], reads=[bqo], writes=[bQT])
                else:
                    Sc.dma(KT_s[c, :, b * 512:(b + 1) * 512], qo[:], reads=[bqo], writes=[bKT])

            def dn_post(b):
                pr, bpr = pre[b % 2]
                for ci in range(12):
                    eng = "dve"
                    acc, bacc = work()
                    Sc.op(eng, lambda e: e.tensor_scalar(out=acc[:], in0=pr[:, ci, 0:512],
                                                         scalar1=cv[:, V_CONV + ci * 5:V_CONV + ci * 5 + 1],
                                                         scalar2=None, op0=ALU.mult),
                          reads=[bpr, bcv], writes=[bacc])
                    for tap in range(1, 5):
                        Sc.op(eng, lambda e, tap=tap: e.scalar_tensor_tensor(
                            out=acc[:], in0=pr[:, ci, tap:tap + 512],
                            scalar=cv[:, V_CONV + ci * 5 + tap:V_CONV + ci * 5 + tap + 1],
                            in1=acc[:], op0=ALU.mult, op1=ALU.add), reads=[bpr, bcv, bacc], writes=[bacc])
                    s, bs = work()
                    Sc.op("act", lambda e: e.activation(out=s[:], in_=acc[:], func=AF.Silu), reads=[bacc], writes=[bs])
                    h = ci % 4
                    if ci < 8:
                        sq, bsq = work()
                        Sc.op("pool", lambda e: e.tensor_tensor(out=sq[:], in0=s[:], in1=s[:], op=ALU.mult),
                              reads=[bs], writes=[bsq])
                        p2, bp2 = bank()
                        Sc.op("pe", lambda e: e.matmul(p2[:], lhsT=cm[:, M_ONES, :], rhs=sq[:], start=True, stop=True),
                              reads=[bcm, bsq], writes=[bp2])
                        rn, brn = work()
                        Sc.op("act", lambda e: e.activation(out=rn[:], in_=p2[:], func=AF.Ln, scale=1.0,
                                                            bias=epst[:, 0:1]), reads=[bp2, bepst], writes=[brn])
                        Sc.op("act", lambda e: e.activation(out=rn[:], in_=rn[:], func=AF.Exp, scale=-0.5),
                              reads=[brn], writes=[brn])
                        o, bo = work()
                        sc_ = (128.0 ** -0.5) if ci < 4 else 1.0
                        Sc.op("dve", lambda e: e.scalar_tensor_tensor(out=o[:], in0=s[:], scalar=sc_, in1=rn[:],
                                                                      op0=ALU.mult, op1=ALU.mult),
                              reads=[bs, brn], writes=[bo])
                        dst = DQT_s if ci < 4 else DKT_s
                        Sc.dma(dst[h, :, b * 512:(b + 1) * 512], o[:], reads=[bo], writes=[bDN])
                    else:
                        o, bo = s, bs
                    if ci >= 4:
                        pt, bpt = bank()
                        for tt in range(4):
                            Sc.op("pe", lambda e, tt=tt: e.transpose(out=pt[:, tt * 128:(tt + 1) * 128],
                                                                     in_=o[:, tt * 128:(tt + 1) * 128],
                                                                     identity=ident),
                                  reads=[bo, bcm], writes=[bpt], signal=(tt == 3))
                        tk, btk = tkm[ci % 2]
                        Sc.op("act", lambda e: e.copy(out=tk[:], in_=pt[:].rearrange("p (t d) -> p t d", t=4)),
                              reads=[bpt], writes=[btk])
                        dst = DK_s if ci < 8 else DV_s
                        Sc.dma(dst[b * 512:(b + 1) * 512, h, :].rearrange("(t p) d -> p t d", p=128), tk[:],
                               reads=[btk], writes=[bDN])

            for b in range(NB if PHMAX >= 1 else 0):
                hT, bhT = hnT[b % 2]
                Sc.dma(cosb[b % 2][0][:], cos_d[:, b * 512:(b + 1) * 512], writes=[cosb[b % 2][1]])
                Sc.dma(sinb[b % 2][0][:], sin_d[:, b * 512:(b + 1) * 512], writes=[sinb[b % 2][1]])
                for tt in range(4):
                    ti = 4 * b + tt
                    x_, bx_ = xt[ti % 2]
                    h_, bh_ = hnb[ti % 2]
                    Sc.dma(x_[:], x_d[ti * 128:(ti + 1) * 128, :], writes=[bx_])
                    Sc.op("pool", lambda e: e.memset(st4[:, 0:1], 0.0), writes=[bst4])
                    Sc.op("act", lambda e: e.activation(out=junk[:], in_=x_[:], func=AF.Square,
                                                        accum_out=st4[:, 0:1]), reads=[bx_, bst4], writes=[bjunk, bst4])
                    rstd_from_ss(st4[:, 0:1], st4[:, 2:3], st4[:, 1:2], 1.0 / D, [bst4])
                    Sc.op("dve", lambda e: e.tensor_scalar(out=h_[:], in0=x_[:], scalar1=st4[:, 2:3], scalar2=None,
                                                           op0=ALU.mult), reads=[bx_, bst4], writes=[bh_])
                    transpose_bf(h_, bh_, 8, hT[:, :, tt * 128:(tt + 1) * 128], bhT)
                chunks = [("q", c, win, lambda kc, c=c: win[:, kc, c * 128:(c + 1) * 128]) for c in range(4)]
                chunks += [("k", g, wkd, lambda kc, g=g: wkd[:, kc, g, :]) for g in range(2)]
                chunks += [("d", ci, win, lambda kc, ci=ci: win[:, kc, 768 + ci * 128:768 + (ci + 1) * 128])
                           for ci in range(12)]
                for kind, c, _, wsel in chunks:
                    pt, bpt = bank()
                    for kc in range(8):
                        Sc.op("pe", lambda e, kc=kc: e.matmul(pt[:], lhsT=wsel(kc), rhs=hT[:, kc, :],
                                                             start=(kc == 0), stop=(kc == 7)),
                              reads=[bwin, bwkd, bhT], writes=[bpt], signal=(kc == 7))
                    if kind == "d":
                        Sc.op("act", lambda e: e.copy(out=pre[b % 2][0][:, c, 2:514], in_=pt[:]),
                              reads=[bpt], writes=[pre[b % 2][1]])
                    else:
                        qk_post(pt, bpt, kind, c, b)
                for tt in range(4):
                    ti = 4 * b + tt
                    pt, bpt = bank()
                    for kc in range(8):
                        Sc.op("pe", lambda e, kc=kc: e.matmul(pt[:, 0:128], lhsT=hT[:, kc, tt * 128:(tt + 1) * 128],
                                                             rhs=win[:, kc, 640:768], start=(kc == 0), stop=(kc == 7)),
                              reads=[bwin, bhT], writes=[bpt], signal=False)
                    for kc in range(8):
                        Sc.op("pe", lambda e, kc=kc: e.matmul(pt[:, 128:144], lhsT=hT[:, kc, tt * 128:(tt + 1) * 128],
                                                             rhs=win[:, kc, 2816:2832], start=(kc == 0), stop=(kc == 7)),
                              reads=[bwin, bhT], writes=[bpt], signal=(kc == 7))
                    vb, bvb = vbt[ti % 2]
                    Sc.op("act", lambda e: e.copy(out=vb[:], in_=pt[:, 0:128]), reads=[bpt], writes=[bvb])
                    Sc.dma(V_s[ti * 128:(ti + 1) * 128, :], vb[:], reads=[bvb], writes=[bV])
                    go, bgo = gbo[ti % 2]
                    Sc.op("act", lambda e: e.activation(out=gbw[:, 0:8], in_=pt[:, 128:136], func=AF.Exp, scale=-1.0),
                          reads=[bpt], writes=[bgbw])
                    Sc.op("dve", lambda e: e.tensor_scalar(out=gbw[:, 0:8], in0=gbw[:, 0:8], scalar1=1.0, scalar2=None,
                                                           op0=ALU.add), reads=[bgbw], writes=[bgbw])
                    Sc.op("dve", lambda e: e.reciprocal(out=go[:, 8:16], in_=gbw[:, 0:8]), reads=[bgbw], writes=[bgo])
                    Sc.op("dve", lambda e: e.tensor_tensor(out=gbw[:, 8:16], in0=pt[:, 136:144],
                                                           in1=cv[:, V_DTB:V_DTB + 8], op=ALU.add),
                          reads=[bpt, bcv], writes=[bgbw])
                    Sc.op("act", lambda e: e.activation(out=gbw[:, 16:24], in_=gbw[:, 8:16], func=AF.Exp),
                          reads=[bgbw], writes=[bgbw])
                    Sc.op("act", lambda e: e.activation(out=gbw[:, 24:32], in_=gbw[:, 16:24], func=AF.Ln,
                                                        bias=epst[:, 1:2]), reads=[bgbw, bepst], writes=[bgbw])
                    Sc.op("dve", lambda e: e.tensor_tensor(out=go[:, 0:8], in0=gbw[:, 24:32], in1=nexpA[:], op=ALU.mult),
                          reads=[bgbw, bnexpA], writes=[bgo])
                    Sc.dma(GB_s[ti * 128:(ti + 1) * 128, :], go[:], reads=[bgo], writes=[bDN])
                    pz, bpz = bank()
                    for kc in range(8):
                        Sc.op("pe", lambda e, kc=kc: e.matmul(pz[:], lhsT=hT[:, kc, tt * 128:(tt + 1) * 128],
                                                             rhs=win[:, kc, 2304:2816], start=(kc == 0), stop=(kc == 7)),
                              reads=[bwin, bhT], writes=[bpz], signal=(kc == 7))
                    z_, bz_ = zt[ti % 2]
                    Sc.op("act", lambda e: e.copy(out=z_[:], in_=pz[:]), reads=[bpz], writes=[bz_])
                    Sc.dma(DZ_s[ti * 128:(ti + 1) * 128, :], z_[:], reads=[bz_], writes=[bDN])
                pr, bpr = pre[b % 2]
                if b == 0:
                    Sc.op("pool", lambda e: e.memset(pr[:, :, 0:2], 0.0), writes=[bpr])
                else:
                    pp, bpp = pre[(b - 1) % 2]
                    Sc.op("pool", lambda e: e.tensor_copy(out=pr[:, :, 0:2], in_=pp[:, :, 512:514]),
                          reads=[bpp], writes=[bpr])
                    Sc.op("pool", lambda e: e.tensor_copy(out=pp[:, :, 514:516], in_=pr[:, :, 2:4]),
                          reads=[bpr], writes=[bpp])
                    dn_post(b - 1)
                if b == NB - 1:
                    Sc.op("pool", lambda e: e.memset(pr[:, :, 514:516], 0.0), writes=[bpr])
                    dn_post(b)

        Sc.barrier()
        ph2 = ExitStack()
        with ph2:
            NW = 13
            dw = [[sb(ph2, "dw%d_%d" % (d, i), [128, 4, 128]) for i in range(NW)] for d in range(2)]
            dinp = [[[sb(ph2, "di%d_%d_%d" % (d, q, i), [128, 4, 128]) for i in range(4)] for q in range(2)]
                    for d in range(2)]
            Sst = [sb(ph2, "Sst%d" % d, [128, 4, 128]) for d in range(2)]
            dsm = [[sb(ph2, "dsm%d_%d" % (d, i), [128, 32]) for i in range(2)] for d in range(2)]
            for d in range(2):
                Sc.op("pool", lambda e, d=d: e.memset(Sst[d][0][:], 0.0), writes=[Sst[d][1]])

            def bc_h(ap_h):
                return ap_h.unsqueeze(2).broadcast_to([128, 4, 128])

            def bc_m(ap_m):
                return ap_m.unsqueeze(1).broadcast_to([128, 4, 128])

            def dn_unit(t, d, step):
                W_ = dw[d]
                wi = [0]

                def wt():
                    r = W_[wi[0]]
                    wi[0] += 1
                    return r
                s0 = t * 128
                (kT, bkT), (qT, bqT), (ktok, bktok), (vtok, bvtok) = dinp[d][step % 2]
                sm, bsm = dsm[d][step % 2]
                S_, bS_ = Sst[d]
                Sc.dma(kT[:], DKT_s[:, :, s0:s0 + 128].rearrange("h p s -> p h s"), reads=[bDN], writes=[bkT])
                Sc.dma(qT[:], DQT_s[:, :, s0:s0 + 128].rearrange("h p s -> p h s"), reads=[bDN], writes=[bqT])
                Sc.dma(ktok[:], DK_s[s0:s0 + 128, :, :], reads=[bDN], writes=[bktok])
                Sc.dma(vtok[:], DV_s[s0:s0 + 128, :, :], reads=[bDN], writes=[bvtok])
                Sc.dma(sm[:, 0:16], GB_s[s0:s0 + 128, :], reads=[bDN], writes=[bsm])
                g_ap = sm[:, 4 * d:4 * d + 4]
                beta_ap = sm[:, 8 + 4 * d:8 + 4 * d + 4]
                tri = M_UPPI if d == 0 else M_LOWI
                m_incl = M_LOWI if d == 0 else M_UPPI
                m_inclT = M_UPPI if d == 0 else M_LOWI
                m_str = M_LOWS if d == 0 else M_UPPS
                yield
                pa, bpa = bank()
                for j, mi in enumerate((tri, M_BLK, M_CI0, M_CI1)):
                    Sc.op("pe", lambda e, j=j, mi=mi: e.matmul(pa[:, 16 * j:16 * j + 16], lhsT=cm[:, mi, :], rhs=sm[:, 0:16],
                                                               start=True, stop=True),
                          reads=[bcm, bsm], writes=[bpa], signal=(j == 3))
                Sc.op("dve", lambda e: e.tensor_copy(
                    out=sm[:, 16:32].rearrange("p (j c) -> p j c", j=4),
                    in_=pa[:, 0:64].rearrange("p (j c) -> p j c", j=4)[:, :, 4 * d:4 * d + 4]), reads=[bpa], writes=[bsm])
                sm2, bsm2 = wt()
                sm2f = sm2[:].rearrange("p h d -> p (h d)")
                Sc.op("act", lambda e: e.activation(out=sm2f[:, 0:16], in_=sm[:, 16:32], func=AF.Exp),
                      reads=[bsm], writes=[bsm2])
                Sc.op("dve", lambda e: e.tensor_tensor(out=sm2f[:, 16:20], in0=sm[:, 20:24], in1=sm[:, 16:20],
                                                       op=ALU.subtract), reads=[bsm], writes=[bsm2])
                Sc.op("act", lambda e: e.activation(out=sm2f[:, 20:24], in_=sm2f[:, 16:20], func=AF.Exp),
                      reads=[bsm2], writes=[bsm2])
                Sc.op("dve", lambda e: e.tensor_tensor(out=sm2f[:, 24:28], in0=beta_ap, in1=sm2f[:, 0:4], op=ALU.mult),
                      reads=[bsm, bsm2], writes=[bsm2])
                G_ap = sm[:, 16:20]
                eGl_ap = sm2f[:, 20:24]
                beG_ap = sm2f[:, 24:28]
                gt_ap = [sm2f[:, 8:12], sm2f[:, 12:16]]
                yield
                dg, bdg = wt()
                Sc.op("dve", lambda e: e.tensor_tensor(out=dg[:], in0=bc_m(ident), in1=bc_h(G_ap), op=ALU.mult),
                      reads=[bcm, bsm], writes=[bdg])
                pB, bpB = bank()
                for h in range(4):
                    Sc.op("pe", lambda e, h=h: e.matmul(pB[:, h * 128:(h + 1) * 128], lhsT=cm[:, M_ONES, :],
                                                        rhs=dg[:, h, :], start=True, stop=True),
                          reads=[bcm, bdg], writes=[bpB], signal=(h == 3))
                pB3 = pB[:].rearrange("p (h d) -> p h d", h=4)
                t1, bt1 = wt()
                ebc, bebc = wt()
                Sc.op("dve", lambda e: e.tensor_copy(out=ebc[:], in_=pB3), reads=[bpB], writes=[bebc])
                Sc.op("dve", lambda e: e.tensor_tensor(out=t1[:], in0=ebc[:], in1=bc_h(G_ap), op=ALU.subtract),
                      reads=[bebc, bsm], writes=[bt1])
                Sc.op("act", lambda e: e.activation(out=ebc[:], in_=ebc[:], func=AF.Exp), reads=[bebc, bt1], writes=[bebc])
                qdT, bqdT = wt()
                Sc.op("pool", lambda e: e.tensor_tensor(out=qdT[:], in0=qT[:], in1=ebc[:], op=ALU.mult),
                      reads=[bqT, bebc], writes=[bqdT])
                ta, bta = wt()
                tb, btb = wt()
                Sc.op("dve", lambda e: e.tensor_scalar(out=ta[:], in0=t1[:], scalar1=0.0, scalar2=-1.0,
                                                       op0=ALU.max, op1=ALU.mult), reads=[bt1], writes=[bta])
                Sc.op("pool", lambda e: e.tensor_scalar(out=tb[:], in0=t1[:], scalar1=0.0, scalar2=None,
                                                        op0=ALU.min), reads=[bt1], writes=[btb])
                Sc.op("act", lambda e: e.activation(out=ta[:], in_=ta[:], func=AF.Exp), reads=[bta], writes=[bta])
                Sc.op("act", lambda e: e.activation(out=tb[:], in_=tb[:], func=AF.Exp), reads=[btb], writes=[btb])
                yield
                DmS, bDmS = wt()
                DmT, bDmT = wt()
                Sc.op("pool", lambda e: e.tensor_tensor(out=DmS[:], in0=ta[:], in1=bc_m(cm[:, m_str, :]), op=ALU.mult),
                      reads=[bta, bcm], writes=[bDmS])
                Sc.op("pool", lambda e: e.tensor_tensor(out=DmT[:], in0=tb[:], in1=bc_m(cm[:, m_inclT, :]), op=ALU.mult),
                      reads=[btb, bcm], writes=[bDmT])
                pC, bpC = bank()
                pD, bpD = bank()
                for h in range(4):
                    Sc.op("pe", lambda e, h=h: e.matmul(pC[:, h * 128:(h + 1) * 128], lhsT=kT[:, h, :], rhs=kT[:, h, :],
                                                        start=True, stop=True), reads=[bkT], writes=[bpC],
                          signal=(h == 3))
                for h in range(4):
                    Sc.op("pe", lambda e, h=h: e.matmul(pD[:, h * 128:(h + 1) * 128], lhsT=kT[:, h, :], rhs=qT[:, h, :],
                                                        start=True, stop=True), reads=[bkT, bqT], writes=[bpD],
                          signal=(h == 3))
                PB = [wt(), (dg, bdg)]
                QB = [wt(), (t1, bt1)]
                WB = [wt(), (ta, bta)]
                P_, bP_ = PB[0]
                Sc.op("dve", lambda e: e.tensor_tensor(out=P_[:], in0=pC[:].rearrange("p (h d) -> p h d", h=4),
                                                       in1=DmS[:], op=ALU.mult), reads=[bpC, bDmS], writes=[bP_])
                Sc.op("dve", lambda e: e.tensor_tensor(out=P_[:], in0=P_[:], in1=bc_h(beta_ap), op=ALU.mult),
                      reads=[bP_, bsm], writes=[bP_])
                QKDT, bQKDT = wt()
                Sc.op("dve", lambda e: e.tensor_tensor(out=QKDT[:], in0=pD[:].rearrange("p (h d) -> p h d", h=4),
                                                       in1=DmT[:], op=ALU.mult), reads=[bpD, bDmT], writes=[bQKDT])
                yield
                pE, bpE = bank()
                for h in range(4):
                    Sc.op("pe", lambda e, h=h: e.transpose(out=pE[:, h * 128:(h + 1) * 128], in_=P_[:, h, :],
                                                           identity=ident), reads=[bP_, bcm], writes=[bpE],
                          signal=(h == 3))
                pE3 = pE[:].rearrange("p (h d) -> p h d", h=4)
                Q_, bQ_ = QB[0]
                Wc, bWc = WB[0]
                Sc.op("dve", lambda e: e.tensor_copy(out=Q_[:], in_=pE3), reads=[bpE], writes=[bQ_])
                Sc.op("dve", lambda e: e.tensor_tensor(out=Wc[:], in0=bc_m(ident), in1=Q_[:], op=ALU.subtract),
                      reads=[bcm, bQ_], writes=[bWc])
                yield
                for lvl in range(1, 6):
                    pX, bpX = bank()
                    for h in range(4):
                        Sc.op("pe", lambda e, h=h: e.matmul(pX[:, h * 128:(h + 1) * 128], lhsT=Q_[:, h, :],
                                                            rhs=P_[:, h, :], start=True, stop=True),
                              reads=[bQ_, bP_], writes=[bpX], signal=(h == 3))
                    if lvl < 5:
                        pY, bpY = bank()
                        for h in range(4):
                            Sc.op("pe", lambda e, h=h: e.matmul(pY[:, h * 128:(h + 1) * 128], lhsT=P_[:, h, :],
                                                                rhs=Q_[:, h, :], start=True, stop=True),
                                  reads=[bQ_, bP_], writes=[bpY], signal=(h == 3))
                    Pn, bPn = PB[lvl % 2]
                    Sc.op("dve", lambda e: e.tensor_copy(out=Pn[:], in_=pX[:].rearrange("p (h d) -> p h d", h=4)),
                          reads=[bpX], writes=[bPn])
                    if lvl < 5:
                        Qn, bQn = QB[lvl % 2]
                        Sc.op("dve", lambda e: e.tensor_copy(out=Qn[:], in_=pY[:].rearrange("p (h d) -> p h d", h=4)),
                              reads=[bpY], writes=[bQn])
                    pZ, bpZ = bank()
                    for h in range(4):
                        Sc.op("pe", lambda e, h=h: e.matmul(pZ[:, h * 128:(h + 1) * 128], lhsT=Pn[:, h, :],
                                                            rhs=Wc[:, h, :], start=True, stop=True),
                              reads=[bPn, bWc], writes=[bpZ], signal=(h == 3))
                    Wn, bWn = WB[lvl % 2]
                    Sc.op("dve", lambda e: e.tensor_tensor(out=Wn[:], in0=pZ[:].rearrange("p (h d) -> p h d", h=4),
                                                           in1=Wc[:], op=ALU.add), reads=[bpZ, bWc], writes=[bWn])
                    Wc, bWc = Wn, bWn
                    P_, bP_ = Pn, bPn
                    if lvl < 5:
                        Q_, bQ_ = Qn, bQn
                    yield
                T1, bT1 = tb, btb
                T2, bT2 = ebc, bebc
                Sc.op("pool", lambda e: e.tensor_tensor(out=T1[:], in0=Wc[:], in1=bc_h(beta_ap), op=ALU.mult),
                      reads=[bWc, bsm], writes=[bT1])
                Sc.op("dve", lambda e: e.tensor_tensor(out=T2[:], in0=Wc[:], in1=bc_h(beG_ap), op=ALU.mult),
                      reads=[bWc, bsm2], writes=[bT2])
                pU, bpU = bank()
                pW, bpW = bank()
                for h in range(4):
                    Sc.op("pe", lambda e, h=h: e.matmul(pU[:, h * 128:(h + 1) * 128], lhsT=T1[:, h, :],
                                                        rhs=vtok[:, h, :], start=True, stop=True),
                          reads=[bT1, bvtok], writes=[bpU], signal=(h == 3))
                for h in range(4):
                    Sc.op("pe", lambda e, h=h: e.matmul(pW[:, h * 128:(h + 1) * 128], lhsT=ktok[:, h, :],
                                                        rhs=T2[:, h, :], start=True, stop=True),
                          reads=[bT2, bktok], writes=[bpW], signal=(h == 3))
                u_, bu_ = DmS, bDmS
                wT_, bwT_ = DmT, bDmT
                Sc.op("dve", lambda e: e.tensor_copy(out=u_[:], in_=pU[:].rearrange("p (h d) -> p h d", h=4)),
                      reads=[bpU], writes=[bu_])
                Sc.op("dve", lambda e: e.tensor_copy(out=wT_[:], in_=pW[:].rearrange("p (h d) -> p h d", h=4)),
                      reads=[bpW], writes=[bwT_])
                kdec, bkdec = PB[0]
                Sc.op("pool", lambda e: e.tensor_tensor(out=kdec[:], in0=ktok[:], in1=bc_h(eGl_ap), op=ALU.mult),
                      reads=[bktok, bsm2], writes=[bkdec])
                vn, bvn = QB[0]
                stmp, bstmp = QB[1]
                yield
                ot, bot = PB[1]
                corder = (0, 1) if d == 0 else (1, 0)
                for c in corder:
                    c0 = 64 * c
                    cs = slice(c0, c0 + 64)
                    tp = (c0, 0) if c0 else None
                    pV, bpV = bank()
                    for h in range(4):
                        Sc.op("pe", lambda e, h=h: e.matmul(pV[:, h * 128:(h + 1) * 128], lhsT=wT_[:, h, :],
                                                            rhs=S_[:, h, :], start=True, stop=True),
                              reads=[bwT_, bS_], writes=[bpV], signal=(h == 3))
                    Sc.op("dve", lambda e: e.tensor_tensor(out=vn[cs], in0=u_[cs],
                                                           in1=pV[cs, :].rearrange("p (h d) -> p h d", h=4),
                                                           op=ALU.subtract), reads=[bu_, bpV], writes=[bvn])
                    pO, bpO = bank()
                    for h in range(4):
                        Sc.op("pe", lambda e, h=h: e.matmul(pO[:, h * 128:(h + 1) * 128], lhsT=qdT[:, h, :],
                                                            rhs=S_[:, h, :], start=True, stop=False),
                              reads=[bqdT, bS_], writes=[bpO], signal=False)
                        Sc.op("pe", lambda e, h=h: e.matmul(pO[:, h * 128:(h + 1) * 128], lhsT=QKDT[:, h, :],
                                                            rhs=vn[:, h, :], start=False, stop=True),
                              reads=[bQKDT, bvn], writes=[bpO], signal=(h == 3))
                    pS, bpS = bank()
                    for h in range(4):
                        Sc.op("pe", lambda e, h=h: e.matmul(pS[:, h * 128:(h + 1) * 128], lhsT=kdec[cs, h, :],
                                                            rhs=vn[cs, h, :], start=True, stop=True,
                                                            tile_position=tp),
                              reads=[bkdec, bvn], writes=[bpS], signal=(h == 3))
                    Sc.op("dve", lambda e: e.tensor_copy(out=ot[cs], in_=pO[cs, :].rearrange("p (h d) -> p h d", h=4)),
                          reads=[bpO], writes=[bot])
                    Sc.op("pool", lambda e: e.tensor_tensor(out=stmp[:], in0=S_[:], in1=bc_h(gt_ap[c]), op=ALU.mult),
                          reads=[bS_, bsm2], writes=[bstmp])
                    Sc.op("dve", lambda e: e.tensor_tensor(out=S_[:], in0=stmp[:],
                                                           in1=pS[:].rearrange("p (h d) -> p h d", h=4), op=ALU.add),
                          reads=[bstmp, bpS], writes=[bS_])
                    yield
                Sc.dma(OF_s[d, s0:s0 + 128, :].rearrange("p (h d) -> p h d", h=4), ot[:], reads=[bot], writes=[bOF])

            def dn_gen():
                for step in range(T if PHMAX >= 2 else 0):
                    gens = [dn_unit(step, 0, step), dn_unit(T - 1 - step, 1, step)]
                    for _ in zip_longest(*gens):
                        yield

            ph3 = ExitStack()
            with ph3:
                KT2, bKT2 = sb(ph3, "KT2", [128, 2, S], BF16)
                Vaug, bVaug = sb(ph3, "Vaug", [128, T, 2, 65], BF16)
                onesr, bonesr = sb(ph3, "onesr", [128, 64])
                Sc.op("pool", lambda e: e.memset(Vaug[:], 1.0), writes=[bVaug])
                Sc.op("pool", lambda e: e.memset(onesr[:], 1.0), writes=[bonesr])
                for g in range(2):
                    for c0 in range(0, S, 2048):
                        c1 = min(S, c0 + 2048)
                        Sc.dma(KT2[:, g, c0:c1], KT_s[g, :, c0:c1], reads=[bKT], writes=[bKT2])
                TG = 8
                for t0 in range(0, T, TG):
                    nt = min(TG, T - t0)
                    for g in range(2):
                        Sc.dma(Vaug[:, t0:t0 + nt, g, 0:64],
                               V_s[t0 * 128:(t0 + nt) * 128, g * 64:(g + 1) * 64].rearrange("(t p) d -> p t d", p=128),
                               reads=[bV], writes=[bVaug])
                qtc = [sb(ph3, "qtc%d" % i, [128, 512], BF16) for i in range(2)]
                PT = [sb(ph3, "PT%d" % i, [128, 2, 512], BF16) for i in range(3)]
                oa = [sb(ph3, "oa%d" % i, [128, 512]) for i in range(2)]
                rs_, brs_ = sb(ph3, "rs", [128, 512])
                aob = [sb(ph3, "aob%d" % i, [64, 512], BF16) for i in range(2)]
                itc = [0]

                def attn_gen():
                  for qb in range(NB if PHMAX >= 3 else 0):
                    for c in range(4):
                        g = c // 2
                        q_, bq_ = qtc[itc[0] % 2]
                        itc[0] += 1
                        Sc.dma(q_[:], QT_s[c, :, qb * 512:(qb + 1) * 512], reads=[bQT], writes=[bq_])
                        pOa, bpOa = pb[6]
                        pOb, bpOb = pb[7]

                        def qk(kt):
                            pSa, bpSa = pb[2 * (kt % 2)]
                            pSb, bpSb = pb[2 * (kt % 2) + 1]
                            Sc.op("pe", lambda e: e.matmul(pSa[:], lhsT=KT2[0:64, g, kt * 128:(kt + 1) * 128],
                                                           rhs=q_[0:64, :], start=True, stop=True),
                                  reads=[bKT2, bq_], writes=[bpSa], signal=False)
                            Sc.op("pe", lambda e: e.matmul(pSb[:], lhsT=KT2[64:128, g, kt * 128:(kt + 1) * 128],
                                                           rhs=q_[64:128, :], start=True, stop=True,
                                                           tile_position=(64, 0)),
                                  reads=[bKT2, bq_], writes=[bpSb])

                        qk(0)
                        for kt in range(T):
                            if kt + 1 < T:
                                qk(kt + 1)
                            bpSa = pb[2 * (kt % 2)][1]
                            bpSb = pb[2 * (kt % 2) + 1][1]
                            P_, bP_ = PT[kt % 3]
                            Sc.op("act", lambda e: e.activation(out=P_[:], in_=pairs[kt % 2][:], func=AF.Exp, scale=0.125),
                                  reads=[bpSa, bpSb], writes=[bP_])
                            Sc.op("pe", lambda e: e.matmul(pOa[0:65, :], lhsT=Vaug[:, kt, g, :], rhs=P_[:, 0, :],
                                                           start=(kt == 0), stop=(kt == T - 1)),
                                  reads=[bVaug, bP_], writes=[bpOa], signal=False)
                            Sc.op("pe", lambda e: e.matmul(pOb[0:65, :], lhsT=Vaug[:, kt, g, :], rhs=P_[:, 1, :],
                                                           start=(kt == 0), stop=(kt == T - 1)),
                                  reads=[bVaug, bP_], writes=[bpOb])
                            yield
                        for j, (pO_, bpO_) in enumerate(((pOa, bpOa), (pOb, bpOb))):
                            hh = 2 * c + j
                            Sc.op("dve", lambda e: e.reciprocal(out=rs_[64:65, :], in_=pO_[64:65, :]),
                                  reads=[bpO_], writes=[brs_])
                            o_, bo_ = oa[j]
                            Sc.op("act", lambda e: e.copy(out=o_[0:64, :], in_=pO_[0:64, :]), reads=[bpO_], writes=[bo_])
                            pN, bpN = pb[j]
                            Sc.op("pe", lambda e: e.matmul(pN[0:64, :], lhsT=onesr[64:65, 0:64], rhs=rs_[64:65, :],
                                                           start=True, stop=True, tile_position=(64, 0)),
                                  reads=[bonesr, brs_], writes=[bpN])
                            ab, bab = aob[j]
                            Sc.op("dve", lambda e: e.tensor_tensor(out=ab[:], in0=o_[0:64, :], in1=pN[0:64, :], op=ALU.mult),
                                  reads=[bo_, bpN], writes=[bab])
                            Sc.dma(AO_s[hh, :, qb * 512:(qb + 1) * 512], ab[:], reads=[bab], writes=[bAO])

                ag = attn_gen()
                dgen = dn_gen()
                a_done = d_done = False
                RA = int(os.environ.get('KRA', '4'))
                if RA == 0:
                    for _ in dgen:
                        pass
                    d_done = True
                    RA = 100000000
                if RA < 100000:
                    bankset[:] = [4, 5]
                while not (a_done and d_done):
                    for _ in range(RA):
                        if a_done:
                            break
                        try:
                            next(ag)
                        except StopIteration:
                            a_done = True
                    if not d_done:
                        try:
                            next(dgen)
                        except StopIteration:
                            d_done = True
                bankset[:] = list(range(8))

        Sc.barrier()
        ph4 = ExitStack()
        with ph4:
            cg, bcg = sb(ph4, "cg", [128, 128 + D])
            Sc.dma(cg[:], cg_d, writes=[bcg])
            woA, bwoA = sb(ph4, "woA", [64, 8, D], BF16)
            woD, bwoD = sb(ph4, "woD", [128, 4, D], BF16)
            wdn, bwdn = sb(ph4, "wdn", [128, NFC, D], BF16)
            wpg, bwpg = sb(ph4, "wpg", [128, 8, D], BF16)
            wpl, bwpl = sb(ph4, "wpl", [128, 2, D], BF16)
            Sc.dma(woA[:], WO_s[0:512, :].rearrange("(h p) n -> p h n", p=64), reads=[bW], writes=[bwoA])
            Sc.dma(woD[:], WO_s[512:1024, :].rearrange("(c p) n -> p c n", p=128), reads=[bW], writes=[bwoD])
            for f0 in range(0, NFC, 6):
                f1 = min(NFC, f0 + 6)
                Sc.dma(wdn[:, f0:f1, :], WD_s[f0 * 128:f1 * 128, :].rearrange("(c p) n -> p c n", p=128),
                       reads=[bW], writes=[bwdn])
            Sc.dma(wpg[:], WPG_s.rearrange("(c p) n -> p c n", p=128), reads=[bW], writes=[bwpg])
            Sc.dma(wpl[:], WPL_s.rearrange("(c p) n -> p c n", p=128), reads=[bW], writes=[bwpl])
            wgs = [sb(ph4, "wgs%d" % i, [128, 8, 128], BF16) for i in range(4)]
            wus = [sb(ph4, "wus%d" % i, [128, 8, 128], BF16) for i in range(4)]
            hx = [sb(ph4, "hx%d" % i, [128, 4, D]) for i in range(1)]
            hT4, bhT4 = sb(ph4, "hT4", [128, 8, 512], BF16)
            hb4 = [sb(ph4, "hb4_%d" % i, [128, D], BF16) for i in range(2)]
            junk4, bjunk4 = sb(ph4, "junk4", [128, D], BF16)
            s4, bs4 = sb(ph4, "s4", [128, 16])
            actT, bactT = sb(ph4, "actT", [128, NFC, 512], BF16)
            aoT, baoT = sb(ph4, "aoT", [64, 8, 512], BF16)
            dnT, bdnT = sb(ph4, "dnT", [128, 4, 512], BF16)
            e4 = [sb(ph4, "e4_%d" % i, [128, 4, 128]) for i in range(6)]
            dnb = [sb(ph4, "dnb%d" % i, [128, 512], BF16) for i in range(2)]
            sg4 = [sb(ph4, "sg4_%d" % i, [128, 512]) for i in range(2)]
            pin, bpin = sb(ph4, "pin", [128, PLE])
            pinb, bpinb = sb(ph4, "pinb", [128, PLE], BF16)
            pT4, bpT4 = sb(ph4, "pT4", [128, 2, 512], BF16)
            yo = [sb(ph4, "yo%d" % i, [128, D]) for i in range(1)]

            def norm_to_T(src_ap, bsrc, tt, idx):
                Sc.op("pool", lambda e: e.memset(s4[:, 0:1], 0.0), writes=[bs4])
                Sc.op("act", lambda e: e.activation(out=junk4[:], in_=src_ap, func=AF.Square,
                                                    accum_out=s4[:, 0:1]), reads=[bsrc, bs4], writes=[bjunk4, bs4])
                rstd_from_ss(s4[:, 0:1], s4[:, 2:3], s4[:, 1:2], 1.0 / D, [bs4])
                h_, bh_ = hb4[idx % 2]
                Sc.op("dve", lambda e: e.tensor_scalar(out=h_[:], in0=src_ap, scalar1=s4[:, 2:3], scalar2=None,
                                                       op0=ALU.mult), reads=[bsrc, bs4], writes=[bh_])
                transpose_bf(h_, bh_, 8, hT4[:, :, tt * 128:(tt + 1) * 128], bhT4)

            for b in range(NB if PHMAX >= 4 else 0):
                H, bH = hx[0]
                for tt in range(4):
                    ti = 4 * b + tt
                    r0 = ti * 128
                    of_, bof_ = e4[3 * (tt % 2)]
                    ob_, bob_ = e4[3 * (tt % 2) + 1]
                    z_, bz_ = e4[3 * (tt % 2) + 2]
                    Sc.dma(of_[:], OF_s[0, r0:r0 + 128, :].rearrange("p (h d) -> p h d", h=4), reads=[bOF], writes=[bof_])
                    Sc.dma(ob_[:], OF_s[1, r0:r0 + 128, :].rearrange("p (h d) -> p h d", h=4), reads=[bOF], writes=[bob_])
                    Sc.dma(z_[:], DZ_s[r0:r0 + 128, :].rearrange("p (h d) -> p h d", h=4), reads=[bDN], writes=[bz_])
                    o_, bo_ = of_, bof_
                    Sc.op("pool", lambda e: e.tensor_tensor(out=o_[:], in0=of_[:], in1=ob_[:], op=ALU.add),
                          reads=[bof_, bob_], writes=[bo_])
                    sq_, bsq_ = ob_, bob_
                    Sc.op("pool", lambda e: e.tensor_tensor(out=sq_[:], in0=o_[:], in1=o_[:], op=ALU.mult),
                          reads=[bo_], writes=[bsq_])
                    Sc.op("dve", lambda e: e.tensor_reduce(out=s4[:, 4:8], in_=sq_[:], axis=AX.X, op=ALU.add),
                          reads=[bsq_], writes=[bs4])
                    rstd_from_ss(s4[:, 4:8], s4[:, 12:16], s4[:, 8:12], 1.0 / 128, [bs4])
                    Sc.op("dve", lambda e: e.tensor_tensor(out=o_[:], in0=o_[:],
                                                           in1=s4[:, 12:16].unsqueeze(2).broadcast_to([128, 4, 128]),
                                                           op=ALU.mult), reads=[bo_, bs4], writes=[bo_])
                    Sc.op("pool", lambda e: e.tensor_tensor(out=o_[:], in0=o_[:],
                                                            in1=cg[:, 0:128].unsqueeze(1).broadcast_to([128, 4, 128]),
                                                            op=ALU.mult), reads=[bo_, bcg], writes=[bo_])
                    zs_, bzs_ = z_, bz_
                    Sc.op("act", lambda e: e.activation(out=zs_[:], in_=z_[:], func=AF.Silu), reads=[bz_], writes=[bzs_])
                    db_, bdb_ = dnb[tt % 2]
                    Sc.op("dve", lambda e: e.tensor_tensor(out=db_[:].rearrange("p (h d) -> p h d", h=4), in0=o_[:],
                                                           in1=zs_[:], op=ALU.mult), reads=[bo_, bzs_], writes=[bdb_])
                    transpose_bf(db_, bdb_, 4, dnT[:, :, tt * 128:(tt + 1) * 128], bdnT, evac="dve")
                Sc.dma(aoT[:], AO_s[:, :, b * 512:(b + 1) * 512].rearrange("h p s -> p h s"), reads=[bAO], writes=[baoT])
                for tt in range(4):
                    ti = 4 * b + tt
                    Sc.dma(H[:, tt, :], x_d[ti * 128:(ti + 1) * 128, :], writes=[bH])
                for tt in range(4):
                    ts_ = slice(tt * 128, (tt + 1) * 128)
                    for nh in range(2):
                        ns = slice(nh * 512, (nh + 1) * 512)
                        pt, bpt = bank()
                        for h in range(8):
                            Sc.op("pe", lambda e, h=h: e.matmul(pt[:], lhsT=aoT[:, h, ts_], rhs=woA[:, h, ns],
                                                                start=(h == 0), stop=False),
                                  reads=[baoT, bwoA], writes=[bpt], signal=False)
                        for cc in range(4):
                            Sc.op("pe", lambda e, cc=cc: e.matmul(pt[:], lhsT=dnT[:, cc, ts_], rhs=woD[:, cc, ns],
                                                                  start=False, stop=(cc == 3)),
                                  reads=[bdnT, bwoD], writes=[bpt], signal=(cc == 3))
                        Sc.op("dve", lambda e: e.tensor_tensor(out=H[:, tt, ns], in0=H[:, tt, ns], in1=pt[:], op=ALU.add),
                              reads=[bH, bpt], writes=[bH])
                    norm_to_T(H[:, tt, :], bH, tt, tt)
                for fc in range(NFC):
                    wg_, bwg_ = wgs[fc % 4]
                    wu_, bwu_ = wus[fc % 4]
                    Sc.dma(wg_[:], WG_s[fc], reads=[bW], writes=[bwg_])
                    Sc.dma(wu_[:], WU_s[fc], reads=[bW], writes=[bwu_])
                    pg, bpg = bank()
                    pu, bpu = bank()
                    for kc in range(8):
                        Sc.op("pe", lambda e, kc=kc: e.matmul(pg[:], lhsT=wg_[:, kc, :], rhs=hT4[:, kc, :],
                                                             start=(kc == 0), stop=(kc == 7)),
                              reads=[bwg_, bhT4], writes=[bpg], signal=(kc == 7))
                    for kc in range(8):
                        Sc.op("pe", lambda e, kc=kc: e.matmul(pu[:], lhsT=wu_[:, kc, :], rhs=hT4[:, kc, :],
                                                             start=(kc == 0), stop=(kc == 7)),
                              reads=[bwu_, bhT4], writes=[bpu], signal=(kc == 7))
                    sg_, bsg_ = sg4[fc % 2]
                    Sc.op("act", lambda e: e.activation(out=sg_[:], in_=pg[:], func=AF.Silu), reads=[bpg], writes=[bsg_])
                    Sc.op("dve", lambda e: e.tensor_tensor(out=actT[:, fc, :], in0=sg_[:], in1=pu[:], op=ALU.mult),
                          reads=[bsg_, bpu], writes=[bactT])
                for tt in range(4):
                    ts_ = slice(tt * 128, (tt + 1) * 128)
                    for nh in range(2):
                        ns = slice(nh * 512, (nh + 1) * 512)
                        pt, bpt = bank()
                        for fc in range(NFC):
                            Sc.op("pe", lambda e, fc=fc: e.matmul(pt[:], lhsT=actT[:, fc, ts_], rhs=wdn[:, fc, ns],
                                                                  start=(fc == 0), stop=(fc == NFC - 1)),
                                  reads=[bactT, bwdn], writes=[bpt], signal=(fc == NFC - 1))
                        Sc.op("dve", lambda e: e.tensor_tensor(out=H[:, tt, ns], in0=H[:, tt, ns], in1=pt[:], op=ALU.add),
                              reads=[bH, bpt], writes=[bH])
                    norm_to_T(H[:, tt, :], bH, tt, tt)
                for tt in range(4):
                    ti = 4 * b + tt
                    Sc.dma(pin[:], p_d[ti * 128:(ti + 1) * 128, :], writes=[bpin])
                    Sc.op("pool", lambda e: e.tensor_copy(out=pinb[:], in_=pin[:]), reads=[bpin], writes=[bpinb])
                    transpose_bf(pinb, bpinb, 2, pT4[:, :, tt * 128:(tt + 1) * 128], bpT4)
                for tt in range(4):
                    ti = 4 * b + tt
                    ts_ = slice(tt * 128, (tt + 1) * 128)
                    for nh in range(2):
                        ns = slice(nh * 512, (nh + 1) * 512)
                        pg, bpg = bank()
                        pl, bpl = bank()
                        for kc in range(8):
                            Sc.op("pe", lambda e, kc=kc: e.matmul(pg[:], lhsT=hT4[:, kc, ts_], rhs=wpg[:, kc, ns],
                                                                  start=(kc == 0), stop=(kc == 7)),
                                  reads=[bhT4, bwpg], writes=[bpg], signal=(kc == 7))
                        for kc in range(2):
                            Sc.op("pe", lambda e, kc=kc: e.matmul(pl[:], lhsT=pT4[:, kc, ts_], rhs=wpl[:, kc, ns],
                                                                  start=(kc == 0), stop=(kc == 1)),
                                  reads=[bpT4, bwpl], writes=[bpl], signal=(kc == 1))
                        sg_, bsg_ = sg4[nh]
                        Sc.op("act", lambda e: e.activation(out=sg_[:], in_=pg[:], func=AF.Sigmoid),
                              reads=[bpg], writes=[bsg_])
                        Sc.op("dve", lambda e: e.tensor_tensor(out=sg_[:], in0=sg_[:], in1=pl[:], op=ALU.mult),
                              reads=[bsg_, bpl], writes=[bsg_])
                        Sc.op("pool", lambda e: e.tensor_tensor(out=H[:, tt, ns], in0=H[:, tt, ns], in1=sg_[:], op=ALU.add),
                              reads=[bH, bsg_], writes=[bH])
                    Sc.op("pool", lambda e: e.memset(s4[:, 0:1], 0.0), writes=[bs4])
                    Sc.op("act", lambda e: e.activation(out=junk4[:], in_=H[:, tt, :], func=AF.Square,
                                                        accum_out=s4[:, 0:1]), reads=[bH, bs4], writes=[bjunk4, bs4])
                    rstd_from_ss(s4[:, 0:1], s4[:, 2:3], s4[:, 1:2], 1.0 / D, [bs4])
                    y_, by_ = yo[0]
                    Sc.op("dve", lambda e: e.scalar_tensor_tensor(out=y_[:], in0=H[:, tt, :], scalar=s4[:, 2:3],
                                                                  in1=cg[:, 128:128 + D], op0=ALU.mult, op1=ALU.mult),
                          reads=[bH, bs4, bcg], writes=[by_])
                    Sc.dma(out_d[ti * 128:(ti + 1) * 128, :], y_[:], reads=[by_])
        Sc.finish()
        print("ops:", Sc.nops, "cnt:", Sc.cnt)
    return nc


def host_consts(S, norm_mix, norm_ffn, norm_ple, q_norm, k_norm, a_log, dt_bias, conv_w, dn_norm, norm_final):
    f = np.float32
    i = np.arange(128)[:, None]
    j = np.arange(128)[None, :]
    same = (i // 64) == (j // 64)
    cm = np.zeros((128, 10, 128), f)
    cm[:, M_ID] = (i == j)
    cm[:, M_LOWI] = same & (i >= j)
    cm[:, M_UPPI] = same & (i <= j)
    cm[:, M_LOWS] = same & (i > j)
    cm[:, M_UPPS] = same & (i < j)
    cm[:, M_BLK] = same
    cm[:, M_ONES] = 1.0
    cm[:, M_CI0] = (i < 64) & (j >= 0)
    cm[:, M_CI1] = (i >= 64) & (j >= 0)
    Rm = np.zeros((128, 128), f)
    for fo in range(128):
        idx = fo % 32
        if idx < 16:
            Rm[fo + 16, fo] = -1.0
        else:
            Rm[fo - 16, fo] = 1.0
    cm[:, M_RM] = Rm
    cv = np.zeros((128, V_END), f)
    cv[:, V_GMIX:V_GMIX + 8] = norm_mix.reshape(8, 128).T
    cv[:, V_GFFN:V_GFFN + 8] = norm_ffn.reshape(8, 128).T
    cv[:, V_GPLE:V_GPLE + 8] = norm_ple.reshape(8, 128).T
    cv[:, V_QNG] = np.tile(q_norm, 2)
    cv[:, V_KNG] = np.tile(k_norm, 2)
    cv[:, V_ALOG:V_ALOG + 8] = a_log[None, :]
    cv[:, V_DTB:V_DTB + 8] = dt_bias[None, :]
    cv[:, V_CONV:V_CONV + 60] = conv_w.reshape(5, 12, 128).transpose(2, 1, 0).reshape(128, 60)
    cg = np.zeros((128, 128 + D), f)
    cg[:, 0:128] = dn_norm[None, :]
    cg[:, 128:] = norm_final[None, :]
    t = np.arange(S)
    row = (t // 64).astype(np.float64)
    col = (t % 64).astype(np.float64)
    inv_freq = (10000.0 ** (-np.arange(0, 32, 2, dtype=np.float32) / np.float32(32))).astype(np.float32)
    cosT = np.zeros((128, S), f)
    sinT = np.zeros((128, S), f)
    for pp in range(128):
        dd = pp % 64
        pos = row if dd < 32 else col
        ang = (pos.astype(np.float32) * inv_freq[dd % 16]).astype(np.float32)
        cosT[pp] = np.cos(ang)
        sinT[pp] = np.sin(ang)
    return cm, cv, cg, cosT, sinT


_NC_CACHE = {}


def run(S, ncores, x, p, norm_mix, w_in, conv_w, q_norm, k_norm, a_log, dt_bias, dn_norm, w_out,
        norm_ffn, w_gate, w_up, w_down, norm_ple, w_ple_gate, w_ple, norm_final):
    A = lambda a: np.ascontiguousarray(np.asarray(a, dtype=np.float32))
    cm, cv, cg, cosT, sinT = host_consts(S, A(norm_mix)[0], A(norm_ffn)[0], A(norm_ple)[0], A(q_norm)[0], A(k_norm)[0],
                                         A(a_log)[0], A(dt_bias)[0], A(conv_w)[0], A(dn_norm)[0], A(norm_final))
    if S not in _NC_CACHE:
        _NC_CACHE[S] = build(S)
    nc = _NC_CACHE[S]
    x = A(x)
    p = A(p)
    shared = {"w_in": A(w_in)[0], "w_out": A(w_out)[0], "w_gate": A(w_gate)[0], "w_up": A(w_up)[0],
              "w_down": A(w_down)[0], "w_pg": A(w_ple_gate)[0], "w_ple": A(w_ple)[0],
              "cm": cm, "cv": cv, "cg": cg, "cosT": cosT, "sinT": sinT}
    in_maps = []
    for c in range(ncores):
        m = dict(shared)
        m["x"] = np.ascontiguousarray(x[c])
        m["p"] = np.ascontiguousarray(p[0, c])
        in_maps.append(m)
    res = run_bass_kernel_spmd(nc, in_maps, core_ids=list(range(ncores)))
    return np.stack([np.asarray(r["out"], dtype=np.float32) for r in res.results], axis=0)


def kernel(**inputs):
    x = inputs["x"]
    return run(x.shape[1], x.shape[0], **inputs)
```

```python
import numpy as np
import concourse.bass as bass
import concourse.mybir as mybir
from concourse.bass_utils import run_bass_kernel_spmd
from contextlib import ExitStack
from itertools import zip_longest
import os
PHMAX = int(os.environ.get('KPH', '9'))

F32 = mybir.dt.float32
BF16 = mybir.dt.bfloat16
ALU = mybir.AluOpType
AF = mybir.ActivationFunctionType
AX = mybir.AxisListType

D = 1024
INW = 2832
FF = 2816
NFC = FF // 128
PLE = 256
EPS = 1e-6


class Buf:
    __slots__ = ("name", "w", "rs")

    def __init__(self, name):
        self.name = name
        self.w = None
        self.rs = []


class MBuf(Buf):
    __slots__ = ("ws",)

    def __init__(self, name):
        Buf.__init__(self, name)
        self.ws = []


def _compact(evl):
    best = {}
    for r in evl:
        if r[0] not in best or best[r[0]][2] < r[2]:
            best[r[0]] = r
    return list(best.values())


class Sched:
    ND = 24

    def __init__(self, nc, stack):
        self.nc = nc
        self.E = {"pe": nc.tensor, "act": nc.scalar, "dve": nc.vector,
                  "pool": nc.gpsimd, "sp": nc.sync}
        self.sem = {}
        self.cnt = {}
        for k in ("pe", "act", "dve", "pool"):
            self.sem[k] = stack.enter_context(nc.semaphore("sem_" + k))
            self.cnt[k] = 0
        self.dsem = [stack.enter_context(nc.semaphore("dsem%d" % i)) for i in range(self.ND)]
        self.dcnt = [0] * self.ND
        self.dnext = 0
        self.seen = {k: {} for k in self.E}
        self.nops = {k: 0 for k in self.E}

    def _wait(self, ek, ev):
        key, sem, val, src = ev
        if self.seen[ek].get(key, 0) >= val:
            return
        self.E[ek].wait_ge(sem, val)
        self.seen[ek][key] = val

    def _note(self, ev, reads, writes):
        for b in reads:
            b.rs.append(ev)
            if len(b.rs) > 16:
                best = {}
                for r in b.rs:
                    if r[0] not in best or best[r[0]][2] < r[2]:
                        best[r[0]] = r
                b.rs = list(best.values())
        for b in writes:
            if isinstance(b, MBuf):
                b.ws.append(ev)
                if len(b.ws) > 40:
                    b.ws = _compact(b.ws)
            else:
                b.w = ev
                b.rs = []

    def op(self, ek, fn, reads=(), writes=(), signal=True):
        evs = []
        for b in reads:
            if isinstance(b, MBuf):
                evs.extend(b.ws)
            elif b.w is not None:
                evs.append(b.w)
        for b in writes:
            if b.w is not None and b.w[3] != ek:
                evs.append(b.w)
            for r in b.rs:
                if r[3] != ek:
                    evs.append(r)
        for e in evs:
            if ek == "pe" and e[3] == "pe":
                continue
            self._wait(ek, e)
        inst = fn(self.E[ek])
        self.nops[ek] += 1
        if signal:
            self.cnt[ek] += 1
            inst.then_inc(self.sem[ek], 1)
            ev = (ek, self.sem[ek], self.cnt[ek], ek)
        else:
            ev = (ek, self.sem[ek], self.cnt[ek] + 1, ek)
        self._note(ev, reads, writes)
        return inst

    def dma(self, out, in_, reads=(), writes=(), q="sp", **kw):
        k = self.dnext
        self.dnext = (self.dnext + 1) % self.ND
        key = "d%d" % k
        if self.dcnt[k] > 0:
            self._wait(q, (key, self.dsem[k], self.dcnt[k], "dma"))
        evs = []
        for b in reads:
            if isinstance(b, MBuf):
                evs.extend(b.ws)
            elif b.w is not None:
                evs.append(b.w)
        for b in writes:
            if b.w is not None:
                evs.append(b.w)
            evs.extend(b.rs)
        for e in evs:
            self._wait(q, e)
        inst = self.E[q].dma_start(out=out, in_=in_, **kw)
        self.dcnt[k] += 16
        inst.then_inc(self.dsem[k], 16)
        self.nops[q] += 1
        ev = (key, self.dsem[k], self.dcnt[k], "dma")
        self._note(ev, reads, writes)
        return inst

    def barrier(self):
        evs = [(k, self.sem[k], self.cnt[k], k) for k in ("pe", "act", "dve", "pool") if self.cnt[k] > 0]
        evs += [("d%d" % k, self.dsem[k], self.dcnt[k], "dma") for k in range(self.ND) if self.dcnt[k] > 0]
        for ek in ("sp", "pe", "act", "dve", "pool"):
            for e in evs:
                self._wait(ek, e)

    def finish(self):
        for k in range(self.ND):
            if self.dcnt[k] > 0:
                self._wait("sp", ("d%d" % k, self.dsem[k], self.dcnt[k], "dma"))


M_ID, M_LOWI, M_UPPI, M_LOWS, M_UPPS, M_BLK, M_ONES, M_CI0, M_CI1, M_RM = range(10)
V_GMIX, V_GFFN, V_GPLE, V_QNG, V_KNG, V_ALOG, V_DTB, V_CONV, V_END = 0, 8, 16, 24, 25, 26, 34, 42, 102


def build(S):
    T = S // 128
    NB = S // 512
    nc = bass.Bass("TRN2", target_bir_lowering=False)

    def din(name, shape, dt=F32):
        return nc.dram_tensor(name, list(shape), dt, kind="ExternalInput").ap()

    def dscr(name, shape, dt=F32):
        return nc.dram_tensor(name, list(shape), dt).ap()

    x_d = din("x", [S, D])
    p_d = din("p", [S, PLE])
    win_d = din("w_in", [D, INW])
    wout_d = din("w_out", [D, D])
    wg_d = din("w_gate", [D, FF])
    wu_d = din("w_up", [D, FF])
    wd_d = din("w_down", [FF, D])
    wpg_d = din("w_pg", [D, D])
    wple_d = din("w_ple", [PLE, D])
    cm_d = din("cm", [128, 10, 128])
    cv_d = din("cv", [128, V_END])
    cg_d = din("cg", [128, 128 + D])
    cos_d = din("cosT", [128, S])
    sin_d = din("sinT", [128, S])
    out_d = nc.dram_tensor("out", [S, D], F32, kind="ExternalOutput").ap()

    QT_s = dscr("QT_s", [4, 128, S], BF16)
    KT_s = dscr("KT_s", [2, 128, S], BF16)
    V_s = dscr("V_s", [S, 128], BF16)
    DQT_s = dscr("DQT_s", [4, 128, S])
    DKT_s = dscr("DKT_s", [4, 128, S])
    DK_s = dscr("DK_s", [S, 4, 128])
    DV_s = dscr("DV_s", [S, 4, 128])
    GB_s = dscr("GB_s", [S, 16])
    DZ_s = dscr("DZ_s", [S, 512])
    AO_s = dscr("AO_s", [8, 64, S], BF16)
    OF_s = dscr("OF_s", [2, S, 512])
    WO_s = dscr("WO_s", [D, D], BF16)
    WG_s = dscr("WG_s", [NFC, 128, 8, 128], BF16)
    WU_s = dscr("WU_s", [NFC, 128, 8, 128], BF16)
    WD_s = dscr("WD_s", [FF, D], BF16)
    WPG_s = dscr("WPG_s", [D, D], BF16)
    WPL_s = dscr("WPL_s", [PLE, D], BF16)
    bQT, bKT, bV = MBuf("QT_s"), MBuf("KT_s"), MBuf("V_s")
    bDN = MBuf("DN_s")
    bAO, bOF = MBuf("AO_s"), MBuf("OF_s")
    bW = MBuf("W_s")

    top = ExitStack()
    with top:
        Sc = Sched(nc, top)

        def sb(stack, name, shape, dt=F32):
            return stack.enter_context(nc.sbuf_tensor("sb_" + name, list(shape), dt)), Buf(name)

        pb = []
        pairs = []
        for i in range(4):
            pr_ = top.enter_context(nc.psum_tensor("pp%d" % i, [128, 2, 512], F32))
            pairs.append(pr_)
            for j in range(2):
                pb.append((pr_[:, j, :], Buf("pb%d" % (2 * i + j))))
        pbn = [0]
        bankset = list(range(8))

        def bank():
            pbn[0] = (pbn[0] + 1) % len(bankset)
            return pb[bankset[pbn[0]]]

        cm, bcm = sb(top, "cm", [128, 10, 128])
        cv, bcv = sb(top, "cv", [128, V_END])
        idb, bidb = sb(top, "idb", [128, 128], BF16)
        epst, bepst = sb(top, "epst", [128, 2])
        nexpA, bnexpA = sb(top, "nexpA", [128, 8])
        Sc.dma(cm[:], cm_d, writes=[bcm])
        Sc.dma(cv[:], cv_d, writes=[bcv])
        Sc.op("dve", lambda e: e.tensor_copy(out=idb[:], in_=cm[:, M_ID, :]), reads=[bcm], writes=[bidb])
        Sc.op("pool", lambda e: e.memset(epst[:, 0:1], EPS), writes=[bepst])
        Sc.op("pool", lambda e: e.memset(epst[:, 1:2], 1.0), writes=[bepst])
        Sc.op("act", lambda e: e.activation(out=nexpA[:], in_=cv[:, V_ALOG:V_ALOG + 8], func=AF.Exp),
              reads=[bcv], writes=[bnexpA])
        Sc.op("dve", lambda e: e.tensor_scalar(out=nexpA[:], in0=nexpA[:], scalar1=-1.0, scalar2=None, op0=ALU.mult),
              reads=[bnexpA], writes=[bnexpA])
        ident = cm[:, M_ID, :]

        def rstd_from_ss(ss_ap, out_ap, tmp_ap, scale, bufs):
            Sc.op("act", lambda e: e.activation(out=tmp_ap, in_=ss_ap, func=AF.Ln, scale=scale, bias=epst[:, 0:1]),
                  reads=bufs + [bepst], writes=bufs)
            Sc.op("act", lambda e: e.activation(out=out_ap, in_=tmp_ap, func=AF.Exp, scale=-0.5),
                  reads=bufs, writes=bufs)

        def transpose_bf(src_tile, bsrc, nblk, dst_ap3, bdst, evac="act"):
            pt, bpt = bank()
            ptb = pt[:].bitcast(BF16)
            for k in range(nblk):
                Sc.op("pe", lambda e, k=k: e.transpose(out=ptb[:, k * 128:(k + 1) * 128],
                                                         in_=src_tile[:, k * 128:(k + 1) * 128], identity=idb[:]),
                      reads=[bsrc, bidb], writes=[bpt], signal=(k == nblk - 1))
            src3 = ptb[:, 0:nblk * 128].rearrange("p (k t) -> p k t", k=nblk)
            if evac == "act":
                Sc.op("act", lambda e: e.copy(out=dst_ap3, in_=src3), reads=[bpt], writes=[bdst])
            else:
                Sc.op("dve", lambda e: e.tensor_copy(out=dst_ap3, in_=src3), reads=[bpt], writes=[bdst])

        ph1 = ExitStack()
        with ph1:
            win, bwin = sb(ph1, "win", [128, 8, INW], BF16)
            wkd, bwkd = sb(ph1, "wkd", [128, 8, 2, 128], BF16)
            ph0 = ExitStack()
            ph0.__enter__()
            stg = [sb(ph0, "stg%d" % i, [128, INW]) for i in range(2)]
            stb = [sb(ph0, "stb%d" % i, [128, FF], BF16) for i in range(2)]
            for kc in range(8):
                st_, bst = stg[kc % 2]
                Sc.dma(st_[:], win_d[kc * 128:(kc + 1) * 128, :], writes=[bst])
                Sc.op("dve", lambda e, kc=kc, st_=st_: e.tensor_scalar(
                    out=win[:, kc, :], in0=st_[:], scalar1=cv[:, V_GMIX + kc:V_GMIX + kc + 1], scalar2=None,
                    op0=ALU.mult), reads=[bst, bcv], writes=[bwin])
            for g in range(2):
                for hf in range(2):
                    Sc.op("pool", lambda e, g=g, hf=hf: e.tensor_copy(
                        out=wkd[:, :, g, hf * 64:(hf + 1) * 64], in_=win[:, :, 512 + 64 * g:512 + 64 * g + 64]),
                        reads=[bwin], writes=[bwkd])
            cnt = [0]

            def conv_w(src_rows, ncols, gain_col, dst_ap):
                i = cnt[0] % 2
                cnt[0] += 1
                st_, bst = stg[i]
                sb_, bsb = stb[i]
                Sc.dma(st_[:, 0:ncols], src_rows, writes=[bst])
                if gain_col is None:
                    Sc.op("pool", lambda e: e.tensor_copy(out=sb_[:, 0:ncols], in_=st_[:, 0:ncols]),
                          reads=[bst], writes=[bsb])
                else:
                    Sc.op("dve", lambda e: e.tensor_scalar(out=sb_[:, 0:ncols], in0=st_[:, 0:ncols],
                                                           scalar1=cv[:, gain_col:gain_col + 1], scalar2=None,
                                                           op0=ALU.mult), reads=[bst, bcv], writes=[bsb])
                return sb_, bsb

            for kc in range(8):
                sb_, bsb = conv_w(wout_d[kc * 128:(kc + 1) * 128, :], D, None, None)
                Sc.dma(WO_s[kc * 128:(kc + 1) * 128, :], sb_[:, 0:D], reads=[bsb], writes=[bW])
            for kc in range(8):
                sb_, bsb = conv_w(wpg_d[kc * 128:(kc + 1) * 128, :], D, V_GPLE + kc, None)
                Sc.dma(WPG_s[kc * 128:(kc + 1) * 128, :], sb_[:, 0:D], reads=[bsb], writes=[bW])
            for kc in range(2):
                sb_, bsb = conv_w(wple_d[kc * 128:(kc + 1) * 128, :], D, None, None)
                Sc.dma(WPL_s[kc * 128:(kc + 1) * 128, :], sb_[:, 0:D], reads=[bsb], writes=[bW])
            for kc in range(NFC):
                sb_, bsb = conv_w(wd_d[kc * 128:(kc + 1) * 128, :], D, None, None)
                Sc.dma(WD_s[kc * 128:(kc + 1) * 128, :], sb_[:, 0:D], reads=[bsb], writes=[bW])
            for (src, dst) in ((wg_d, WG_s), (wu_d, WU_s)):
                for kc in range(8):
                    sb_, bsb = conv_w(src[kc * 128:(kc + 1) * 128, :], FF, V_GFFN + kc, None)
                    Sc.dma(dst[:, :, kc, :].rearrange("f p j -> p f j"),
                           sb_[:, 0:FF].rearrange("p (f j) -> p f j", f=NFC), reads=[bsb], writes=[bW])

            Sc.barrier()
            ph0.close()
            xt = [sb(ph1, "xt%d" % i, [128, D]) for i in range(2)]
            hnb = [sb(ph1, "hnb%d" % i, [128, D], BF16) for i in range(2)]
            junk, bjunk = sb(ph1, "junk", [128, D], BF16)
            st4, bst4 = sb(ph1, "st4", [128, 8])
            hnT = [sb(ph1, "hnT%d" % i, [128, 8, 512], BF16) for i in range(2)]
            pre = [sb(ph1, "pre%d" % i, [128, 12, 516]) for i in range(2)]
            cosb = [sb(ph1, "cosb%d" % i, [128, 512]) for i in range(2)]
            sinb = [sb(ph1, "sinb%d" % i, [128, 512]) for i in range(2)]
            wk = [sb(ph1, "wk%d" % i, [128, 512]) for i in range(8)]
            wkn = [0]

            def work():
                i = wkn[0]
                wkn[0] = (i + 1) % 8
                return wk[i]
            qkout = [sb(ph1, "qko%d" % i, [128, 512], BF16) for i in range(2)]
            vbt = [sb(ph1, "vbt%d" % i, [128, 128], BF16) for i in range(2)]
            zt = [sb(ph1, "zt%d" % i, [128, 512]) for i in range(2)]
            gbw, bgbw = sb(ph1, "gbw", [128, 64])
            gbo = [sb(ph1, "gbo%d" % i, [128, 16]) for i in range(2)]
            tkm = [sb(ph1, "tkm%d" % i, [128, 4, 128]) for i in range(2)]

            def qk_post(pt, bpt, kind, c, b):
                xs, bxs = work()
                Sc.op("act", lambda e: e.copy(out=xs[:], in_=pt[:]), reads=[bpt], writes=[bxs])
                sq, bsq = work()
                Sc.op("pool", lambda e: e.tensor_tensor(out=sq[:], in0=xs[:], in1=xs[:], op=ALU.mult),
                      reads=[bxs], writes=[bsq])
                p2, bp2 = bank()
                Sc.op("pe", lambda e: e.matmul(p2[:], lhsT=cm[:, M_BLK, :], rhs=sq[:], start=True, stop=True),
                      reads=[bcm, bsq], writes=[bp2])
                rn, brn = work()
                Sc.op("act", lambda e: e.activation(out=rn[:], in_=p2[:], func=AF.Ln, scale=1.0 / 64,
                                                    bias=epst[:, 0:1]), reads=[bp2, bepst], writes=[brn])
                Sc.op("act", lambda e: e.activation(out=rn[:], in_=rn[:], func=AF.Exp, scale=-0.5),
                      reads=[brn], writes=[brn])
                gcol = V_QNG if kind == "q" else V_KNG
                xn, bxn = work()
                Sc.op("dve", lambda e: e.scalar_tensor_tensor(out=xn[:], in0=xs[:], scalar=cv[:, gcol:gcol + 1],
                                                              in1=rn[:], op0=ALU.mult, op1=ALU.mult),
                      reads=[bxs, bcv, brn], writes=[bxn])
                p3, bp3 = bank()
                Sc.op("pe", lambda e: e.matmul(p3[:], lhsT=cm[:, M_RM, :], rhs=xn[:], start=True, stop=True),
                      reads=[bcm, bxn], writes=[bp3])
                t1, bt1 = work()
                Sc.op("pool", lambda e: e.tensor_tensor(out=t1[:], in0=xn[:], in1=cosb[b % 2][0][:], op=ALU.mult),
                      reads=[bxn, cosb[b % 2][1]], writes=[bt1])
                t2, bt2 = work()
                Sc.op("dve", lambda e: e.tensor_tensor(out=t2[:], in0=p3[:], in1=sinb[b % 2][0][:], op=ALU.mult),
                      reads=[bp3, sinb[b % 2][1]], writes=[bt2])
                qo, bqo = qkout[c % 2]
                Sc.op("dve", lambda e: e.tensor_tensor(out=qo[:], in0=t1[:], in1=t2[:], op=ALU.add),
                      reads=[bt1, bt2], writes=[bqo])
                if kind == "q":
                    Sc.dma(QT_s[c, :, b * 512:(b + 1) * 512], qo[:], reads=[bqo], writes=[bQT])
                else:
                    Sc.dma(KT_s[c, :, b * 512:(b + 1) * 512], qo[:], reads=[bqo], writes=[bKT])

            def dn_post(b):
                pr, bpr = pre[b % 2]
                for ci in range(12):
                    eng = "dve"
                    acc, bacc = work()
                    Sc.op(eng, lambda e: e.tensor_scalar(out=acc[:], in0=pr[:, ci, 0:512],
                                                         scalar1=cv[:, V_CONV + ci * 5:V_CONV + ci * 5 + 1],
                                                         scalar2=None, op0=ALU.mult),
                          reads=[bpr, bcv], writes=[bacc])
                    for tap in range(1, 5):
                        Sc.op(eng, lambda e, tap=tap: e.scalar_tensor_tensor(
                            out=acc[:], in0=pr[:, ci, tap:tap + 512],
                            scalar=cv[:, V_CONV + ci * 5 + tap:V_CONV + ci * 5 + tap + 1],
                            in1=acc[:], op0=ALU.mult, op1=ALU.add), reads=[bpr, bcv, bacc], writes=[bacc])
                    s, bs = work()
                    Sc.op("act", lambda e: e.activation(out=s[:], in_=acc[:], func=AF.Silu), reads=[bacc], writes=[bs])
                    h = ci % 4
                    if ci < 8:
                        sq, bsq = work()
                        Sc.op("pool", lambda e: e.tensor_tensor(out=sq[:], in0=s[:], in1=s[:], op=ALU.mult),
                              reads=[bs], writes=[bsq])
                        p2, bp2 = bank()
                        Sc.op("pe", lambda e: e.matmul(p2[:], lhsT=cm[:, M_ONES, :], rhs=sq[:], start=True, stop=True),
                              reads=[bcm, bsq], writes=[bp2])
                        rn, brn = work()
                        Sc.op("act", lambda e: e.activation(out=rn[:], in_=p2[:], func=AF.Ln, scale=1.0,
                                                            bias=epst[:, 0:1]), reads=[bp2, bepst], writes=[brn])
                        Sc.op("act", lambda e: e.activation(out=rn[:], in_=rn[:], func=AF.Exp, scale=-0.5),
                              reads=[brn], writes=[brn])
                        o, bo = work()
                        sc_ = (128.0 ** -0.5) if ci < 4 else 1.0
                        Sc.op("dve", lambda e: e.scalar_tensor_tensor(out=o[:], in0=s[:], scalar=sc_, in1=rn[:],
                                                                      op0=ALU.mult, op1=ALU.mult),
                              reads=[bs, brn], writes=[bo])
                        dst = DQT_s if ci < 4 else DKT_s
                        Sc.dma(dst[h, :, b * 512:(b + 1) * 512], o[:], reads=[bo], writes=[bDN])
                    else:
                        o, bo = s, bs
                    if ci >= 4:
                        pt, bpt = bank()
                        for tt in range(4):
                            Sc.op("pe", lambda e, tt=tt: e.transpose(out=pt[:, tt * 128:(tt + 1) * 128],
                                                                     in_=o[:, tt * 128:(tt + 1) * 128],
                                                                     identity=ident),
                                  reads=[bo, bcm], writes=[bpt], signal=(tt == 3))
                        tk, btk = tkm[ci % 2]
                        Sc.op("act", lambda e: e.copy(out=tk[:], in_=pt[:].rearrange("p (t d) -> p t d", t=4)),
                              reads=[bpt], writes=[btk])
                        dst = DK_s if ci < 8 else DV_s
                        Sc.dma(dst[b * 512:(b + 1) * 512, h, :].rearrange("(t p) d -> p t d", p=128), tk[:],
                               reads=[btk], writes=[bDN])

            for b in range(NB if PHMAX >= 1 else 0):
                hT, bhT = hnT[b % 2]
                Sc.dma(cosb[b % 2][0][:], cos_d[:, b * 512:(b + 1) * 512], writes=[cosb[b % 2][1]])
                Sc.dma(sinb[b % 2][0][:], sin_d[:, b * 512:(b + 1) * 512], writes=[sinb[b % 2][1]])
                for tt in range(4):
                    ti = 4 * b + tt
                    x_, bx_ = xt[ti % 2]
                    h_, bh_ = hnb[ti % 2]
                    Sc.dma(x_[:], x_d[ti * 128:(ti + 1) * 128, :], writes=[bx_])
                    Sc.op("pool", lambda e: e.memset(st4[:, 0:1], 0.0), writes=[bst4])
                    Sc.op("act", lambda e: e.activation(out=junk[:], in_=x_[:], func=AF.Square,
                                                        accum_out=st4[:, 0:1]), reads=[bx_, bst4], writes=[bjunk, bst4])
                    rstd_from_ss(st4[:, 0:1], st4[:, 2:3], st4[:, 1:2], 1.0 / D, [bst4])
                    Sc.op("dve", lambda e: e.tensor_scalar(out=h_[:], in0=x_[:], scalar1=st4[:, 2:3], scalar2=None,
                                                           op0=ALU.mult), reads=[bx_, bst4], writes=[bh_])
                    transpose_bf(h_, bh_, 8, hT[:, :, tt * 128:(tt + 1) * 128], bhT)
                chunks = [("q", c, win, lambda kc, c=c: win[:, kc, c * 128:(c + 1) * 128]) for c in range(4)]
                chunks += [("k", g, wkd, lambda kc, g=g: wkd[:, kc, g, :]) for g in range(2)]
                chunks += [("d", ci, win, lambda kc, ci=ci: win[:, kc, 768 + ci * 128:768 + (ci + 1) * 128])
                           for ci in range(12)]
                for kind, c, _, wsel in chunks:
                    pt, bpt = bank()
                    for kc in range(8):
                        Sc.op("pe", lambda e, kc=kc: e.matmul(pt[:], lhsT=wsel(kc), rhs=hT[:, kc, :],
                                                             start=(kc == 0), stop=(kc == 7)),
                              reads=[bwin, bwkd, bhT], writes=[bpt], signal=(kc == 7))
                    if kind == "d":
                        Sc.op("act", lambda e: e.copy(out=pre[b % 2][0][:, c, 2:514], in_=pt[:]),
                              reads=[bpt], writes=[pre[b % 2][1]])
                    else:
                        qk_post(pt, bpt, kind, c, b)
                for tt in range(4):
                    ti = 4 * b + tt
                    pt, bpt = bank()
                    for kc in range(8):
                        Sc.op("pe", lambda e, kc=kc: e.matmul(pt[:, 0:128], lhsT=hT[:, kc, tt * 128:(tt + 1) * 128],
                                                             rhs=win[:, kc, 640:768], start=(kc == 0), stop=(kc == 7)),
                              reads=[bwin, bhT], writes=[bpt], signal=False)
                    for kc in range(8):
                        Sc.op("pe", lambda e, kc=kc: e.matmul(pt[:, 128:144], lhsT=hT[:, kc, tt * 128:(tt + 1) * 128],
                                                             rhs=win[:, kc, 2816:2832], start=(kc == 0), stop=(kc == 7)),
                              reads=[bwin, bhT], writes=[bpt], signal=(kc == 7))
                    vb, bvb = vbt[ti % 2]
                    Sc.op("act", lambda e: e.copy(out=vb[:], in_=pt[:, 0:128]), reads=[bpt], writes=[bvb])
                    Sc.dma(V_s[ti * 128:(ti + 1) * 128, :], vb[:], reads=[bvb], writes=[bV])
                    go, bgo = gbo[ti % 2]
                    Sc.op("act", lambda e: e.activation(out=gbw[:, 0:8], in_=pt[:, 128:136], func=AF.Exp, scale=-1.0),
                          reads=[bpt], writes=[bgbw])
                    Sc.op("dve", lambda e: e.tensor_scalar(out=gbw[:, 0:8], in0=gbw[:, 0:8], scalar1=1.0, scalar2=None,
                                                           op0=ALU.add), reads=[bgbw], writes=[bgbw])
                    Sc.op("dve", lambda e: e.reciprocal(out=go[:, 8:16], in_=gbw[:, 0:8]), reads=[bgbw], writes=[bgo])
                    Sc.op("dve", lambda e: e.tensor_tensor(out=gbw[:, 8:16], in0=pt[:, 136:144],
                                                           in1=cv[:, V_DTB:V_DTB + 8], op=ALU.add),
                          reads=[bpt, bcv], writes=[bgbw])
                    Sc.op("act", lambda e: e.activation(out=gbw[:, 16:24], in_=gbw[:, 8:16], func=AF.Exp),
                          reads=[bgbw], writes=[bgbw])
                    Sc.op("act", lambda e: e.activation(out=gbw[:, 24:32], in_=gbw[:, 16:24], func=AF.Ln,
                                                        bias=epst[:, 1:2]), reads=[bgbw, bepst], writes=[bgbw])
                    Sc.op("dve", lambda e: e.tensor_tensor(out=go[:, 0:8], in0=gbw[:, 24:32], in1=nexpA[:], op=ALU.mult),
                          reads=[bgbw, bnexpA], writes=[bgo])
                    Sc.dma(GB_s[ti * 128:(ti + 1) * 128, :], go[:], reads=[bgo], writes=[bDN])
                    pz, bpz = bank()
                    for kc in range(8):
                        Sc.op("pe", lambda e, kc=kc: e.matmul(pz[:], lhsT=hT[:, kc, tt * 128:(tt + 1) * 128],
                                                             rhs=win[:, kc, 2304:2816], start=(kc == 0), stop=(kc == 7)),
                              reads=[bwin, bhT], writes=[bpz], signal=(kc == 7))
                    z_, bz_ = zt[ti % 2]
                    Sc.op("act", lambda e: e.copy(out=z_[:], in_=pz[:]), reads=[bpz], writes=[bz_])
                    Sc.dma(DZ_s[ti * 128:(ti + 1) * 128, :], z_[:], reads=[bz_], writes=[bDN])
                pr, bpr = pre[b % 2]
                if b == 0:
                    Sc.op("pool", lambda e: e.memset(pr[:, :, 0:2], 0.0), writes=[bpr])
                else:
                    pp, bpp = pre[(b - 1) % 2]
                    Sc.op("pool", lambda e: e.tensor_copy(out=pr[:, :, 0:2], in_=pp[:, :, 512:514]),
                          reads=[bpp], writes=[bpr])
                    Sc.op("pool", lambda e: e.tensor_copy(out=pp[:, :, 514:516], in_=pr[:, :, 2:4]),
                          reads=[bpr], writes=[bpp])
                    dn_post(b - 1)
                if b == NB - 1:
                    Sc.op("pool", lambda e: e.memset(pr[:, :, 514:516], 0.0), writes=[bpr])
                    dn_post(b)

        Sc.barrier()
        ph2 = ExitStack()
        with ph2:
            NW = 13
            dw = [[sb(ph2, "dw%d_%d" % (d, i), [128, 4, 128]) for i in range(NW)] for d in range(2)]
            dinp = [[[sb(ph2, "di%d_%d_%d" % (d, q, i), [128, 4, 128]) for i in range(4)] for q in range(2)]
                    for d in range(2)]
            Sst = [sb(ph2, "Sst%d" % d, [128, 4, 128]) for d in range(2)]
            dsm = [[sb(ph2, "dsm%d_%d" % (d, i), [128, 32]) for i in range(2)] for d in range(2)]
            for d in range(2):
                Sc.op("pool", lambda e, d=d: e.memset(Sst[d][0][:], 0.0), writes=[Sst[d][1]])

            def bc_h(ap_h):
                return ap_h.unsqueeze(2).broadcast_to([128, 4, 128])

            def bc_m(ap_m):
                return ap_m.unsqueeze(1).broadcast_to([128, 4, 128])

            def dn_unit(t, d, step):
                W_ = dw[d]
                wi = [0]

                def wt():
                    r = W_[wi[0]]
                    wi[0] += 1
                    return r
                s0 = t * 128
                (kT, bkT), (qT, bqT), (ktok, bktok), (vtok, bvtok) = dinp[d][step % 2]
                sm, bsm = dsm[d][step % 2]
                S_, bS_ = Sst[d]
                Sc.dma(kT[:], DKT_s[:, :, s0:s0 + 128].rearrange("h p s -> p h s"), reads=[bDN], writes=[bkT])
                Sc.dma(qT[:], DQT_s[:, :, s0:s0 + 128].rearrange("h p s -> p h s"), reads=[bDN], writes=[bqT])
                Sc.dma(ktok[:], DK_s[s0:s0 + 128, :, :], reads=[bDN], writes=[bktok])
                Sc.dma(vtok[:], DV_s[s0:s0 + 128, :, :], reads=[bDN], writes=[bvtok])
                Sc.dma(sm[:, 0:16], GB_s[s0:s0 + 128, :], reads=[bDN], writes=[bsm])
                g_ap = sm[:, 4 * d:4 * d + 4]
                beta_ap = sm[:, 8 + 4 * d:8 + 4 * d + 4]
                tri = M_UPPI if d == 0 else M_LOWI
                m_incl = M_LOWI if d == 0 else M_UPPI
                m_inclT = M_UPPI if d == 0 else M_LOWI
                m_str = M_LOWS if d == 0 else M_UPPS
                yield
                pa, bpa = bank()
                for j, mi in enumerate((tri, M_BLK, M_CI0, M_CI1)):
                    Sc.op("pe", lambda e, j=j, mi=mi: e.matmul(pa[:, 16 * j:16 * j + 16], lhsT=cm[:, mi, :], rhs=sm[:, 0:16],
                                                               start=True, stop=True),
                          reads=[bcm, bsm], writes=[bpa], signal=(j == 3))
                Sc.op("dve", lambda e: e.tensor_copy(
                    out=sm[:, 16:32].rearrange("p (j c) -> p j c", j=4),
                    in_=pa[:, 0:64].rearrange("p (j c) -> p j c", j=4)[:, :, 4 * d:4 * d + 4]), reads=[bpa], writes=[bsm])
                sm2, bsm2 = wt()
                sm2f = sm2[:].rearrange("p h d -> p (h d)")
                Sc.op("act", lambda e: e.activation(out=sm2f[:, 0:16], in_=sm[:, 16:32], func=AF.Exp),
                      reads=[bsm], writes=[bsm2])
                Sc.op("dve", lambda e: e.tensor_tensor(out=sm2f[:, 16:20], in0=sm[:, 20:24], in1=sm[:, 16:20],
                                                       op=ALU.subtract), reads=[bsm], writes=[bsm2])
                Sc.op("act", lambda e: e.activation(out=sm2f[:, 20:24], in_=sm2f[:, 16:20], func=AF.Exp),
                      reads=[bsm2], writes=[bsm2])
                Sc.op("dve", lambda e: e.tensor_tensor(out=sm2f[:, 24:28], in0=beta_ap, in1=sm2f[:, 0:4], op=ALU.mult),
                      reads=[bsm, bsm2], writes=[bsm2])
                G_ap = sm[:, 16:20]
                eGl_ap = sm2f[:, 20:24]
                beG_ap = sm2f[:, 24:28]
                gt_ap = [sm2f[:, 8:12], sm2f[:, 12:16]]
                yield
                dg, bdg = wt()
                Sc.op("dve", lambda e: e.tensor_tensor(out=dg[:], in0=bc_m(ident), in1=bc_h(G_ap), op=ALU.mult),
                      reads=[bcm, bsm], writes=[bdg])
                pB, bpB = bank()
                for h in range(4):
                    Sc.op("pe", lambda e, h=h: e.matmul(pB[:, h * 128:(h + 1) * 128], lhsT=cm[:, M_ONES, :],
                                                        rhs=dg[:, h, :], start=True, stop=True),
                          reads=[bcm, bdg], writes=[bpB], signal=(h == 3))
                pB3 = pB[:].rearrange("p (h d) -> p h d", h=4)
                t1, bt1 = wt()
                ebc, bebc = wt()
                Sc.op("dve", lambda e: e.tensor_copy(out=ebc[:], in_=pB3), reads=[bpB], writes=[bebc])
                Sc.op("dve", lambda e: e.tensor_tensor(out=t1[:], in0=ebc[:], in1=bc_h(G_ap), op=ALU.subtract),
                      reads=[bebc, bsm], writes=[bt1])
                Sc.op("act", lambda e: e.activation(out=ebc[:], in_=ebc[:], func=AF.Exp), reads=[bebc, bt1], writes=[bebc])
                qdT, bqdT = wt()
                Sc.op("pool", lambda e: e.tensor_tensor(out=qdT[:], in0=qT[:], in1=ebc[:], op=ALU.mult),
                      reads=[bqT, bebc], writes=[bqdT])
                ta, bta = wt()
                tb, btb = wt()
                Sc.op("dve", lambda e: e.tensor_scalar(out=ta[:], in0=t1[:], scalar1=0.0, scalar2=-1.0,
                                                       op0=ALU.max, op1=ALU.mult), reads=[bt1], writes=[bta])
                Sc.op("dve", lambda e: e.tensor_scalar(out=tb[:], in0=t1[:], scalar1=0.0, scalar2=None,
                                                       op0=ALU.min), reads=[bt1], writes=[btb])
                Sc.op("act", lambda e: e.activation(out=ta[:], in_=ta[:], func=AF.Exp), reads=[bta], writes=[bta])
                Sc.op("act", lambda e: e.activation(out=tb[:], in_=tb[:], func=AF.Exp), reads=[btb], writes=[btb])
                yield
                DmS, bDmS = wt()
                DmT, bDmT = wt()
                Sc.op("pool", lambda e: e.tensor_tensor(out=DmS[:], in0=ta[:], in1=bc_m(cm[:, m_str, :]), op=ALU.mult),
                      reads=[bta, bcm], writes=[bDmS])
                Sc.op("pool", lambda e: e.tensor_tensor(out=DmT[:], in0=tb[:], in1=bc_m(cm[:, m_inclT, :]), op=ALU.mult),
                      reads=[btb, bcm], writes=[bDmT])
                pC, bpC = bank()
                pD, bpD = bank()
                for h in range(4):
                    Sc.op("pe", lambda e, h=h: e.matmul(pC[:, h * 128:(h + 1) * 128], lhsT=kT[:, h, :], rhs=kT[:, h, :],
                                                        start=True, stop=True), reads=[bkT], writes=[bpC],
                          signal=(h == 3))
                for h in range(4):
                    Sc.op("pe", lambda e, h=h: e.matmul(pD[:, h * 128:(h + 1) * 128], lhsT=kT[:, h, :], rhs=qT[:, h, :],
                                                        start=True, stop=True), reads=[bkT, bqT], writes=[bpD],
                          signal=(h == 3))
                PB = [wt(), (dg, bdg)]
                QB = [wt(), (t1, bt1)]
                WB = [wt(), (ta, bta)]
                P_, bP_ = PB[0]
                Sc.op("dve", lambda e: e.tensor_tensor(out=P_[:], in0=pC[:].rearrange("p (h d) -> p h d", h=4),
                                                       in1=DmS[:], op=ALU.mult), reads=[bpC, bDmS], writes=[bP_])
                Sc.op("dve", lambda e: e.tensor_tensor(out=P_[:], in0=P_[:], in1=bc_h(beta_ap), op=ALU.mult),
                      reads=[bP_, bsm], writes=[bP_])
                QKDT, bQKDT = wt()
                Sc.op("dve", lambda e: e.tensor_tensor(out=QKDT[:], in0=pD[:].rearrange("p (h d) -> p h d", h=4),
                                                       in1=DmT[:], op=ALU.mult), reads=[bpD, bDmT], writes=[bQKDT])
                yield
                pE, bpE = bank()
                for h in range(4):
                    Sc.op("pe", lambda e, h=h: e.transpose(out=pE[:, h * 128:(h + 1) * 128], in_=P_[:, h, :],
                                                           identity=ident), reads=[bP_, bcm], writes=[bpE],
                          signal=(h == 3))
                pE3 = pE[:].rearrange("p (h d) -> p h d", h=4)
                Q_, bQ_ = QB[0]
                Wc, bWc = WB[0]
                Sc.op("dve", lambda e: e.tensor_copy(out=Q_[:], in_=pE3), reads=[bpE], writes=[bQ_])
                Sc.op("dve", lambda e: e.tensor_tensor(out=Wc[:], in0=bc_m(ident), in1=Q_[:], op=ALU.subtract),
                      reads=[bcm, bQ_], writes=[bWc])
                yield
                for lvl in range(1, 6):
                    pX, bpX = bank()
                    for h in range(4):
                        Sc.op("pe", lambda e, h=h: e.matmul(pX[:, h * 128:(h + 1) * 128], lhsT=Q_[:, h, :],
                                                            rhs=P_[:, h, :], start=True, stop=True),
                              reads=[bQ_, bP_], writes=[bpX], signal=(h == 3))
                    if lvl < 5:
                        pY, bpY = bank()
                        for h in range(4):
                            Sc.op("pe", lambda e, h=h: e.matmul(pY[:, h * 128:(h + 1) * 128], lhsT=P_[:, h, :],
                                                                rhs=Q_[:, h, :], start=True, stop=True),
                                  reads=[bQ_, bP_], writes=[bpY], signal=(h == 3))
                    Pn, bPn = PB[lvl % 2]
                    Sc.op("dve", lambda e: e.tensor_copy(out=Pn[:], in_=pX[:].rearrange("p (h d) -> p h d", h=4)),
                          reads=[bpX], writes=[bPn])
                    if lvl < 5:
                        Qn, bQn = QB[lvl % 2]
                        Sc.op("dve", lambda e: e.tensor_copy(out=Qn[:], in_=pY[:].rearrange("p (h d) -> p h d", h=4)),
                              reads=[bpY], writes=[bQn])
                    pZ, bpZ = bank()
                    for h in range(4):
                        Sc.op("pe", lambda e, h=h: e.matmul(pZ[:, h * 128:(h + 1) * 128], lhsT=Pn[:, h, :],
                                                            rhs=Wc[:, h, :], start=True, stop=True),
                              reads=[bPn, bWc], writes=[bpZ], signal=(h == 3))
                    Wn, bWn = WB[lvl % 2]
                    Sc.op("dve", lambda e: e.tensor_tensor(out=Wn[:], in0=pZ[:].rearrange("p (h d) -> p h d", h=4),
                                                           in1=Wc[:], op=ALU.add), reads=[bpZ, bWc], writes=[bWn])
                    Wc, bWc = Wn, bWn
                    P_, bP_ = Pn, bPn
                    if lvl < 5:
                        Q_, bQ_ = Qn, bQn
                    yield
                T1, bT1 = tb, btb
                T2, bT2 = ebc, bebc
                Sc.op("pool", lambda e: e.tensor_tensor(out=T1[:], in0=Wc[:], in1=bc_h(beta_ap), op=ALU.mult),
                      reads=[bWc, bsm], writes=[bT1])
                Sc.op("dve", lambda e: e.tensor_tensor(out=T2[:], in0=Wc[:], in1=bc_h(beG_ap), op=ALU.mult),
                      reads=[bWc, bsm2], writes=[bT2])
                pU, bpU = bank()
                pW, bpW = bank()
                for h in range(4):
                    Sc.op("pe", lambda e, h=h: e.matmul(pU[:, h * 128:(h + 1) * 128], lhsT=T1[:, h, :],
                                                        rhs=vtok[:, h, :], start=True, stop=True),
                          reads=[bT1, bvtok], writes=[bpU], signal=(h == 3))
                for h in range(4):
                    Sc.op("pe", lambda e, h=h: e.matmul(pW[:, h * 128:(h + 1) * 128], lhsT=ktok[:, h, :],
                                                        rhs=T2[:, h, :], start=True, stop=True),
                          reads=[bT2, bktok], writes=[bpW], signal=(h == 3))
                u_, bu_ = DmS, bDmS
                wT_, bwT_ = DmT, bDmT
                Sc.op("dve", lambda e: e.tensor_copy(out=u_[:], in_=pU[:].rearrange("p (h d) -> p h d", h=4)),
                      reads=[bpU], writes=[bu_])
                Sc.op("dve", lambda e: e.tensor_copy(out=wT_[:], in_=pW[:].rearrange("p (h d) -> p h d", h=4)),
                      reads=[bpW], writes=[bwT_])
                kdec, bkdec = PB[0]
                Sc.op("pool", lambda e: e.tensor_tensor(out=kdec[:], in0=ktok[:], in1=bc_h(eGl_ap), op=ALU.mult),
                      reads=[bktok, bsm2], writes=[bkdec])
                vn, bvn = QB[0]
                stmp, bstmp = QB[1]
                yield
                ot, bot = PB[1]
                corder = (0, 1) if d == 0 else (1, 0)
                for c in corder:
                    c0 = 64 * c
                    cs = slice(c0, c0 + 64)
                    tp = (c0, 0) if c0 else None
                    pV, bpV = bank()
                    for h in range(4):
                        Sc.op("pe", lambda e, h=h: e.matmul(pV[:, h * 128:(h + 1) * 128], lhsT=wT_[:, h, :],
                                                            rhs=S_[:, h, :], start=True, stop=True),
                              reads=[bwT_, bS_], writes=[bpV], signal=(h == 3))
                    Sc.op("dve", lambda e: e.tensor_tensor(out=vn[cs], in0=u_[cs],
                                                           in1=pV[cs, :].rearrange("p (h d) -> p h d", h=4),
                                                           op=ALU.subtract), reads=[bu_, bpV], writes=[bvn])
                    pO, bpO = bank()
                    for h in range(4):
                        Sc.op("pe", lambda e, h=h: e.matmul(pO[:, h * 128:(h + 1) * 128], lhsT=qdT[:, h, :],
                                                            rhs=S_[:, h, :], start=True, stop=False),
                              reads=[bqdT, bS_], writes=[bpO], signal=False)
                        Sc.op("pe", lambda e, h=h: e.matmul(pO[:, h * 128:(h + 1) * 128], lhsT=QKDT[:, h, :],
                                                            rhs=vn[:, h, :], start=False, stop=True),
                              reads=[bQKDT, bvn], writes=[bpO], signal=(h == 3))
                    pS, bpS = bank()
                    for h in range(4):
                        Sc.op("pe", lambda e, h=h: e.matmul(pS[:, h * 128:(h + 1) * 128], lhsT=kdec[cs, h, :],
                                                            rhs=vn[cs, h, :], start=True, stop=True,
                                                            tile_position=tp),
                              reads=[bkdec, bvn], writes=[bpS], signal=(h == 3))
                    Sc.op("dve", lambda e: e.tensor_copy(out=ot[cs], in_=pO[cs, :].rearrange("p (h d) -> p h d", h=4)),
                          reads=[bpO], writes=[bot])
                    Sc.op("pool", lambda e: e.tensor_tensor(out=stmp[:], in0=S_[:], in1=bc_h(gt_ap[c]), op=ALU.mult),
                          reads=[bS_, bsm2], writes=[bstmp])
                    Sc.op("dve", lambda e: e.tensor_tensor(out=S_[:], in0=stmp[:],
                                                           in1=pS[:].rearrange("p (h d) -> p h d", h=4), op=ALU.add),
                          reads=[bstmp, bpS], writes=[bS_])
                    yield
                Sc.dma(OF_s[d, s0:s0 + 128, :].rearrange("p (h d) -> p h d", h=4), ot[:], reads=[bot], writes=[bOF])

            def dn_gen():
                for step in range(T if PHMAX >= 2 else 0):
                    gens = [dn_unit(step, 0, step), dn_unit(T - 1 - step, 1, step)]
                    for _ in zip_longest(*gens):
                        yield

            ph3 = ExitStack()
            with ph3:
                KT2, bKT2 = sb(ph3, "KT2", [128, 2, S], BF16)
                Vaug, bVaug = sb(ph3, "Vaug", [128, T, 2, 65], BF16)
                onesr, bonesr = sb(ph3, "onesr", [128, 64])
                Sc.op("pool", lambda e: e.memset(Vaug[:], 1.0), writes=[bVaug])
                Sc.op("pool", lambda e: e.memset(onesr[:], 1.0), writes=[bonesr])
                for g in range(2):
                    for c0 in range(0, S, 2048):
                        c1 = min(S, c0 + 2048)
                        Sc.dma(KT2[:, g, c0:c1], KT_s[g, :, c0:c1], reads=[bKT], writes=[bKT2])
                TG = 8
                for t0 in range(0, T, TG):
                    nt = min(TG, T - t0)
                    for g in range(2):
                        Sc.dma(Vaug[:, t0:t0 + nt, g, 0:64],
                               V_s[t0 * 128:(t0 + nt) * 128, g * 64:(g + 1) * 64].rearrange("(t p) d -> p t d", p=128),
                               reads=[bV], writes=[bVaug])
                qtc = [sb(ph3, "qtc%d" % i, [128, 512], BF16) for i in range(2)]
                PT = [sb(ph3, "PT%d" % i, [128, 2, 512], BF16) for i in range(3)]
                oa = [sb(ph3, "oa%d" % i, [128, 512]) for i in range(2)]
                rs_, brs_ = sb(ph3, "rs", [128, 512])
                aob = [sb(ph3, "aob%d" % i, [64, 512], BF16) for i in range(2)]
                itc = [0]

                def attn_gen():
                  for qb in range(NB if PHMAX >= 3 else 0):
                    for c in range(4):
                        g = c // 2
                        q_, bq_ = qtc[itc[0] % 2]
                        itc[0] += 1
                        Sc.dma(q_[:], QT_s[c, :, qb * 512:(qb + 1) * 512], reads=[bQT], writes=[bq_])
                        pOa, bpOa = pb[6]
                        pOb, bpOb = pb[7]

                        def qk(kt):
                            pSa, bpSa = pb[2 * (kt % 2)]
                            pSb, bpSb = pb[2 * (kt % 2) + 1]
                            Sc.op("pe", lambda e: e.matmul(pSa[:], lhsT=KT2[0:64, g, kt * 128:(kt + 1) * 128],
                                                           rhs=q_[0:64, :], start=True, stop=True),
                                  reads=[bKT2, bq_], writes=[bpSa], signal=False)
                            Sc.op("pe", lambda e: e.matmul(pSb[:], lhsT=KT2[64:128, g, kt * 128:(kt + 1) * 128],
                                                           rhs=q_[64:128, :], start=True, stop=True,
                                                           tile_position=(64, 0)),
                                  reads=[bKT2, bq_], writes=[bpSb])

                        qk(0)
                        for kt in range(T):
                            if kt + 1 < T:
                                qk(kt + 1)
                            bpSa = pb[2 * (kt % 2)][1]
                            bpSb = pb[2 * (kt % 2) + 1][1]
                            P_, bP_ = PT[kt % 3]
                            Sc.op("act", lambda e: e.activation(out=P_[:], in_=pairs[kt % 2][:], func=AF.Exp, scale=0.125),
                                  reads=[bpSa, bpSb], writes=[bP_])
                            Sc.op("pe", lambda e: e.matmul(pOa[0:65, :], lhsT=Vaug[:, kt, g, :], rhs=P_[:, 0, :],
                                                           start=(kt == 0), stop=(kt == T - 1)),
                                  reads=[bVaug, bP_], writes=[bpOa], signal=False)
                            Sc.op("pe", lambda e: e.matmul(pOb[0:65, :], lhsT=Vaug[:, kt, g, :], rhs=P_[:, 1, :],
                                                           start=(kt == 0), stop=(kt == T - 1)),
                                  reads=[bVaug, bP_], writes=[bpOb])
                            yield
                        for j, (pO_, bpO_) in enumerate(((pOa, bpOa), (pOb, bpOb))):
                            hh = 2 * c + j
                            Sc.op("dve", lambda e: e.reciprocal(out=rs_[64:65, :], in_=pO_[64:65, :]),
                                  reads=[bpO_], writes=[brs_])
                            o_, bo_ = oa[j]
                            Sc.op("act", lambda e: e.copy(out=o_[0:64, :], in_=pO_[0:64, :]), reads=[bpO_], writes=[bo_])
                            pN, bpN = pb[j]
                            Sc.op("pe", lambda e: e.matmul(pN[0:64, :], lhsT=onesr[64:65, 0:64], rhs=rs_[64:65, :],
                                                           start=True, stop=True, tile_position=(64, 0)),
                                  reads=[bonesr, brs_], writes=[bpN])
                            ab, bab = aob[j]
                            Sc.op("dve", lambda e: e.tensor_tensor(out=ab[:], in0=o_[0:64, :], in1=pN[0:64, :], op=ALU.mult),
                                  reads=[bo_, bpN], writes=[bab])
                            Sc.dma(AO_s[hh, :, qb * 512:(qb + 1) * 512], ab[:], reads=[bab], writes=[bAO])

                ag = attn_gen()
                dgen = dn_gen()
                a_done = d_done = False
                RA = int(os.environ.get('KRA', '5'))
                if RA == 0:
                    for _ in dgen:
                        pass
                    d_done = True
                    RA = 100000000
                if RA < 100000:
                    bankset[:] = [4, 5]
                while not (a_done and d_done):
                    for _ in range(RA):
                        if a_done:
                            break
                        try:
                            next(ag)
                        except StopIteration:
                            a_done = True
                    if not d_done:
                        try:
                            next(dgen)
                        except StopIteration:
                            d_done = True
                bankset[:] = list(range(8))

        Sc.barrier()
        ph4 = ExitStack()
        with ph4:
            cg, bcg = sb(ph4, "cg", [128, 128 + D])
            Sc.dma(cg[:], cg_d, writes=[bcg])
            woA, bwoA = sb(ph4, "woA", [64, 8, D], BF16)
            woD, bwoD = sb(ph4, "woD", [128, 4, D], BF16)
            wdn, bwdn = sb(ph4, "wdn", [128, NFC, D], BF16)
            wpg, bwpg = sb(ph4, "wpg", [128, 8, D], BF16)
            wpl, bwpl = sb(ph4, "wpl", [128, 2, D], BF16)
            Sc.dma(woA[:], WO_s[0:512, :].rearrange("(h p) n -> p h n", p=64), reads=[bW], writes=[bwoA])
            Sc.dma(woD[:], WO_s[512:1024, :].rearrange("(c p) n -> p c n", p=128), reads=[bW], writes=[bwoD])
            for f0 in range(0, NFC, 6):
                f1 = min(NFC, f0 + 6)
                Sc.dma(wdn[:, f0:f1, :], WD_s[f0 * 128:f1 * 128, :].rearrange("(c p) n -> p c n", p=128),
                       reads=[bW], writes=[bwdn])
            Sc.dma(wpg[:], WPG_s.rearrange("(c p) n -> p c n", p=128), reads=[bW], writes=[bwpg])
            Sc.dma(wpl[:], WPL_s.rearrange("(c p) n -> p c n", p=128), reads=[bW], writes=[bwpl])
            wgs = [sb(ph4, "wgs%d" % i, [128, 8, 128], BF16) for i in range(4)]
            wus = [sb(ph4, "wus%d" % i, [128, 8, 128], BF16) for i in range(4)]
            hx = [sb(ph4, "hx%d" % i, [128, 4, D]) for i in range(1)]
            hT4, bhT4 = sb(ph4, "hT4", [128, 8, 512], BF16)
            hb4 = [sb(ph4, "hb4_%d" % i, [128, D], BF16) for i in range(2)]
            junk4, bjunk4 = sb(ph4, "junk4", [128, D], BF16)
            s4, bs4 = sb(ph4, "s4", [128, 16])
            actT, bactT = sb(ph4, "actT", [128, NFC, 512], BF16)
            aoT, baoT = sb(ph4, "aoT", [64, 8, 512], BF16)
            dnT, bdnT = sb(ph4, "dnT", [128, 4, 512], BF16)
            e4 = [sb(ph4, "e4_%d" % i, [128, 4, 128]) for i in range(6)]
            dnb = [sb(ph4, "dnb%d" % i, [128, 512], BF16) for i in range(2)]
            sg4 = [sb(ph4, "sg4_%d" % i, [128, 512]) for i in range(2)]
            pin, bpin = sb(ph4, "pin", [128, PLE])
            pinb, bpinb = sb(ph4, "pinb", [128, PLE], BF16)
            pT4, bpT4 = sb(ph4, "pT4", [128, 2, 512], BF16)
            yo = [sb(ph4, "yo%d" % i, [128, D]) for i in range(1)]

            def norm_to_T(src_ap, bsrc, tt, idx):
                Sc.op("pool", lambda e: e.memset(s4[:, 0:1], 0.0), writes=[bs4])
                Sc.op("act", lambda e: e.activation(out=junk4[:], in_=src_ap, func=AF.Square,
                                                    accum_out=s4[:, 0:1]), reads=[bsrc, bs4], writes=[bjunk4, bs4])
                rstd_from_ss(s4[:, 0:1], s4[:, 2:3], s4[:, 1:2], 1.0 / D, [bs4])
                h_, bh_ = hb4[idx % 2]
                Sc.op("dve", lambda e: e.tensor_scalar(out=h_[:], in0=src_ap, scalar1=s4[:, 2:3], scalar2=None,
                                                       op0=ALU.mult), reads=[bsrc, bs4], writes=[bh_])
                transpose_bf(h_, bh_, 8, hT4[:, :, tt * 128:(tt + 1) * 128], bhT4)

            for b in range(NB if PHMAX >= 4 else 0):
                H, bH = hx[0]
                for tt in range(4):
                    ti = 4 * b + tt
                    r0 = ti * 128
                    of_, bof_ = e4[3 * (tt % 2)]
                    ob_, bob_ = e4[3 * (tt % 2) + 1]
                    z_, bz_ = e4[3 * (tt % 2) + 2]
                    Sc.dma(of_[:], OF_s[0, r0:r0 + 128, :].rearrange("p (h d) -> p h d", h=4), reads=[bOF], writes=[bof_])
                    Sc.dma(ob_[:], OF_s[1, r0:r0 + 128, :].rearrange("p (h d) -> p h d", h=4), reads=[bOF], writes=[bob_])
                    Sc.dma(z_[:], DZ_s[r0:r0 + 128, :].rearrange("p (h d) -> p h d", h=4), reads=[bDN], writes=[bz_])
                    o_, bo_ = of_, bof_
                    Sc.op("pool", lambda e: e.tensor_tensor(out=o_[:], in0=of_[:], in1=ob_[:], op=ALU.add),
                          reads=[bof_, bob_], writes=[bo_])
                    sq_, bsq_ = ob_, bob_
                    Sc.op("pool", lambda e: e.tensor_tensor(out=sq_[:], in0=o_[:], in1=o_[:], op=ALU.mult),
                          reads=[bo_], writes=[bsq_])
                    Sc.op("dve", lambda e: e.tensor_reduce(out=s4[:, 4:8], in_=sq_[:], axis=AX.X, op=ALU.add),
                          reads=[bsq_], writes=[bs4])
                    rstd_from_ss(s4[:, 4:8], s4[:, 12:16], s4[:, 8:12], 1.0 / 128, [bs4])
                    Sc.op("dve", lambda e: e.tensor_tensor(out=o_[:], in0=o_[:],
                                                           in1=s4[:, 12:16].unsqueeze(2).broadcast_to([128, 4, 128]),
                                                           op=ALU.mult), reads=[bo_, bs4], writes=[bo_])
                    Sc.op("pool", lambda e: e.tensor_tensor(out=o_[:], in0=o_[:],
                                                            in1=cg[:, 0:128].unsqueeze(1).broadcast_to([128, 4, 128]),
                                                            op=ALU.mult), reads=[bo_, bcg], writes=[bo_])
                    zs_, bzs_ = z_, bz_
                    Sc.op("act", lambda e: e.activation(out=zs_[:], in_=z_[:], func=AF.Silu), reads=[bz_], writes=[bzs_])
                    db_, bdb_ = dnb[tt % 2]
                    Sc.op("dve", lambda e: e.tensor_tensor(out=db_[:].rearrange("p (h d) -> p h d", h=4), in0=o_[:],
                                                           in1=zs_[:], op=ALU.mult), reads=[bo_, bzs_], writes=[bdb_])
                    transpose_bf(db_, bdb_, 4, dnT[:, :, tt * 128:(tt + 1) * 128], bdnT, evac="dve")
                Sc.dma(aoT[:], AO_s[:, :, b * 512:(b + 1) * 512].rearrange("h p s -> p h s"), reads=[bAO], writes=[baoT])
                for tt in range(4):
                    ti = 4 * b + tt
                    Sc.dma(H[:, tt, :], x_d[ti * 128:(ti + 1) * 128, :], writes=[bH])
                for tt in range(4):
                    ts_ = slice(tt * 128, (tt + 1) * 128)
                    for nh in range(2):
                        ns = slice(nh * 512, (nh + 1) * 512)
                        pt, bpt = bank()
                        for h in range(8):
                            Sc.op("pe", lambda e, h=h: e.matmul(pt[:], lhsT=aoT[:, h, ts_], rhs=woA[:, h, ns],
                                                                start=(h == 0), stop=False),
                                  reads=[baoT, bwoA], writes=[bpt], signal=False)
                        for cc in range(4):
                            Sc.op("pe", lambda e, cc=cc: e.matmul(pt[:], lhsT=dnT[:, cc, ts_], rhs=woD[:, cc, ns],
                                                                  start=False, stop=(cc == 3)),
                                  reads=[bdnT, bwoD], writes=[bpt], signal=(cc == 3))
                        Sc.op("dve", lambda e: e.tensor_tensor(out=H[:, tt, ns], in0=H[:, tt, ns], in1=pt[:], op=ALU.add),
                              reads=[bH, bpt], writes=[bH])
                    norm_to_T(H[:, tt, :], bH, tt, tt)
                for fc in range(NFC):
                    wg_, bwg_ = wgs[fc % 4]
                    wu_, bwu_ = wus[fc % 4]
                    Sc.dma(wg_[:], WG_s[fc], reads=[bW], writes=[bwg_])
                    Sc.dma(wu_[:], WU_s[fc], reads=[bW], writes=[bwu_])
                    pg, bpg = bank()
                    pu, bpu = bank()
                    for kc in range(8):
                        Sc.op("pe", lambda e, kc=kc: e.matmul(pg[:], lhsT=wg_[:, kc, :], rhs=hT4[:, kc, :],
                                                             start=(kc == 0), stop=(kc == 7)),
                              reads=[bwg_, bhT4], writes=[bpg], signal=(kc == 7))
                    for kc in range(8):
                        Sc.op("pe", lambda e, kc=kc: e.matmul(pu[:], lhsT=wu_[:, kc, :], rhs=hT4[:, kc, :],
                                                             start=(kc == 0), stop=(kc == 7)),
                              reads=[bwu_, bhT4], writes=[bpu], signal=(kc == 7))
                    sg_, bsg_ = sg4[fc % 2]
                    Sc.op("act", lambda e: e.activation(out=sg_[:], in_=pg[:], func=AF.Silu), reads=[bpg], writes=[bsg_])
                    Sc.op("dve", lambda e: e.tensor_tensor(out=actT[:, fc, :], in0=sg_[:], in1=pu[:], op=ALU.mult),
                          reads=[bsg_, bpu], writes=[bactT])
                for tt in range(4):
                    ts_ = slice(tt * 128, (tt + 1) * 128)
                    for nh in range(2):
                        ns = slice(nh * 512, (nh + 1) * 512)
                        pt, bpt = bank()
                        for fc in range(NFC):
                            Sc.op("pe", lambda e, fc=fc: e.matmul(pt[:], lhsT=actT[:, fc, ts_], rhs=wdn[:, fc, ns],
                                                                  start=(fc == 0), stop=(fc == NFC - 1)),
                                  reads=[bactT, bwdn], writes=[bpt], signal=(fc == NFC - 1))
                        Sc.op("dve", lambda e: e.tensor_tensor(out=H[:, tt, ns], in0=H[:, tt, ns], in1=pt[:], op=ALU.add),
                              reads=[bH, bpt], writes=[bH])
                    norm_to_T(H[:, tt, :], bH, tt, tt)
                for tt in range(4):
                    ti = 4 * b + tt
                    Sc.dma(pin[:], p_d[ti * 128:(ti + 1) * 128, :], writes=[bpin])
                    Sc.op("pool", lambda e: e.tensor_copy(out=pinb[:], in_=pin[:]), reads=[bpin], writes=[bpinb])
                    transpose_bf(pinb, bpinb, 2, pT4[:, :, tt * 128:(tt + 1) * 128], bpT4)
                for tt in range(4):
                    ti = 4 * b + tt
                    ts_ = slice(tt * 128, (tt + 1) * 128)
                    for nh in range(2):
                        ns = slice(nh * 512, (nh + 1) * 512)
                        pg, bpg = bank()
                        pl, bpl = bank()
                        for kc in range(8):
                            Sc.op("pe", lambda e, kc=kc: e.matmul(pg[:], lhsT=hT4[:, kc, ts_], rhs=wpg[:, kc, ns],
                                                                  start=(kc == 0), stop=(kc == 7)),
                                  reads=[bhT4, bwpg], writes=[bpg], signal=(kc == 7))
                        for kc in range(2):
                            Sc.op("pe", lambda e, kc=kc: e.matmul(pl[:], lhsT=pT4[:, kc, ts_], rhs=wpl[:, kc, ns],
                                                                  start=(kc == 0), stop=(kc == 1)),
                                  reads=[bpT4, bwpl], writes=[bpl], signal=(kc == 1))
                        sg_, bsg_ = sg4[nh]
                        Sc.op("act", lambda e: e.activation(out=sg_[:], in_=pg[:], func=AF.Sigmoid),
                              reads=[bpg], writes=[bsg_])
                        Sc.op("dve", lambda e: e.tensor_tensor(out=sg_[:], in0=sg_[:], in1=pl[:], op=ALU.mult),
                              reads=[bsg_, bpl], writes=[bsg_])
                        Sc.op("pool", lambda e: e.tensor_tensor(out=H[:, tt, ns], in0=H[:, tt, ns], in1=sg_[:], op=ALU.add),
                              reads=[bH, bsg_], writes=[bH])
                    Sc.op("pool", lambda e: e.memset(s4[:, 0:1], 0.0), writes=[bs4])
                    Sc.op("act", lambda e: e.activation(out=junk4[:], in_=H[:, tt, :], func=AF.Square,
                                                        accum_out=s4[:, 0:1]), reads=[bH, bs4], writes=[bjunk4, bs4])
                    rstd_from_ss(s4[:, 0:1], s4[:, 2:3], s4[:, 1:2], 1.0 / D, [bs4])
                    y_, by_ = yo[0]
                    Sc.op("dve", lambda e: e.scalar_tensor_tensor(out=y_[:], in0=H[:, tt, :], scalar=s4[:, 2:3],
                                                                  in1=cg[:, 128:128 + D], op0=ALU.mult, op1=ALU.mult),
                          reads=[bH, bs4, bcg], writes=[by_])
                    Sc.dma(out_d[ti * 128:(ti + 1) * 128, :], y_[:], reads=[by_])
        Sc.finish()
        print("ops:", Sc.nops, "cnt:", Sc.cnt)
    return nc


def host_consts(S, norm_mix, norm_ffn, norm_ple, q_norm, k_norm, a_log, dt_bias, conv_w, dn_norm, norm_final):
    f = np.float32
    i = np.arange(128)[:, None]
    j = np.arange(128)[None, :]
    same = (i // 64) == (j // 64)
    cm = np.zeros((128, 10, 128), f)
    cm[:, M_ID] = (i == j)
    cm[:, M_LOWI] = same & (i >= j)
    cm[:, M_UPPI] = same & (i <= j)
    cm[:, M_LOWS] = same & (i > j)
    cm[:, M_UPPS] = same & (i < j)
    cm[:, M_BLK] = same
    cm[:, M_ONES] = 1.0
    cm[:, M_CI0] = (i < 64) & (j >= 0)
    cm[:, M_CI1] = (i >= 64) & (j >= 0)
    Rm = np.zeros((128, 128), f)
    for fo in range(128):
        idx = fo % 32
        if idx < 16:
            Rm[fo + 16, fo] = -1.0
        else:
            Rm[fo - 16, fo] = 1.0
    cm[:, M_RM] = Rm
    cv = np.zeros((128, V_END), f)
    cv[:, V_GMIX:V_GMIX + 8] = norm_mix.reshape(8, 128).T
    cv[:, V_GFFN:V_GFFN + 8] = norm_ffn.reshape(8, 128).T
    cv[:, V_GPLE:V_GPLE + 8] = norm_ple.reshape(8, 128).T
    cv[:, V_QNG] = np.tile(q_norm, 2)
    cv[:, V_KNG] = np.tile(k_norm, 2)
    cv[:, V_ALOG:V_ALOG + 8] = a_log[None, :]
    cv[:, V_DTB:V_DTB + 8] = dt_bias[None, :]
    cv[:, V_CONV:V_CONV + 60] = conv_w.reshape(5, 12, 128).transpose(2, 1, 0).reshape(128, 60)
    cg = np.zeros((128, 128 + D), f)
    cg[:, 0:128] = dn_norm[None, :]
    cg[:, 128:] = norm_final[None, :]
    t = np.arange(S)
    row = (t // 64).astype(np.float64)
    col = (t % 64).astype(np.float64)
    inv_freq = (10000.0 ** (-np.arange(0, 32, 2, dtype=np.float32) / np.float32(32))).astype(np.float32)
    cosT = np.zeros((128, S), f)
    sinT = np.zeros((128, S), f)
    for pp in range(128):
        dd = pp % 64
        pos = row if dd < 32 else col
        ang = (pos.astype(np.float32) * inv_freq[dd % 16]).astype(np.float32)
        cosT[pp] = np.cos(ang)
        sinT[pp] = np.sin(ang)
    return cm, cv, cg, cosT, sinT


_NC_CACHE = {}


def run(S, ncores, x, p, norm_mix, w_in, conv_w, q_norm, k_norm, a_log, dt_bias, dn_norm, w_out,
        norm_ffn, w_gate, w_up, w_down, norm_ple, w_ple_gate, w_ple, norm_final):
    A = lambda a: np.ascontiguousarray(np.asarray(a, dtype=np.float32))
    cm, cv, cg, cosT, sinT = host_consts(S, A(norm_mix)[0], A(norm_ffn)[0], A(norm_ple)[0], A(q_norm)[0], A(k_norm)[0],
                                         A(a_log)[0], A(dt_bias)[0], A(conv_w)[0], A(dn_norm)[0], A(norm_final))
    if S not in _NC_CACHE:
        _NC_CACHE[S] = build(S)
    nc = _NC_CACHE[S]
    x = A(x)
    p = A(p)
    shared = {"w_in": A(w_in)[0], "w_out": A(w_out)[0], "w_gate": A(w_gate)[0], "w_up": A(w_up)[0],
              "w_down": A(w_down)[0], "w_pg": A(w_ple_gate)[0], "w_ple": A(w_ple)[0],
              "cm": cm, "cv": cv, "cg": cg, "cosT": cosT, "sinT": sinT}
    in_maps = []
    for c in range(ncores):
        m = dict(shared)
        m["x"] = np.ascontiguousarray(x[c])
        m["p"] = np.ascontiguousarray(p[0, c])
        in_maps.append(m)
    res = run_bass_kernel_spmd(nc, in_maps, core_ids=list(range(ncores)))
    return np.stack([np.asarray(r["out"], dtype=np.float32) for r in res.results], axis=0)


def kernel(**inputs):
    x = inputs["x"]
    return run(x.shape[1], x.shape[0], **inputs)
```
